# Optimizing a Trainium2 kernel written in Bass

```python
import jax, jax.numpy as jnp
from jax import lax
import numpy as np

D_MODEL = 2048
BATCH = 16
SEQ = 256
DEPTH = 4
DEC_BATCH = 2
DEC_SEQ = 1024
PAST_LEN = 512

GRID_W = 64
N_MIXERS = 3
HEAD_DIM = 128
EPS = 1e-6
NEG_INF = -1e30
ROPE_THETA = 10000.0
WIN_HEADS = D_MODEL // HEAD_DIM
WIN_KV_HEADS = WIN_HEADS // 4
WIN_WIDTH = WIN_HEADS * HEAD_DIM
WIN_KV_WIDTH = WIN_KV_HEADS * HEAD_DIM
WINDOW = 128
BLOCK = 128
NAT_HEADS = D_MODEL // HEAD_DIM
NAT_WIDTH = NAT_HEADS * HEAD_DIM
NAT_ROWS = 8
NAT_COLS = 16
NAT_QCOLS = 16
NAT_KCOLS = NAT_QCOLS + NAT_COLS
GMLP_WIDTH = 2 * D_MODEL
GMLP_GROUPS = 16
CHUNK = 128

kernel_name = 'hybrid_diffusion_prefix_step'


def _n_of_kind(kind):
    return len(range(kind, DEPTH, N_MIXERS))


def _rmsnorm(x, g):
    xf = x.astype(jnp.float32)
    y = xf * lax.rsqrt(jnp.mean(xf * xf, axis=-1, keepdims=True) + EPS)
    return (y * g.astype(jnp.float32)).astype(x.dtype)


def _adaln(cond, w, b):
    m = jax.nn.silu(cond) @ w + b
    return jnp.split(m[:, None, :], 3, axis=-1)


def _axial_rope(x):
    b, s, h, dh = x.shape
    nf = dh // 4
    t = jnp.arange(s)
    row = (t // GRID_W).astype(jnp.float32)
    col = (t % GRID_W).astype(jnp.float32)
    inv = ROPE_THETA ** (-jnp.arange(nf, dtype=jnp.float32) / nf)
    ang = jnp.stack([row[:, None] * inv, col[:, None] * inv], axis=1)
    cos = jnp.cos(ang)[None, :, None]
    sin = jnp.sin(ang)[None, :, None]
    xs = x.astype(jnp.float32).reshape(b, s, h, 2, 2, nf)
    x1, x2 = xs[..., 0, :], xs[..., 1, :]
    out = jnp.stack([x1 * cos - x2 * sin, x2 * cos + x1 * sin], axis=-2)
    return out.reshape(b, s, h, dh).astype(x.dtype)


def _dense_ctx_attn(q, k, v, sink):
    b, p, h, dh = q.shape
    kvh = k.shape[2]
    g = h // kvh
    nb = p // BLOCK
    scale = dh ** -0.5
    qb = q.reshape(b, nb, BLOCK, kvh, g, dh).transpose(1, 0, 2, 3, 4, 5)

    def one_block(qblk):
        s = jnp.einsum('bqkgd,bpkd->bkgqp', qblk, k).astype(jnp.float32) * scale
        if sink is not None:
            sk = jnp.broadcast_to(sink.astype(jnp.float32).reshape(kvh, g, 1, 1), s.shape[:-1] + (1,))
            pr = jax.nn.softmax(jnp.concatenate([s, sk], axis=-1), axis=-1)[..., :-1]
        else:
            pr = jax.nn.softmax(s, axis=-1)
        return jnp.einsum('bkgqp,bpkd->bqkgd', pr.astype(v.dtype), v)

    o = lax.map(one_block, qb)
    return o.transpose(1, 0, 2, 3, 4, 5).reshape(b, p, h * dh)


def _window_attn(q, k, v, kc, vc, sink):
    b, s, h, dh = q.shape
    kvh = k.shape[2]
    g = h // kvh
    nb = s // BLOCK
    p = kc.shape[1]
    scale = dh ** -0.5
    pad = ((0, 0), (BLOCK, BLOCK), (0, 0), (0, 0))
    kp = jnp.pad(k, pad).reshape(b, nb + 2, BLOCK, kvh, dh)
    vp = jnp.pad(v, pad).reshape(b, nb + 2, BLOCK, kvh, dh)
    kb = jnp.concatenate([kp[:, 0:nb], kp[:, 1:nb + 1], kp[:, 2:nb + 2]], axis=2)
    vb = jnp.concatenate([vp[:, 0:nb], vp[:, 1:nb + 1], vp[:, 2:nb + 2]], axis=2)
    n = jnp.arange(nb)
    qpos = n[:, None] * BLOCK + jnp.arange(BLOCK)[None]
    kpos = n[:, None] * BLOCK - BLOCK + jnp.arange(3 * BLOCK)[None]
    diff = kpos[:, None, :] - qpos[:, :, None]
    valid = (jnp.abs(diff) <= WINDOW) & (kpos[:, None, :] >= 0) & (kpos[:, None, :] < s)
    qb = q.reshape(b, nb, BLOCK, kvh, g, dh)
    s_band = jnp.einsum('bnqkgd,bnjkd->bnkgqj', qb, kb).astype(jnp.float32) * scale
    s_band = jnp.where(valid[None, :, None, None], s_band, NEG_INF)
    s_ctx = jnp.einsum('bnqkgd,bpkd->bnkgqp', qb, kc).astype(jnp.float32) * scale
    sk = jnp.broadcast_to(sink.astype(jnp.float32).reshape(1, 1, kvh, g, 1, 1), s_band.shape[:-1] + (1,))
    pr = jax.nn.softmax(jnp.concatenate([s_band, s_ctx, sk], axis=-1), axis=-1)
    nw = 3 * BLOCK
    p_band = pr[..., :nw].astype(v.dtype)
    p_ctx = pr[..., nw:nw + p].astype(v.dtype)
    o = jnp.einsum('bnkgqj,bnjkd->bnqkgd', p_band, vb) + jnp.einsum('bnkgqp,bpkd->bnqkgd', p_ctx, vc)
    return o.reshape(b, s, h * dh)


def _neigh_attn(q, k, v, kc, vc, rel_bias):
    b, s, h, dh = q.shape
    rows = s // GRID_W
    kr = min(NAT_ROWS, rows)
    nj = GRID_W // NAT_QCOLS
    scale = dh ** -0.5
    r = jnp.arange(rows)
    rs = jnp.clip(r - kr // 2, 0, rows - kr)
    row_idx = rs[:, None] + jnp.arange(kr)[None]
    j = jnp.arange(nj)
    cstart = jnp.clip(j * NAT_QCOLS - NAT_COLS // 2, 0, GRID_W - NAT_KCOLS)
    col_idx = cstart[:, None] + jnp.arange(NAT_KCOLS)[None]
    qcol = j[:, None] * NAT_QCOLS + jnp.arange(NAT_QCOLS)[None]
    cs = jnp.clip(qcol - NAT_COLS // 2, 0, GRID_W - NAT_COLS)
    kcol = col_idx[:, None, :]
    valid = (kcol >= cs[:, :, None]) & (kcol < cs[:, :, None] + NAT_COLS)
    dr = row_idx - r[:, None]
    dc = kcol - qcol[:, :, None]
    bias = rel_bias[:, (dr + NAT_ROWS - 1)[:, None, None, :, None],
                    jnp.clip(dc + NAT_COLS - 1, 0, 2 * NAT_COLS - 2)[None, :, :, None, :]]
    bias = bias.astype(jnp.float32).transpose(1, 2, 0, 3, 4, 5)
    bias = bias + jnp.where(valid, 0.0, NEG_INF)[None, :, None, :, None, :]
    bias = bias.reshape(rows, nj, h, NAT_QCOLS, kr * NAT_KCOLS)
    kg = k.reshape(b, rows, GRID_W, h, dh)
    vg = v.reshape(b, rows, GRID_W, h, dh)
    ri = row_idx[:, None, :, None]
    ci = col_idx[None, :, None, :]
    kn = kg[:, ri, ci].reshape(b, rows, nj, kr * NAT_KCOLS, h, dh)
    vn = vg[:, ri, ci].reshape(b, rows, nj, kr * NAT_KCOLS, h, dh)
    qn = q.reshape(b, rows, nj, NAT_QCOLS, h, dh)
    s_nb = jnp.einsum('brjqhd,brjkhd->brjhqk', qn, kn).astype(jnp.float32) * scale + bias[None]
    s_ctx = jnp.einsum('brjqhd,bphd->brjhqp', qn, kc).astype(jnp.float32) * scale
    pr = jax.nn.softmax(jnp.concatenate([s_nb, s_ctx], axis=-1), axis=-1)
    nk = kr * NAT_KCOLS
    o = (jnp.einsum('brjhqk,brjkhd->brjqhd', pr[..., :nk].astype(v.dtype), vn)
         + jnp.einsum('brjhqp,bphd->brjqhd', pr[..., nk:].astype(v.dtype), vc))
    return o.reshape(b, s, h * dh)


def _win_qkvz(hn, w_in, qg, kg, rope):
    b, s, _ = hn.shape
    q, k, v, z = jnp.split(hn @ w_in, [WIN_WIDTH, WIN_WIDTH + WIN_KV_WIDTH, WIN_WIDTH + 2 * WIN_KV_WIDTH], axis=-1)
    q = _rmsnorm(q.reshape(b, s, WIN_HEADS, HEAD_DIM), qg)
    k = _rmsnorm(k.reshape(b, s, WIN_KV_HEADS, HEAD_DIM), kg)
    v = v.reshape(b, s, WIN_KV_HEADS, HEAD_DIM)
    if rope:
        q, k = _axial_rope(q), _axial_rope(k)
    return q, k, v, z


def _nat_qkvz(hn, w_in, qg, kg):
    b, s, _ = hn.shape
    q, k, v, z = jnp.split(hn @ w_in, [NAT_WIDTH, 2 * NAT_WIDTH, 3 * NAT_WIDTH], axis=-1)
    q = _rmsnorm(q.reshape(b, s, NAT_HEADS, HEAD_DIM), qg)
    k = _rmsnorm(k.reshape(b, s, NAT_HEADS, HEAD_DIM), kg)
    v = v.reshape(b, s, NAT_HEADS, HEAD_DIM)
    return q, k, v, z


def _gmlp_branch(hn, w_in, ln_g, ln_b, w_s, b_s, w_out):
    b, s, _ = hn.shape
    nc = s // CHUNK
    proj = hn @ w_in
    uv = jax.nn.gelu(proj[..., :2 * GMLP_WIDTH])
    z = proj[..., 2 * GMLP_WIDTH:]
    u, v = jnp.split(uv, 2, axis=-1)
    vf = v.astype(jnp.float32)
    mu = jnp.mean(vf, axis=-1, keepdims=True)
    var = jnp.mean(jnp.square(vf - mu), axis=-1, keepdims=True)
    vn = ((vf - mu) * lax.rsqrt(var + EPS) * ln_g.astype(jnp.float32) + ln_b.astype(jnp.float32)).astype(v.dtype)
    vg = vn.reshape(b, nc, CHUNK, GMLP_GROUPS, GMLP_WIDTH // GMLP_GROUPS)
    sv = jnp.einsum('gij,bnjgc->bnigc', w_s, vg) + b_s.T[None, None, :, :, None]
    return (u * sv.reshape(b, s, GMLP_WIDTH) * jax.nn.silu(z)) @ w_out


def setup_inputs(seed: int = 0) -> dict:
    key = jax.random.key(seed)
    ks = iter(jax.random.split(key, 32))
    nrm = lambda shape: jax.random.normal(next(ks), shape, jnp.float32)
    n_win, n_nat, n_gm = _n_of_kind(0), _n_of_kind(1), _n_of_kind(2)
    d = D_MODEL
    return {
        'x_prompt': nrm((BATCH, SEQ, d)),
        'x_sample': nrm((DEC_BATCH, DEC_SEQ, d)),
        'cache_win_k': nrm((DEC_BATCH, n_win, PAST_LEN, WIN_KV_HEADS, HEAD_DIM)),
        'cache_win_v': nrm((DEC_BATCH, n_win, PAST_LEN, WIN_KV_HEADS, HEAD_DIM)),
        'cache_nat_k': nrm((DEC_BATCH, n_nat, PAST_LEN, NAT_HEADS, HEAD_DIM)),
        'cache_nat_v': nrm((DEC_BATCH, n_nat, PAST_LEN, NAT_HEADS, HEAD_DIM)),
        'c': nrm((DEC_BATCH, d)),
        'c_ctx': nrm((d,)),
        'norm_g': 1.0 + 0.02 * nrm((DEPTH, d)),
        'w_ada': nrm((DEPTH, d, 3 * d)) * (0.2 * d ** -0.5),
        'b_ada': 0.02 * nrm((DEPTH, 3 * d)),
        'win_w_in': nrm((n_win, d, 2 * WIN_WIDTH + 2 * WIN_KV_WIDTH)) * d ** -0.5,
        'win_q_norm': 1.0 + 0.02 * nrm((n_win, HEAD_DIM)),
        'win_k_norm': 1.0 + 0.02 * nrm((n_win, HEAD_DIM)),
        'win_sink': nrm((n_win, WIN_HEADS)),
        'win_w_out': nrm((n_win, WIN_WIDTH, d)) * WIN_WIDTH ** -0.5,
        'nat_w_in': nrm((n_nat, d, 4 * NAT_WIDTH)) * d ** -0.5,
        'nat_q_norm': 1.0 + 0.02 * nrm((n_nat, HEAD_DIM)),
        'nat_k_norm': 1.0 + 0.02 * nrm((n_nat, HEAD_DIM)),
        'nat_rel_bias': 0.5 * nrm((n_nat, NAT_HEADS, 2 * NAT_ROWS - 1, 2 * NAT_COLS - 1)),
        'nat_w_out': nrm((n_nat, NAT_WIDTH, d)) * NAT_WIDTH ** -0.5,
        'gmlp_w_in': nrm((n_gm, d, 3 * GMLP_WIDTH)) * d ** -0.5,
        'gmlp_ln_g': 1.0 + 0.02 * nrm((n_gm, GMLP_WIDTH)),
        'gmlp_ln_b': 0.02 * nrm((n_gm, GMLP_WIDTH)),
        'gmlp_w_s': nrm((n_gm, GMLP_GROUPS, CHUNK, CHUNK)) * CHUNK ** -0.5,
        'gmlp_b_s': 1.0 + 0.02 * nrm((n_gm, GMLP_GROUPS, CHUNK)),
        'gmlp_w_out': nrm((n_gm, GMLP_WIDTH, d)) * GMLP_WIDTH ** -0.5,
    }


def reference(x_prompt, x_sample, cache_win_k, cache_win_v, cache_nat_k, cache_nat_v, c, c_ctx,
              norm_g, w_ada, b_ada,
              win_w_in, win_q_norm, win_k_norm, win_sink, win_w_out,
              nat_w_in, nat_q_norm, nat_k_norm, nat_rel_bias, nat_w_out,
              gmlp_w_in, gmlp_ln_g, gmlp_ln_b, gmlp_w_s, gmlp_b_s, gmlp_w_out):
    xp, xs = x_prompt, x_sample
    new_win_k, new_win_v, new_nat_k, new_nat_v = [], [], [], []
    for i in range(DEPTH):
        kind = i % N_MIXERS
        li = i // N_MIXERS
        sh_p, sc_p, gt_p = _adaln(c_ctx[None], w_ada[i], b_ada[i])
        sh_s, sc_s, gt_s = _adaln(c, w_ada[i], b_ada[i])
        hp = _rmsnorm(xp, norm_g[i]) * (1.0 + sc_p) + sh_p
        hs = _rmsnorm(xs, norm_g[i]) * (1.0 + sc_s) + sh_s
        if kind == 0:
            qp, kp, vp, zp = _win_qkvz(hp, win_w_in[li], win_q_norm[li], win_k_norm[li], False)
            yp = (_dense_ctx_attn(qp, kp, vp, win_sink[li]) * jax.nn.silu(zp)) @ win_w_out[li]
            new_win_k.append(kp)
            new_win_v.append(vp)
            qs, ks_, vs, zs = _win_qkvz(hs, win_w_in[li], win_q_norm[li], win_k_norm[li], True)
            ys = (_window_attn(qs, ks_, vs, cache_win_k[:, li], cache_win_v[:, li], win_sink[li])
                  * jax.nn.silu(zs)) @ win_w_out[li]
        elif kind == 1:
            qp, kp, vp, zp = _nat_qkvz(hp, nat_w_in[li], nat_q_norm[li], nat_k_norm[li])
            yp = (_dense_ctx_attn(qp, kp, vp, None) * jax.nn.silu(zp)) @ nat_w_out[li]
            new_nat_k.append(kp)
            new_nat_v.append(vp)
            qs, ks_, vs, zs = _nat_qkvz(hs, nat_w_in[li], nat_q_norm[li], nat_k_norm[li])
            ys = (_neigh_attn(qs, ks_, vs, cache_nat_k[:, li], cache_nat_v[:, li], nat_rel_bias[li])
                  * jax.nn.silu(zs)) @ nat_w_out[li]
        else:
            yp = _gmlp_branch(hp, gmlp_w_in[li], gmlp_ln_g[li], gmlp_ln_b[li], gmlp_w_s[li], gmlp_b_s[li], gmlp_w_out[li])
            ys = _gmlp_branch(hs, gmlp_w_in[li], gmlp_ln_g[li], gmlp_ln_b[li], gmlp_w_s[li], gmlp_b_s[li], gmlp_w_out[li])
        xp = xp + gt_p * yp
        xs = xs + gt_s * ys
    return (xp, xs, jnp.stack(new_win_k, axis=1), jnp.stack(new_win_v, axis=1),
            jnp.stack(new_nat_k, axis=1), jnp.stack(new_nat_v, axis=1))
```

```python
import os
import numpy as np
import concourse.bass as bass
import concourse.mybir as mybir
from concourse.bass_utils import run_bass_kernel_spmd

F32 = mybir.dt.float32
BF16 = mybir.dt.bfloat16
AF = mybir.ActivationFunctionType
ALU = mybir.AluOpType
AX = mybir.AxisListType

D = 2048
T = 1024
NT = 8
KC = 16
NEG = -30000.0
EPS = 1e-6
SCALE = 128.0 ** -0.5
ENGS = ["pe", "act", "dve", "pool", "sp"]
EPI_DELAY = 2

NAT_J = {0: [0, 1, 2, 3], 1: [0, 1, 2, 3], 2: [0, 1, 2, 3, 4], 3: [1, 2, 3, 4, 5],
         4: [2, 3, 4, 5, 6], 5: [3, 4, 5, 6, 7], 6: [4, 5, 6, 7], 7: [4, 5, 6, 7]}
NAT_OFF = {}
_o = 0
for _i in range(8):
    NAT_OFF[_i] = _o
    _o += len(NAT_J[_i])
NAT_NB = _o


class Res:
    __slots__ = ("name", "w", "r")

    def __init__(self, name):
        self.name = name
        self.w = None
        self.r = {}


class Prog:
    def __init__(self, nc):
        self.nc = nc
        self.ops = {e: [] for e in ENGS}
        self.signal = {e: set() for e in ENGS}
        self.seen = {e: {} for e in ENGS}
        self.dcount = {}

    def _deps(self, eng, reads, writes, acc):
        evs = []
        for r in reads:
            if r.w is not None:
                evs.append(r.w)
        for w in writes:
            if w.w is not None:
                if not (acc and w.w[0] == "E" and w.w[1] == "pe" and eng == "pe"):
                    evs.append(w.w)
            evs.extend(w.r.values())
        out = []
        for ev in evs:
            kind, key, val = ev
            if kind == "D":
                val = self.dcount[key]
            if self.seen[eng].get((kind, key), -1) >= val:
                continue
            self.seen[eng][(kind, key)] = val
            out.append((kind, key, val))
            if kind == "E":
                self.signal[key].add(val)
        return out

    def op(self, eng, fn, reads=(), writes=(), acc=False):
        waits = self._deps(eng, reads, writes, acc)
        idx = len(self.ops[eng])
        ev = ("E", eng, idx)
        self.ops[eng].append((waits, fn, None))
        for r in reads:
            r.r[eng] = ev
        for w in writes:
            w.w = ev
            w.r = {}
        return ev

    def dma(self, eng, key, out, in_, reads=(), writes=(), **kw):
        key = key + "_" + eng
        waits = self._deps(eng, reads, writes, False)
        self.dcount[key] = self.dcount.get(key, 0) + 16
        ev = ("D", key, self.dcount[key])
        self.ops[eng].append((waits, (lambda e, o=out, i=in_, k=kw: e.dma_start(out=o, in_=i, **k)), key))
        for r in reads:
            r.r["D" + key] = ev
        for w in writes:
            w.w = ev
            w.r = {}
        return ev

    def wait_all(self, eng):
        waits = []
        for e in ENGS:
            n = len(self.ops[e])
            if n == 0:
                continue
            for idx in range(n - 1, -1, -1):
                if self.ops[e][idx][2] is None and self.ops[e][idx][1] is not None:
                    if self.seen[eng].get(("E", e), -1) < idx:
                        waits.append(("E", e, idx))
                        self.signal[e].add(idx)
                        self.seen[eng][("E", e)] = idx
                    break
        for key, cnt in self.dcount.items():
            if self.seen[eng].get(("D", key), -1) < cnt:
                waits.append(("D", key, cnt))
                self.seen[eng][("D", key)] = cnt
        self.ops[eng].append((waits, None, None))

    def emit(self, block, esem, dsem):
        tick = {}
        for e in ENGS:
            tick[e] = {}
            c = 0
            for idx in sorted(self.signal[e]):
                c += 1
                tick[e][idx] = c

        def run(e, h):
            for idx, (waits, fn, dkey) in enumerate(self.ops[e]):
                for kind, key, val in waits:
                    if kind == "E":
                        h.wait_ge(esem[key], tick[key][val])
                    else:
                        h.wait_ge(dsem[key], val)
                if fn is None:
                    continue
                ins = fn(h)
                if dkey is not None:
                    ins.then_inc(dsem[dkey], 16)
                elif idx in self.signal[e]:
                    ins.then_inc(esem[e], 1)

        @block.tensor
        def _(h):
            run("pe", h)

        @block.scalar
        def _(h):
            run("act", h)

        @block.vector
        def _(h):
            run("dve", h)

        @block.gpsimd
        def _(h):
            run("pool", h)

        @block.sync
        def _(h):
            run("sp", h)


class Ring:
    def __init__(self, items):
        self.items = items
        self.i = 0

    def next(self):
        it = self.items[self.i % len(self.items)]
        self.i += 1
        return it


def vap(ap, off, pairs, parts=None):
    p = ap.ap[0]
    n = p[1] if parts is None else parts
    return bass.AP(ap.tensor, ap.offset + off, [[p[0], n]] + [list(x) for x in pairs])


def build(nlayers=4, passes=((8, False), (4, True))):
    nc = bass.Bass("TRN2", target_bir_lowering=False)
    P = Prog(nc)

    def din(name, shape):
        return nc.dram_tensor(name, list(shape), F32, kind="ExternalInput").ap()

    def dout(name, shape):
        return nc.dram_tensor(name, list(shape), F32, kind="ExternalOutput").ap()

    PT = []
    for pi, (ntp, static_prompt) in enumerate(passes):
        d_ = {"nt": ntp, "static": static_prompt, "pi": pi}
        sfx = "_%d" % pi
        d_["xin"] = din("xin" + sfx, [ntp * 128, D])
        d_["cond"] = din("cond" + sfx, [128, 16])
        d_["y"] = dout("y" + sfx, [ntp * 128, D])
        d_["owk"] = dout("owk" + sfx, [2, ntp * 128, 512])
        d_["owv"] = dout("owv" + sfx, [2, ntp * 128, 512])
        d_["onk"] = dout("onk" + sfx, [ntp * 128, D])
        d_["onv"] = dout("onv" + sfx, [ntp * 128, D])
        if not static_prompt:
            d_["wmask"] = din("wmask" + sfx, [8, 3, 128, 128])
            d_["ctxb_d"] = din("ctxb" + sfx, [128, 1])
            d_["ropec"] = din("ropec" + sfx, [ntp * 128, 64])
            d_["ropes"] = din("ropes" + sfx, [ntp * 128, 64])
            d_["cwk"] = din("cwk" + sfx, [2, 512, 512])
            d_["cwv"] = din("cwv" + sfx, [2, 512, 512])
            if nlayers > 1:
                d_["nmask"] = din("nmask" + sfx, [16, NAT_NB, 128, 128])
                d_["cnk"] = din("cnk" + sfx, [512, 2048])
                d_["cnv"] = din("cnv" + sfx, [512, 2048])
        PT.append(d_)
    zmask = din("zmask", [2, 128, 128])
    gsc = nc.dram_tensor("gsc", [4, 2048], F32, kind="Internal").ap()
    cur = {}
    ident_d = din("ident", [128, 128])
    ones_d = din("onesrow", [1, 4096])
    normg = din("normg", [4, 128, 16])
    kinds_ = [0, 1, 2, 0][:nlayers]
    w_ada = [din("w_ada%d" % l, [D, 3 * D]) for l in range(nlayers)]
    b_ada = din("b_ada", [4, 1, 3 * D])
    nwin = len([k for k in kinds_ if k == 0])
    win_w_in = [din("win_w_in%d" % i, [D, 5120]) for i in range(nwin)]
    win_qn = din("win_qn", [2, 1, 128])
    win_kn = din("win_kn", [2, 1, 128])
    win_sink = din("win_sink", [2, 1, 16])
    win_w_out = [din("win_w_out%d" % i, [D, D]) for i in range(nwin)]
    if 1 in kinds_:
        nat_w_in = [din("nat_w_in", [D, 8192])]
        nat_qn = din("nat_qn", [1, 1, 128])
        nat_kn = din("nat_kn", [1, 1, 128])
        nat_w_out = [din("nat_w_out", [D, D])]
    if 2 in kinds_:
        g_w_in = [din("g_w_in", [D, 12288])]
        g_ln_g = din("g_ln_g", [1, 1, 4096])
        g_ln_b = din("g_ln_b", [1, 1, 4096])
        g_w_s = din("g_w_s", [1, 16, 128, 128])
        g_b_s = din("g_b_s", [1, 1, 2048])
        g_w_out = [din("g_w_out", [4096, D])]

    es = {}
    from contextlib import ExitStack
    st = ExitStack()
    with st:
        def sb(name, shape, dt):
            return st.enter_context(nc.sbuf_tensor(name, list(shape), dt))

        def pst(name, shape, dt):
            return st.enter_context(nc.psum_tensor(name, list(shape), dt))

        ident = sb("ident_sb", [128, 128], BF16)
        onesb = sb("onesb", [128, 2], BF16)
        onesf = sb("onesf", [33, 128], F32)
        scb2 = sb("scb2", [128, 16 * 33], BF16)
        condT2 = sb("condT2", [128, 16], F32)
        modA_st = sb("modA_st", [128, 64], F32)
        modB_st = sb("modB_st", [128, 64], F32)
        hnT = sb("hnT", [128, KC, T], BF16)
        big = sb("big", [128, 16 * T], BF16)
        attn_big = sb("attn_big", [128, 16640], BF16)
        wring = [sb("w%d" % i, [128, KC, 512], BF16) for i in range(2)]
        xt = [sb("xt%d" % i, [128, 2048], F32) for i in range(1)]
        xsb = [sb("xsb%d" % i, [128, 2048], BF16) for i in range(1)]
        gate_bc = sb("gate_bc", [128, 2048], F32)
        mrow = sb("mrow", [33, 512], F32)
        brow = sb("brow", [33, 512], F32)
        condT = sb("condT", [128, 16], F32)
        ng = sb("ng", [128, 16], F32)
        modA = sb("modA", [128, 16], F32)
        modB = sb("modB", [128, 16], F32)
        small = sb("small", [128, 64], F32)
        smalls = [sb("small_r%d" % i, [128, 64], F32) for i in range(3)]
        ctxb = sb("ctxb_sb", [128, 1], F32)
        qn_bc = sb("qn_bc", [128, 128], F32)
        kn_bc = sb("kn_bc", [128, 128], F32)
        esink = sb("esink", [128, 16], F32)
        cosb = sb("cosb", [128, NT, 64], F32)
        sinb = sb("sinb", [128, NT, 64], F32)
        qT = vap(attn_big[:], 0, [[T, 4], [1, T]])
        kT = vap(attn_big[:], 4096, [[T, 4], [1, T]])
        kTc = vap(attn_big[:], 8192, [[512, 4], [1, 512]])
        cstage = vap(attn_big[:], 10240, [[512, 4], [1, 512]])
        vaug = vap(attn_big[:], 12288, [[520, NT], [130, 4], [1, 130]])
        vaugc = sb("vaugc", [128, 4, 4, 130], BF16)
        t32 = [sb("t32_%d" % i, [128, 512], F32) for i in range(4)]
        t16 = [sb("t16_%d" % i, [128, 512], BF16) for i in range(4)]
        ebuf = sb("ebuf", [128, 4096], BF16)
        Eb = [vap(ebuf[:], i * 512, [[1, 512]]) for i in range(8)]
        svbuf = vap(ebuf[:], 0, [[512, NT], [1, 512]])
        mk = [sb("mk%d" % i, [128, 5, 128], F32) for i in range(2)]
        wm4 = sb("wm4", [128, 4, 128], BF16)
        xp = [sb("xp%d" % i, [128, 512], F32) for i in range(2)]
        wsn = sb("wsn", [128, 16, 128], BF16)
        wsT = sb("wsT", [128, 16, 128], BF16)
        rb2 = sb("rb2", [2, 256], F32)
        lb2 = sb("lb2", [2, 512], F32)
        lng = sb("lng", [128, 512], BF16)
        gstat = sb("gstat", [128, 2, NT, 8], F32)

        pj = [pst("pj%d" % i, [128, 512], F32) for i in range(2)]
        ps_ = [pst("ps%d" % i, [128, 512], F32) for i in range(2)]
        po = [pst("po%d" % i, [128, 512], F32) for i in range(2)]
        ptr = [pst("pt%d" % i, [128, 1024], BF16) for i in range(2)]

        R = {}

        def res(name):
            if name not in R:
                R[name] = Res(name)
            return R[name]

        pjR = Ring([(pj[i], res("pj%d" % i)) for i in range(2)] + [(ps_[i], res("ps%d" % i)) for i in range(2)])
        psR = pjR
        poR = Ring([(po[i], res("po%d" % i)) for i in range(2)])
        ptR = Ring([(ptr[i], res("pt%d" % i)) for i in range(2)])
        wR = Ring([(wring[i], res("w%d" % i), "w%d" % i) for i in range(2)])
        xtR = Ring([(xt[i], res("xt%d" % i), "xt%d" % i) for i in range(1)])
        xsR = Ring([(xsb[i], res("xsb%d" % i)) for i in range(1)])
        t32R = Ring([(t32[i], res("t32_%d" % i)) for i in range(4)])
        t16R = Ring([(t16[i], res("t16_%d" % i)) for i in range(4)])
        smR = Ring([(smalls[i], res("small_r%d" % i)) for i in range(3)])
        mkR = Ring([(mk[i], res("mk%d" % i), "mk%d" % i) for i in range(2)])
        xpR = Ring([(xp[i], res("xp%d" % i), "xp%d" % i) for i in range(2)])
        ER = [(Eb[i], res("E%d" % i)) for i in range(8)]
        evac_flip = [0]

        rS = res("setup")
        P.dma("pool", "setup", ident[:], ident_d[:, :], writes=[rS])
        P.dma("sp", "setup", onesf[0:1, :], ones_d[0:1, 0:128], writes=[rS])
        P.dma("sp", "setup", onesf[32:33, :], ones_d[0:1, 0:128], writes=[rS])
        P.op("dve", lambda e: e.memset(scb2[:], 0.0), writes=[res("scb")])
        P.op("dve", lambda e: e.memset(onesb[:], 1.0), writes=[res("onesb")])
        P.op("dve", lambda e: e.memset(vaugc[:], 1.0), writes=[res("vaugc")])
        rsc = res("scb")
        rPS = res("pass_setup")

        def pass_setup():
            P.dma("sp", "psetup", condT[:], cur["cond"][:, :], writes=[rPS])
            if not cur["static"]:
                ntp = cur["nt"]
                P.dma("sp", "psetup", ctxb[:], cur["ctxb_d"][:, :], writes=[rPS])
                P.dma("sp", "psetup", cosb[:, 0:ntp, :], cur["ropec"].rearrange("(t p) f -> p t f", p=128), writes=[rPS])
                P.dma("sp", "psetup", sinb[:, 0:ntp, :], cur["ropes"].rearrange("(t p) f -> p t f", p=128), writes=[rPS])
                for slot_, (i_, jj_) in enumerate([(2, 0), (1, 0), (0, 2), (1, 2)]):
                    P.dma("pool", "wm4", wm4[:, slot_, :], cur["wmask"][i_, jj_, :, :], writes=[res("wm4")])
            if cur["pi"] == 0:
                P.op("act", lambda e: e.activation(out=vap(scb2[:], 0, [[33, 16]]), in_=condT[:], func=AF.Silu),
                     reads=[rPS], writes=[rsc])
                if len(passes) > 1:
                    P.dma("sp", "psetup", condT2[:], PT[1]["cond"][:, :], writes=[rPS])
                    P.op("act", lambda e: e.activation(out=vap(scb2[:], 32, [[33, 16]]), in_=condT2[:], func=AF.Silu),
                         reads=[rPS], writes=[rsc])

        def load_w(src2d, r0, c0):
            slot, r, key = wR.next()
            P.dma("pool", key, slot[:], src2d[r0:r0 + 2048, c0:c0 + 512].rearrange("(kc p) n -> p kc n", p=128),
                  writes=[r])
            return slot, r

        def transpose_to(dst_fn, src_ap_fn, n, rsrc, rdst, evac=None, dstm=None):
            i = 0
            while i < n:
                m = min(4, n - i)
                pt, rpt = ptR.next()
                for jx in range(m):
                    P.op("pe", lambda e, o=pt[:, jx * 128:(jx + 1) * 128], s=src_ap_fn(i + jx): e.transpose(
                        out=o, in_=s, identity=ident[:]), reads=[rsrc, rS], writes=[rpt])
                if evac is not None:
                    evac(i, m, pt, rpt)
                elif dstm is not None:
                    dst = dstm(i, m)
                    src = vap(pt[:, :], 0, [[128, m], [1, 128]])
                    evac_flip[0] ^= 1
                    if evac_flip[0]:
                        P.op("dve", lambda e, o=dst, s=src: e.tensor_copy(out=o, in_=s), reads=[rpt], writes=[rdst])
                    else:
                        P.op("act", lambda e, o=dst, s=src: e.copy(out=o, in_=s), reads=[rpt], writes=[rdst])
                else:
                    for jx in range(m):
                        dst = dst_fn(i + jx)
                        src = pt[:, jx * 128:(jx + 1) * 128]
                        evac_flip[0] ^= 1
                        if evac_flip[0]:
                            P.op("dve", lambda e, o=dst, s=src: e.tensor_copy(out=o, in_=s), reads=[rpt], writes=[rdst])
                        else:
                            P.op("act", lambda e, o=dst, s=src: e.copy(out=o, in_=s), reads=[rpt], writes=[rdst])
                i += m

        def adaln(l):
            rm = res("mrow")
            rb = res("brow")
            rg = res("gate")
            rmod = res("mod")
            rst = res("modst")
            two = len(passes) > 1
            if cur["pi"] == 1:
                P.op("dve", lambda e: e.tensor_copy(out=modA[:], in_=modA_st[:, l * 16:(l + 1) * 16]), reads=[rst], writes=[rmod])
                P.op("dve", lambda e: e.tensor_copy(out=modB[:], in_=modB_st[:, l * 16:(l + 1) * 16]), reads=[rst], writes=[rmod])
                P.dma("sp", "gatebc", gate_bc[:], gsc[l:l + 1, :].partition_broadcast(128), reads=[res("gsc")], writes=[rg])
                return
            P.dma("sp", "ng", ng[:], normg[l, :, :], writes=[res("ng")])
            pc, rpc = psR.next()
            pcB, rpcB = psR.next()
            M = 33 if two else 1
            for cg in range(12):
                slot, rw = load_w(w_ada[l], 0, cg * 512)
                P.dma("sp", "brow", brow[0:1, :], b_ada[l, :, cg * 512:(cg + 1) * 512], writes=[rb])
                if two:
                    P.dma("sp", "brow", brow[32:33, :], b_ada[l, :, cg * 512:(cg + 1) * 512], writes=[rb])
                pp, rp = poR.next()
                for kc in range(KC):
                    P.op("pe", lambda e, o=pp[0:M, :], a=vap(scb2[:], kc * 33, [[1, M]]), b=slot[:, kc, :], s=(kc == 0), t=(kc == KC - 1):
                         e.matmul(o, lhsT=a, rhs=b, start=s, stop=t), reads=[rw, rsc], writes=[rp], acc=(kc > 0))
                P.op("dve", lambda e, a=pp[0:1, :]: e.tensor_tensor(out=mrow[0:1, :], in0=a, in1=brow[0:1, :], op=ALU.add),
                     reads=[rp, rb], writes=[rm])
                if two:
                    P.op("dve", lambda e, a=pp[32:33, :]: e.tensor_tensor(out=mrow[32:33, :], in0=a, in1=brow[32:33, :], op=ALU.add),
                         reads=[rp, rb], writes=[rm])
                if cg < 8:
                    for j in range(4):
                        col = cg * 4 + j
                        P.op("pe", lambda e, o=pc[:, 2 * col:2 * col + 2], a=mrow[0:1, j * 128:(j + 1) * 128], b=onesf[0:1, 0:2]:
                             e.matmul(o, lhsT=a, rhs=b, start=True, stop=True), reads=[rm, rS], writes=[rpc])
                        if two:
                            P.op("pe", lambda e, o=pcB[:, 2 * col:2 * col + 2], a=mrow[32:33, j * 128:(j + 1) * 128], b=onesf[32:33, 0:2]:
                                 e.matmul(o, lhsT=a, rhs=b, start=True, stop=True), reads=[rm, rS], writes=[rpcB])
                else:
                    dg = cg - 8
                    pg, rpg = poR.next()
                    P.op("pe", lambda e, o=pg[:, :], a=onesf[0:1, :], b=mrow[0:1, :]:
                         e.matmul(o, lhsT=a, rhs=b, start=True, stop=True), reads=[rm, rS], writes=[rpg])
                    P.op("act", lambda e, o=gate_bc[:, dg * 512:(dg + 1) * 512], a=pg[:, :]: e.copy(out=o, in_=a),
                         reads=[rpg], writes=[rg])
                    if two:
                        P.dma("sp", "gsc", gsc[l:l + 1, dg * 512:(dg + 1) * 512], mrow[32:33, :], reads=[rm], writes=[res("gsc")])
            P.op("dve", lambda e: e.tensor_copy(out=modB[:], in_=vap(pc[:, :], 0, [[2, 16]])), reads=[rpc], writes=[rmod])
            P.op("dve", lambda e: e.scalar_tensor_tensor(out=modA[:], in0=vap(pc[:, :], 32, [[2, 16]]), scalar=1.0, in1=ng[:],
                                                         op0=ALU.add, op1=ALU.mult), reads=[rpc, res("ng")], writes=[rmod])
            if two:
                P.op("dve", lambda e: e.tensor_copy(out=modB_st[:, l * 16:(l + 1) * 16], in_=vap(pcB[:, :], 0, [[2, 16]])),
                     reads=[rpcB], writes=[rst])
                P.op("dve", lambda e: e.scalar_tensor_tensor(out=modA_st[:, l * 16:(l + 1) * 16], in0=vap(pcB[:, :], 32, [[2, 16]]),
                                                             scalar=1.0, in1=ng[:], op0=ALU.add, op1=ALU.mult),
                     reads=[rpcB, res("ng")], writes=[rst])

        def norm_phase(l):
            src = cur["xin"] if l == 0 else cur["y"]
            rh = res("hnT")
            rmod = res("mod")
            for tt in range(cur["nt"]):
                xtile, rx, key = xtR.next()
                P.dma("sp", key, xtile[:], src[tt * 128:(tt + 1) * 128, :], reads=[res("y%d" % tt)], writes=[rx])
                xs, rxs = xsR.next()
                sm, rsm = smR.next()
                P.op("act", lambda e, o=vap(small[:], 60, [[0, 2048]]), a=xtile[:], sm=sm: e.activation(
                    out=o, in_=a, func=AF.Square, accum_out=sm[:, 0:1]), reads=[rx], writes=[res("junk"), rsm])
                P.op("act", lambda e, sm=sm: e.activation(out=sm[:, 1:2], in_=sm[:, 0:1], func=AF.Sqrt, scale=1.0 / D,
                                                          bias=ctxb_eps[:]), reads=[rsm, rS], writes=[rsm])
                P.op("dve", lambda e, sm=sm: e.reciprocal(out=sm[:, 2:3], in_=sm[:, 1:2]), reads=[rsm], writes=[rsm])
                P.op("dve", lambda e, o=xs[:], a=xtile[:], sm=sm: e.tensor_scalar(out=o, in0=a, scalar1=sm[:, 2:3], scalar2=None,
                                                                                  op0=ALU.mult), reads=[rx, rsm, rxs], writes=[rxs])

                def evac(kc0, m, pt, rpt, tt=tt):
                    if (kc0 // 4) % 2 == 0:
                        dst = vap(hnT[:], kc0 * T + tt * 128, [[T, m], [1, 128]])
                        srcp = vap(pt[:, :], 0, [[128, m], [1, 128]])
                        P.op("dve", lambda e: e.tensor_tensor(out=dst, in0=srcp, in1=vap(modA[:], kc0, [[1, m], [0, 128]]), op=ALU.mult),
                             reads=[rpt, rmod], writes=[rh])
                        P.op("dve", lambda e: e.tensor_tensor(out=dst, in0=dst, in1=vap(modB[:], kc0, [[1, m], [0, 128]]), op=ALU.add),
                             reads=[rmod, rh], writes=[rh])
                    else:
                        for jx in range(m):
                            k = kc0 + jx
                            P.op("act", lambda e, o=hnT[:, k, tt * 128:(tt + 1) * 128], s_=pt[:, jx * 128:(jx + 1) * 128], k=k: e.activation(
                                out=o, in_=s_, func=AF.Identity, scale=modA[:, k:k + 1], bias=modB[:, k:k + 1]),
                                reads=[rpt, rmod], writes=[rh])

                transpose_to(lambda kc, tt=tt: hnT[:, kc, tt * 128:(tt + 1) * 128],
                             lambda kc, xs=xs: xs[:, kc * 128:(kc + 1) * 128], KC, rxs, rh, evac=evac)

        def project(wsrc, r0, cg, epilogue, lhs=None, rlhs=None, koff=0):
            lhs = hnT if lhs is None else lhs
            rlhs = res("hnT") if rlhs is None else rlhs
            slot, rw = load_w(wsrc, r0, cg * 512)
            pend = []
            for tt in range(cur["nt"]):
                pp, rp = pjR.next()
                for kc in range(KC):
                    P.op("pe", lambda e, o=pp[:, :], a=lhs[:, koff + kc, tt * 128:(tt + 1) * 128], b=slot[:, kc, :],
                         s=(kc == 0), t=(kc == KC - 1): e.matmul(o, lhsT=a, rhs=b, start=s, stop=t),
                         reads=[rw, rlhs], writes=[rp], acc=(kc > 0))
                pend.append((tt, pp, rp))
                if len(pend) > EPI_DELAY:
                    epilogue(*pend.pop(0))
            for p_ in pend:
                epilogue(*p_)

        def headnorm(pp, rp, gain_bc, rgain, rope):
            sq, rsq = t32R.next()
            sm, rsm = smR.next()
            v4 = lambda t: vap(t[:], 0, [[128, 4], [1, 128]])
            P.op("act", lambda e: e.activation(out=sq[:], in_=pp[:, :], func=AF.Square), reads=[rp], writes=[rsq])
            P.op("dve", lambda e: e.tensor_reduce(out=sm[:, 8:12], in_=v4(sq), axis=AX.X, op=ALU.add), reads=[rsq], writes=[rsm])
            P.op("act", lambda e: e.activation(out=sm[:, 12:16], in_=sm[:, 8:12], func=AF.Sqrt, scale=1.0 / 128,
                                               bias=ctxb_eps[:]), reads=[rsm, rS], writes=[rsm])
            P.op("dve", lambda e: e.reciprocal(out=sm[:, 16:20], in_=sm[:, 12:16]), reads=[rsm], writes=[rsm])
            P.op("dve", lambda e: e.tensor_tensor(out=v4(sq), in0=vap(pp[:, :], 0, [[128, 4], [1, 128]]),
                                                  in1=vap(sm[:], 16, [[1, 4], [0, 128]]), op=ALU.mult),
                 reads=[rp, rsm, rsq], writes=[rsq])
            P.op("dve", lambda e: e.tensor_tensor(out=v4(sq), in0=v4(sq), in1=vap(gain_bc[:], 0, [[0, 4], [1, 128]]), op=ALU.mult),
                 reads=[rsq, rgain], writes=[rsq])
            if rope is None:
                return sq, rsq
            tt = rope
            o2, ro2 = t32R.next()
            tmp, rtmp = t16R.next()
            cos3 = vap(cosb[:], tt * 64, [[0, 4], [32, 2], [1, 32]])
            sin3 = vap(sinb[:], tt * 64, [[0, 4], [32, 2], [1, 32]])

            def xv(t, p):
                return vap(t[:], p * 32, [[128, 4], [64, 2], [1, 32]])

            for p_ in range(2):
                P.op("dve", lambda e, p_=p_: e.tensor_tensor(out=xv(o2, p_), in0=xv(sq, p_), in1=cos3, op=ALU.mult),
                     reads=[rsq, rPS], writes=[ro2])
            P.op("dve", lambda e: e.tensor_tensor(out=xv(tmp, 0), in0=xv(sq, 1), in1=sin3, op=ALU.mult),
                 reads=[rsq, rPS], writes=[rtmp])
            P.op("dve", lambda e: e.tensor_tensor(out=xv(tmp, 1), in0=xv(sq, 0), in1=sin3, op=ALU.mult),
                 reads=[rsq, rPS, rtmp], writes=[rtmp])
            P.op("dve", lambda e: e.tensor_tensor(out=xv(o2, 0), in0=xv(o2, 0), in1=xv(tmp, 0), op=ALU.subtract),
                 reads=[rtmp, ro2], writes=[ro2])
            P.op("dve", lambda e: e.tensor_tensor(out=xv(o2, 1), in0=xv(o2, 1), in1=xv(tmp, 1), op=ALU.add),
                 reads=[rtmp, ro2], writes=[ro2])
            return o2, ro2

        def attention(i, nheads, khead_of, q_heads, local_blocks, mask_src, ctx, sink_cols, out_cols):
            G = len(q_heads)
            assert q_heads == list(range(q_heads[0], q_heads[0] + G))

            def etile(bi):
                if G == 4:
                    return ER[bi]
                t_, r_ = ER[bi // 4]
                return vap(t_, (bi % 4) * 128, [[1, 128]]), r_
            if mask_src is not None:
                nm = mask_src.shape[0]
                mt, rm_, mkey = mkR.next()
                P.dma("sp", mkey, mt[:, 0:nm, :], mask_src.rearrange("j k q -> k j q"), writes=[rm_])
            blocks = [("l", j, mi) for (j, mi) in local_blocks]
            if ctx:
                blocks += [("c", c, None) for c in range(4)]
            rq, rk, rv = res("qT"), res("kT"), res("vaug")
            rkc, rvc = res("kTc"), res("vaugc")
            for bi, (kind, j, mi) in enumerate(blocks):
                sp_, rsp = psR.next()
                Et, rE = etile(bi)
                rhs = vap(qT, q_heads[0] * T + i * 128, [[T, G], [1, 128]])
                if kind == "l":
                    lhsT = kT[:, khead_of, j * 128:(j + 1) * 128]
                    rr = [rq, rk]
                else:
                    lhsT = kTc[:, khead_of, j * 128:(j + 1) * 128]
                    rr = [rq, rkc]
                pemask = kind == "l" and isinstance(mi, tuple)
                P.op("pe", lambda e, o=sp_[:, 0:G * 128], a=lhsT, b=rhs, t=(not pemask): e.matmul(o, lhsT=a, rhs=b, start=True, stop=t),
                     reads=rr, writes=[rsp])
                if pemask:
                    P.op("pe", lambda e, o=sp_[:, 0:G * 128], b=vap(wm4[:], mi[1] * 128, [[0, G], [1, 128]]): e.matmul(
                        o, lhsT=ident[:], rhs=b, start=False, stop=True), reads=[res("wm4"), rS], writes=[rsp], acc=True)
                if kind == "l" and (mi is None or pemask):
                    P.op("act", lambda e, o=Et[:, 0:G * 128], a=sp_[:, 0:G * 128]: e.activation(
                        out=o, in_=a, func=AF.Exp, scale=SCALE), reads=[rsp], writes=[rE])
                elif kind == "l":
                    tb, rtb = t32R.next()
                    P.op("dve", lambda e, o=vap(tb[:], 0, [[128, G], [1, 128]]), a=vap(sp_[:, :], 0, [[128, G], [1, 128]]),
                         m=vap(mt[:], mi * 128, [[0, G], [1, 128]]): e.scalar_tensor_tensor(
                             out=o, in0=a, scalar=SCALE, in1=m, op0=ALU.mult, op1=ALU.add),
                         reads=[rsp, rm_], writes=[rtb])
                    P.op("act", lambda e, o=Et[:, 0:G * 128], a=tb[:, 0:G * 128]: e.activation(out=o, in_=a, func=AF.Exp),
                         reads=[rtb], writes=[rE])
                else:
                    P.op("act", lambda e, o=Et[:, 0:G * 128], a=sp_[:, 0:G * 128]: e.activation(
                        out=o, in_=a, func=AF.Exp, scale=SCALE, bias=ctxb[:]), reads=[rsp, rPS], writes=[rE])
            rsz = res("sz")
            for g in range(G):
                op_, rop = poR.next()
                nb = len(blocks)
                for bi, (kind, j, mi) in enumerate(blocks):
                    Et, rE = etile(bi)
                    if kind == "l":
                        rhs = vaug[:, j, khead_of, 0:129]
                        rr = [rE, rv]
                    else:
                        rhs = vaugc[:, j, khead_of, 0:129]
                        rr = [rE, rvc]
                    P.op("pe", lambda e, o=op_[:, 0:129], a=Et[:, g * 128:(g + 1) * 128], b=rhs, s=(bi == 0), t=(bi == nb - 1):
                         e.matmul(o, lhsT=a, rhs=b, start=s, stop=t), reads=rr, writes=[rop], acc=(bi > 0))
                sm, rsm = smR.next()
                if sink_cols is not None:
                    P.op("dve", lambda e, c=sink_cols[g], d=op_[:, 128:129], sm=sm: e.tensor_tensor(
                        out=sm[:, 32:33], in0=d, in1=esink[:, c:c + 1], op=ALU.add),
                         reads=[rop, res("esink")], writes=[rsm])
                    P.op("dve", lambda e, sm=sm: e.reciprocal(out=sm[:, 33:34], in_=sm[:, 32:33]), reads=[rsm], writes=[rsm])
                else:
                    P.op("dve", lambda e, d=op_[:, 128:129], sm=sm: e.reciprocal(out=sm[:, 33:34], in_=d), reads=[rop], writes=[rsm])
                oc = out_cols[g]
                szs = vap(big[:], i * 2048 + oc, [[1, 128]])
                P.op("dve", lambda e, o=szs, d=op_[:, 0:128], sm=sm: e.scalar_tensor_tensor(
                    out=o, in0=d, scalar=sm[:, 33:34], in1=o, op0=ALU.mult, op1=ALU.mult),
                     reads=[rop, rsm, rsz], writes=[rsz])

        def attention_nat(i, hg):
            static = cur["static"]
            if static:
                blocks = [("l", 2 * (i // 2), None), ("l", 2 * (i // 2) + 1, None)]
            else:
                J = NAT_J[i]
                blocks = [("l", j, jx) for jx, j in enumerate(J)] + [("c", c, None) for c in range(4)]
            rq, rk, rv = res("qT"), res("kT"), res("vaug")
            rkc, rvc = res("kTc"), res("vaugc")
            extra = None
            etl = []
            for bi in range(len(blocks)):
                if bi < 8:
                    etl.append(ER[bi])
                else:
                    if extra is None:
                        extra = t16R.next()
                    etl.append(extra)
            for bi, (kind, j, jx) in enumerate(blocks):
                sp_, rsp = psR.next()
                Et, rE = etl[bi]
                for h in range(4):
                    if kind == "l":
                        lhsT = kT[:, h, j * 128:(j + 1) * 128]
                        rr = [rq, rk]
                    else:
                        lhsT = kTc[:, h, j * 128:(j + 1) * 128]
                        rr = [rq, rkc]
                    P.op("pe", lambda e, o=sp_[:, h * 128:(h + 1) * 128], a=lhsT, b=qT[:, h, i * 128:(i + 1) * 128]:
                         e.matmul(o, lhsT=a, rhs=b, start=True, stop=True), reads=rr, writes=[rsp])
                if kind == "l" and static:
                    P.op("act", lambda e, o=Et[:, 0:512], a=sp_[:, :]: e.activation(
                        out=o, in_=a, func=AF.Exp, scale=SCALE), reads=[rsp], writes=[rE])
                elif kind == "l":
                    mt, rm_, mkey = mkR.next()
                    P.dma("sp", mkey, mt[:, 0:4, :],
                          cur["nmask"][hg * 4:(hg + 1) * 4, NAT_OFF[i] + jx, :, :].rearrange("h k q -> k h q"), writes=[rm_])
                    tb, rtb = t32R.next()
                    P.op("dve", lambda e, o=tb[:], a=sp_[:, :], m=vap(mt[:], 0, [[1, 512]]): e.scalar_tensor_tensor(
                        out=o, in0=a, scalar=SCALE, in1=m, op0=ALU.mult, op1=ALU.add), reads=[rsp, rm_], writes=[rtb])
                    P.op("act", lambda e, o=Et[:, 0:512], a=tb[:]: e.activation(out=o, in_=a, func=AF.Exp), reads=[rtb], writes=[rE])
                else:
                    P.op("act", lambda e, o=Et[:, 0:512], a=sp_[:, :]: e.activation(
                        out=o, in_=a, func=AF.Exp, scale=SCALE, bias=ctxb[:]), reads=[rsp, rPS], writes=[rE])
            rsz = res("sz")
            nb = len(blocks)
            for h in range(4):
                op_, rop = poR.next()
                for bi, (kind, j, jx) in enumerate(blocks):
                    Et, rE = etl[bi]
                    if kind == "l":
                        rhs = vaug[:, j, h, 0:129]
                        rr = [rE, rv]
                    else:
                        rhs = vaugc[:, j, h, 0:129]
                        rr = [rE, rvc]
                    P.op("pe", lambda e, o=op_[:, 0:129], a=Et[:, h * 128:(h + 1) * 128], b=rhs, s_=(bi == 0), t=(bi == nb - 1):
                         e.matmul(o, lhsT=a, rhs=b, start=s_, stop=t), reads=rr, writes=[rop], acc=(bi > 0))
                sm, rsm = smR.next()
                P.op("dve", lambda e, d=op_[:, 128:129], sm=sm: e.reciprocal(out=sm[:, 33:34], in_=d), reads=[rop], writes=[rsm])
                szs = vap(big[:], i * 2048 + (hg * 4 + h) * 128, [[1, 128]])
                P.op("dve", lambda e, o=szs, d=op_[:, 0:128], sm=sm: e.scalar_tensor_tensor(
                    out=o, in0=d, scalar=sm[:, 33:34], in1=o, op0=ALU.mult, op1=ALU.mult),
                     reads=[rop, rsm, rsz], writes=[rsz])

        def g_kv(khead_of, g):
            return khead_of

        def out_proj_residual(l, wsrc, nk, lhs, rlhs):
            rg = res("gate")
            for dg in range(4):
                for half in range(nk // KC):
                    src = cur["xin"] if (l == 0 and half == 0) else cur["y"]

                    def epi(tt, pp, rp, dg=dg, src=src):
                        xpt, rxp, key = xpR.next()
                        ry = res("y%d" % tt)
                        P.dma("sp", key, xpt[:], src[tt * 128:(tt + 1) * 128, dg * 512:(dg + 1) * 512], reads=[ry], writes=[rxp])
                        tb, rtb = t32R.next()
                        P.op("dve", lambda e: e.tensor_tensor(out=tb[:], in0=pp[:, :], in1=gate_bc[:, dg * 512:(dg + 1) * 512],
                                                              op=ALU.mult), reads=[rp, rg], writes=[rtb])
                        P.op("dve", lambda e: e.tensor_tensor(out=xpt[:], in0=xpt[:], in1=tb[:], op=ALU.add),
                             reads=[rtb, rxp], writes=[rxp])
                        P.dma("sp", "yst%d" % tt, cur["y"][tt * 128:(tt + 1) * 128, dg * 512:(dg + 1) * 512], xpt[:],
                              reads=[rxp], writes=[ry])
                    project(wsrc, half * 2048, dg, epi, lhs=lhs, rlhs=rlhs, koff=half * KC)

        def a_transpose():
            rh = res("hnT")
            for tt in range(cur["nt"]):
                transpose_to(None, lambda fc, tt=tt: vap(big[:], tt * 2048 + fc * 128, [[1, 128]]), KC, res("sz"), rh,
                             dstm=lambda i0, m, tt=tt: vap(hnT[:], i0 * T + tt * 128, [[T, m], [1, 128]]))

        def load_bc(dst, src_row, key):
            P.dma("sp", key, dst[:], src_row.partition_broadcast(128), writes=[res(key)])

        def win_layer(l, li):
            W = win_w_in[li]
            P.op("dve", lambda e: e.memset(attn_big[:, 12288:16640], 1.0), writes=[res("vaug")])
            load_bc(qn_bc, win_qn[li, :, :], "qn_bc")
            load_bc(kn_bc, win_kn[li, :, :], "kn_bc")
            P.dma("sp", "esink", esink[:], win_sink[li, :, :].partition_broadcast(128), writes=[res("esink")])
            P.op("act", lambda e: e.activation(out=esink[:], in_=esink[:], func=AF.Exp), reads=[res("esink")], writes=[res("esink")])
            static = cur["static"]
            if not static:
                P.dma("pool", "cstage", cstage[:], cur["cwk"][li].rearrange("(c p) n -> p c n", p=128),
                      reads=[res("cstage")], writes=[res("cstage")])
                for h in range(4):
                    transpose_to(lambda c, h=h: kTc[:, h, c * 128:(c + 1) * 128],
                                 lambda c, h=h: cstage[:, c, h * 128:(h + 1) * 128], 4, res("cstage"), res("kTc"))
                for c_ in range(4):
                    P.dma("pool", "vaugc", vaugc[:, c_, :, 0:128],
                          cur["cwv"][li, c_ * 128:(c_ + 1) * 128, :].rearrange("p (h d) -> p h d", d=128),
                          reads=[res("vaugc")], writes=[res("vaugc")])

            def k_epi(tt, pp, rp):
                kr, rkr = headnorm(pp, rp, kn_bc, res("kn_bc"), None if static else tt)
                P.dma("sp", "kout", cur["owk"][li, tt * 128:(tt + 1) * 128, :], kr[:], reads=[rkr])
                kb, rkb = t16R.next()
                P.op("act", lambda e: e.copy(out=kb[:], in_=kr[:]), reads=[rkr], writes=[rkb])
                transpose_to(None, lambda h: kb[:, h * 128:(h + 1) * 128], 4, rkb, res("kT"),
                             dstm=lambda i0, m: vap(kT, i0 * T + tt * 128, [[T, m], [1, 128]]))

            def v_epi(tt, pp, rp):
                vf, rvf = t32R.next()
                P.op("act", lambda e: e.copy(out=vf[:], in_=pp[:, :]), reads=[rp], writes=[rvf])
                P.dma("sp", "vout", cur["owv"][li, tt * 128:(tt + 1) * 128, :], vf[:], reads=[rvf])
                P.op("dve", lambda e: e.tensor_copy(out=vaug[:, tt, :, 0:128], in_=vap(vf[:], 0, [[128, 4], [1, 128]])),
                     reads=[rvf], writes=[res("vaug")])

            def z_epi_for(cgz):
                def z_epi(tt, pp, rp):
                    P.op("act", lambda e: e.activation(out=vap(big[:], tt * 2048 + cgz * 512, [[1, 512]]), in_=pp[:, :],
                                                       func=AF.Silu), reads=[rp], writes=[res("sz")])
                return z_epi

            def q_epi(tt, pp, rp):
                qr, rqr = headnorm(pp, rp, qn_bc, res("qn_bc"), None if static else tt)
                qb, rqb = t16R.next()
                P.op("act", lambda e: e.copy(out=qb[:], in_=qr[:]), reads=[rqr], writes=[rqb])
                transpose_to(None, lambda h: qb[:, h * 128:(h + 1) * 128], 4, rqb, res("qT"),
                             dstm=lambda i0, m: vap(qT, i0 * T + tt * 128, [[T, m], [1, 128]]))

            STG = int(os.environ.get("KSTAGE", "99"))
            project(W, 0, 4, k_epi)
            if STG < 5:
                return
            KSUB = os.environ.get("KSUB", "")
            if KSUB != "z":
                project(W, 0, 5, v_epi)
            if KSUB == "v":
                return
            for cgz in range(4):
                project(W, 0, 6 + cgz, z_epi_for(cgz))
            if STG < 6:
                return
            for g in range(4):
                if STG < 8 and g > 0:
                    return
                project(W, 0, g, q_epi)
                if STG < 7:
                    return
                for i in range(cur["nt"]):
                    if static:
                        loc = [(2 * (i // 2), None), (2 * (i // 2) + 1, None)]
                    else:
                        loc = []
                        if i - 1 >= 0:
                            loc.append((i - 1, ("pe", 0 if i % 2 == 0 else 1)))
                        loc.append((i, None))
                        if i + 1 < cur["nt"]:
                            loc.append((i + 1, ("pe", 2 if i % 2 == 0 else 3)))
                    msrc = None
                    attention(i, 4, g, [0, 1, 2, 3], loc, msrc, not static,
                              [4 * g + h for h in range(4)], [(4 * g + h) * 128 for h in range(4)])
            if STG < 9:
                return
            a_transpose()
            if STG < 10:
                return
            out_proj_residual(l, win_w_out[li], KC, hnT, res("hnT"))

        def nat_layer(l):
            W = nat_w_in[0]
            P.op("dve", lambda e: e.memset(attn_big[:, 12288:16640], 1.0), writes=[res("vaug")])
            load_bc(qn_bc, nat_qn[0, :, :], "qn_bc")
            load_bc(kn_bc, nat_kn[0, :, :], "kn_bc")

            def z_epi_for(cgz):
                def z_epi(tt, pp, rp):
                    P.op("act", lambda e: e.activation(out=vap(big[:], tt * 2048 + cgz * 512, [[1, 512]]), in_=pp[:, :],
                                                       func=AF.Silu), reads=[rp], writes=[res("sz")])
                return z_epi

            for cgz in range(4):
                project(W, 0, 12 + cgz, z_epi_for(cgz))
            for hg in range(4):
                def k_epi(tt, pp, rp, hg=hg):
                    kr, rkr = headnorm(pp, rp, kn_bc, res("kn_bc"), None)
                    P.dma("sp", "kout", cur["onk"][tt * 128:(tt + 1) * 128, hg * 512:(hg + 1) * 512], kr[:], reads=[rkr])
                    kb, rkb = t16R.next()
                    P.op("act", lambda e: e.copy(out=kb[:], in_=kr[:]), reads=[rkr], writes=[rkb])
                    transpose_to(None, lambda h: kb[:, h * 128:(h + 1) * 128], 4, rkb, res("kT"),
                             dstm=lambda i0, m: vap(kT, i0 * T + tt * 128, [[T, m], [1, 128]]))

                def v_epi(tt, pp, rp, hg=hg):
                    vf, rvf = t32R.next()
                    P.op("act", lambda e: e.copy(out=vf[:], in_=pp[:, :]), reads=[rp], writes=[rvf])
                    P.dma("sp", "vout", cur["onv"][tt * 128:(tt + 1) * 128, hg * 512:(hg + 1) * 512], vf[:], reads=[rvf])
                    P.op("dve", lambda e: e.tensor_copy(out=vaug[:, tt, :, 0:128], in_=vap(vf[:], 0, [[128, 4], [1, 128]])),
                         reads=[rvf], writes=[res("vaug")])

                def q_epi(tt, pp, rp):
                    qr, rqr = headnorm(pp, rp, qn_bc, res("qn_bc"), None)
                    qb, rqb = t16R.next()
                    P.op("act", lambda e: e.copy(out=qb[:], in_=qr[:]), reads=[rqr], writes=[rqb])
                    transpose_to(None, lambda h: qb[:, h * 128:(h + 1) * 128], 4, rqb, res("qT"),
                             dstm=lambda i0, m: vap(qT, i0 * T + tt * 128, [[T, m], [1, 128]]))

                static = cur["static"]
                project(W, 0, 4 + hg, k_epi)
                project(W, 0, 8 + hg, v_epi)
                if not static:
                    if hg == 0:
                        P.dma("pool", "cstage", cstage[:], cur["cnk"][:, 0:512].rearrange("(c p) n -> p c n", p=128),
                              reads=[res("cstage")], writes=[res("cstage")])
                    for h in range(4):
                        transpose_to(lambda c, h=h: kTc[:, h, c * 128:(c + 1) * 128],
                                     lambda c, h=h: cstage[:, c, h * 128:(h + 1) * 128], 4, res("cstage"), res("kTc"))
                    if hg < 3:
                        P.dma("pool", "cstage", cstage[:],
                              cur["cnk"][:, (hg + 1) * 512:(hg + 2) * 512].rearrange("(c p) n -> p c n", p=128),
                              reads=[res("cstage")], writes=[res("cstage")])
                    for c_ in range(4):
                        P.dma("pool", "vaugc", vaugc[:, c_, :, 0:128],
                              cur["cnv"][c_ * 128:(c_ + 1) * 128, hg * 512:(hg + 1) * 512].rearrange("p (h d) -> p h d", d=128),
                              reads=[res("vaugc")], writes=[res("vaugc")])
                project(W, 0, hg, q_epi)
                for i in range(cur["nt"]):
                    J = NAT_J[i]
                    attention_nat(i, hg)
                    continue
                    for h in range(4):
                        head = hg * 4 + h
                        if static:
                            attention(i, 1, h, [h], [(2 * (i // 2), None), (2 * (i // 2) + 1, None)], None, False, None,
                                      [head * 128])
                        else:
                            attention(i, 1, h, [h], [(j, jx) for jx, j in enumerate(J)],
                                      cur["nmask"][head, NAT_OFF[i]:NAT_OFF[i] + len(J), :, :], True, None, [head * 128])
            a_transpose()
            out_proj_residual(l, nat_w_out[0], KC, hnT, res("hnT"))

        def gmlp_layer(l):
            W = g_w_in[0]

            def chunk(fc, off, pairs):
                base = big if fc < 16 else attn_big
                return vap(base[:], (fc % 16) * T + off, pairs)

            P.dma("pool", "wsn", wsn[:], g_w_s[0].rearrange("g i j -> i g j"), writes=[res("wsn")])
            transpose_to(lambda g: wsT[:, g, :], lambda g: wsn[:, g, :], 16, res("wsn"), res("wsT"))
            rlb = res("lb2")
            rrb = res("rb2")
            P.dma("sp", "lb2", lb2[1:2, :], ones_d[0:1, 0:512], writes=[rlb])
            rvh = [res("vh%d" % fc) for fc in range(32)]
            rgs = res("gstat")

            def v_epi_for(cgv):
                def v_epi(tt, pp, rp):
                    tb, rtb = t32R.next()
                    P.op("act", lambda e: e.activation(out=tb[:], in_=pp[:, :], func=AF.Gelu_apprx_tanh,
                                                       accum_out=gstat[:, 0, tt, cgv:cgv + 1]), reads=[rp], writes=[rtb, rgs])
                    jb, rjb = t16R.next()
                    P.op("act", lambda e: e.activation(out=jb[:], in_=tb[:], func=AF.Square,
                                                       accum_out=gstat[:, 1, tt, cgv:cgv + 1]), reads=[rtb], writes=[rjb, rgs])
                    P.op("dve", lambda e: e.tensor_copy(out=chunk(4 * cgv, tt * 128, [[T, 4], [1, 128]]),
                                                        in_=vap(tb[:], 0, [[128, 4], [1, 128]])),
                         reads=[rtb], writes=[rvh[4 * cgv + q] for q in range(4)])
                return v_epi
            for cgv in range(8):
                project(W, 0, 8 + cgv, v_epi_for(cgv))
            rsm = res("small3")
            P.op("dve", lambda e: e.tensor_reduce(out=small[:, 40:56], in_=vap(gstat[:], 0, [[8, 16], [1, 8]]), axis=AX.X,
                                                  op=ALU.add), reads=[rgs], writes=[rsm])
            P.op("dve", lambda e: e.tensor_scalar(out=small[:, 40:56], in0=small[:, 40:56], scalar1=1.0 / 4096, scalar2=None,
                                                  op0=ALU.mult), reads=[rsm], writes=[rsm])
            P.op("dve", lambda e: e.tensor_tensor(out=small[:, 56:64], in0=small[:, 40:48], in1=small[:, 40:48], op=ALU.mult),
                 reads=[rsm], writes=[rsm])
            P.op("dve", lambda e: e.tensor_tensor(out=small[:, 48:56], in0=small[:, 48:56], in1=small[:, 56:64], op=ALU.subtract),
                 reads=[rsm], writes=[rsm])
            P.op("act", lambda e: e.activation(out=small[:, 48:56], in_=small[:, 48:56], func=AF.Sqrt, bias=ctxb_eps[:]),
                 reads=[rsm, rS], writes=[rsm])
            P.op("dve", lambda e: e.reciprocal(out=small[:, 48:56], in_=small[:, 48:56]), reads=[rsm], writes=[rsm])
            P.op("dve", lambda e: e.scalar_tensor_tensor(out=small[:, 56:64], in0=small[:, 40:48], scalar=-1.0, in1=small[:, 48:56],
                                                         op0=ALU.mult, op1=ALU.mult), reads=[rsm], writes=[rsm])
            for tt in range(cur["nt"]):
                for half in range(2):
                    view = chunk(16 * half, tt * 128, [[T, 16], [1, 128]])
                    rr_ = rvh[16 * half:16 * half + 16]
                    P.op("act", lambda e, v=view, tt=tt: e.activation(out=v, in_=v, func=AF.Identity,
                                                                      scale=small[:, 48 + tt:49 + tt],
                                                                      bias=small[:, 56 + tt:57 + tt]),
                         reads=[rsm] + rr_, writes=rr_)
            rsv = res("svbuf")
            rlng = res("lng")
            for cgu in range(8):
                P.dma("pool", "lng", lng[:], g_ln_g[0, :, cgu * 512:(cgu + 1) * 512].partition_broadcast(128), writes=[rlng])
                P.dma("sp", "lb2", lb2[0:1, :], g_ln_b[0, :, cgu * 512:(cgu + 1) * 512], writes=[rlb])
                P.dma("sp", "rb2", rb2[1:2, :], g_b_s[0, :, cgu * 256:(cgu + 1) * 256], writes=[rrb])
                pr, rpr = poR.next()
                P.op("pe", lambda e, o=pr[0:1, 0:256], b=vap(wsT[:], 2 * cgu * 128, [[1, 256]]): e.matmul(
                    o, lhsT=onesb[:, 0:1], rhs=b, start=True, stop=True), reads=[res("wsT"), res("onesb")], writes=[rpr])
                P.op("act", lambda e, a=pr[0:1, 0:256]: e.copy(out=rb2[0:1, :], in_=a), reads=[rpr], writes=[rrb])
                for tt in range(cur["nt"]):
                    v4 = chunk(4 * cgu, tt * 128, [[T, 4], [1, 128]])
                    r4 = rvh[4 * cgu:4 * cgu + 4]
                    P.op("dve", lambda e, v=v4: e.tensor_tensor(out=v, in0=v, in1=vap(lng[:], 0, [[128, 4], [1, 128]]), op=ALU.mult),
                         reads=[rlng] + r4, writes=r4)
                    pp, rp = psR.next()
                    for gi in range(2):
                        g = 2 * cgu + gi
                        P.op("pe", lambda e, o=pp[:, gi * 256:(gi + 1) * 256], a=wsT[:, g, :],
                             b=chunk(2 * g, tt * 128, [[T, 2], [1, 128]]): e.matmul(o, lhsT=a, rhs=b, start=True, stop=False),
                             reads=[res("wsT"), rvh[2 * g], rvh[2 * g + 1]], writes=[rp])
                        P.op("pe", lambda e, o=pp[:, gi * 256:(gi + 1) * 256], a=rb2[0:2, gi * 128:(gi + 1) * 128],
                             b=lb2[0:2, gi * 256:(gi + 1) * 256]: e.matmul(o, lhsT=a, rhs=b, start=False, stop=True),
                             reads=[rrb, rlb], writes=[rp], acc=True)
                    P.op("act", lambda e, o=svbuf[:, tt, :], a=pp[:, :]: e.copy(out=o, in_=a), reads=[rp], writes=[rsv])
                slot_u, rwu = load_w(W, 0, cgu * 512)
                for tt in range(cur["nt"]):
                    pu, rpu = pjR.next()
                    for kc in range(KC):
                        P.op("pe", lambda e, o=pu[:, :], a=hnT[:, kc, tt * 128:(tt + 1) * 128], b=slot_u[:, kc, :], s=(kc == 0),
                             t=(kc == KC - 1): e.matmul(o, lhsT=a, rhs=b, start=s, stop=t), reads=[rwu, res("hnT")], writes=[rpu], acc=(kc > 0))
                    gu, rgu = t16R.next()
                    P.op("act", lambda e, o=gu[:], a=pu[:, :]: e.activation(out=o, in_=a, func=AF.Gelu_apprx_tanh), reads=[rpu], writes=[rgu])
                    P.op("dve", lambda e, o=svbuf[:, tt, :], b=gu[:]: e.tensor_tensor(out=o, in0=o, in1=b, op=ALU.mult),
                         reads=[rsv, rgu], writes=[rsv])
                slot_z, rwz = load_w(W, 0, (16 + cgu) * 512)

                def z_epi(tt, pz, rpz, cgu=cgu):
                    gz, rgz = t16R.next()
                    P.op("act", lambda e, o=gz[:], a=pz[:, :]: e.activation(out=o, in_=a, func=AF.Silu), reads=[rpz], writes=[rgz])
                    P.op("dve", lambda e, o=gz[:], b=svbuf[:, tt, :]: e.tensor_tensor(out=o, in0=o, in1=b, op=ALU.mult),
                         reads=[rsv, rgz], writes=[rgz])
                    pt, rpt = ptR.next()
                    for q in range(4):
                        P.op("pe", lambda e, o=pt[:, q * 128:(q + 1) * 128], s_=gz[:, q * 128:(q + 1) * 128]: e.transpose(
                            out=o, in_=s_, identity=ident[:]), reads=[rgz, rS], writes=[rpt])
                    P.op("dve", lambda e, o=chunk(4 * cgu, tt * 128, [[T, 4], [1, 128]]),
                         s_=vap(pt[:, :], 0, [[128, 4], [1, 128]]): e.tensor_copy(out=o, in_=s_),
                         reads=[rpt], writes=[rvh[4 * cgu + q] for q in range(4)])

                pend = []
                for tt in range(cur["nt"]):
                    pz, rpz = pjR.next()
                    for kc in range(KC):
                        P.op("pe", lambda e, o=pz[:, :], a=hnT[:, kc, tt * 128:(tt + 1) * 128], b=slot_z[:, kc, :], s=(kc == 0),
                             t=(kc == KC - 1): e.matmul(o, lhsT=a, rhs=b, start=s, stop=t), reads=[rwz, res("hnT")], writes=[rpz], acc=(kc > 0))
                    pend.append((tt, pz, rpz))
                    if len(pend) > EPI_DELAY:
                        z_epi(*pend.pop(0))
                for p_ in pend:
                    z_epi(*p_)

            class L:
                def __getitem__(self, key):
                    _, kk, sl = key
                    return chunk(kk, sl.start, [[1, sl.stop - sl.start]])
            rall = res("aTall")
            P.op("dve", lambda e: e.memset(small[:, 63:64], 0.0), reads=rvh, writes=[rall, res("small4")])
            out_proj_residual(l, g_w_out[0], 32, L(), rall)

        ctxb_eps = sb("epsb", [128, 1], F32)
        P.op("dve", lambda e: e.memset(ctxb_eps[:], EPS), writes=[rS])

        kinds = [0, 1, 2, 0]
        first = True
        for pi in range(len(passes)):
            cur.clear()
            cur.update(PT[pi])
            STAGE = int(os.environ.get("KSTAGE", "99"))
            if STAGE < 1:
                break
            pass_setup()
            for l in range(nlayers):
                if kinds[l] == 2 or (l > 0 and kinds[l - 1] == 2):
                    for e_ in ENGS:
                        P.wait_all(e_)
                first = False
                if STAGE < 2:
                    break
                adaln(l)
                if STAGE < 3:
                    break
                norm_phase(l)
                if STAGE < 4:
                    break
                k = kinds[l]
                if k == 0:
                    win_layer(l, l // 3)
                elif k == 1:
                    nat_layer(l)
                else:
                    gmlp_layer(l)
        for e in ENGS:
            P.wait_all(e)

        esem = {e: st.enter_context(nc.semaphore("s_" + e)) for e in ENGS}
        dsem = {k: st.enter_context(nc.semaphore("d_" + k)) for k in P.dcount}
        block = st.enter_context(nc.Block())
        P.emit(block, esem, dsem)
    return nc


NCORES = 4
PASSES = ((8, False), (4, True)) if os.environ.get('KPASSES', '2') == '2' else ((8, False),)
ASSIGN = {0: (("s", 0), [12, 13]), 1: (("s", 1), [14, 15]),
          2: (("p", [0, 1, 2, 3]), [4, 5]), 3: (("p", [6, 7, 8, 9]), [10, 11])}


def _win_mask(sample):
    m = np.full((8, 3, 128, 128), NEG, np.float32)
    a = np.arange(128)[:, None]
    b = np.arange(128)[None, :]
    for i in range(8):
        m[i, 1] = 0.0
        if sample:
            m[i, 0] = np.where(b <= a, 0.0, NEG)
            m[i, 2] = np.where(a <= b, 0.0, NEG)
        else:
            if i % 2 == 1:
                m[i, 0] = 0.0
            else:
                m[i, 2] = 0.0
    return m


def _nat_mask(sample, rel_bias):
    m = np.full((16, NAT_NB, 128, 128), NEG, np.float32)
    a = np.arange(128)
    for i in range(8):
        for jx, j in enumerate(NAT_J[i]):
            blk = NAT_OFF[i] + jx
            if not sample:
                if j // 2 == i // 2:
                    m[:, blk] = 0.0
                continue
            krow = (2 * j + a // 64)[:, None]
            kcol = (a % 64)[:, None]
            qrow = (2 * i + a // 64)[None, :]
            qcol = (a % 64)[None, :]
            rs = np.clip(qrow - 4, 0, 8)
            cs = np.clip(qcol - 8, 0, 48)
            valid = (krow >= rs) & (krow < rs + 8) & (kcol >= cs) & (kcol < cs + 16)
            dr = np.clip(krow - qrow + 7, 0, 14)
            dc = np.clip(kcol - qcol + 15, 0, 30)
            g = rel_bias[:, dr, dc]
            m[:, blk] = np.where(valid[None], g, np.float32(NEG))
    return m


def _rope_tables(sample):
    if not sample:
        return np.ones((T, 64), np.float32), np.zeros((T, 64), np.float32)
    t = np.arange(T)
    row = (t // 64).astype(np.float32)
    col = (t % 64).astype(np.float32)
    inv = (np.float32(10000.0) ** (-np.arange(32, dtype=np.float32) / np.float32(32))).astype(np.float32)
    ang = np.concatenate([row[:, None] * inv[None], col[:, None] * inv[None]], axis=1).astype(np.float32)
    return np.cos(ang).astype(np.float32), np.sin(ang).astype(np.float32)


_NC_CACHE = {}


def kernel(x_prompt, x_sample, cache_win_k, cache_win_v, cache_nat_k, cache_nat_v, c, c_ctx,
           norm_g, w_ada, b_ada, win_w_in, win_q_norm, win_k_norm, win_sink, win_w_out,
           nat_w_in, nat_q_norm, nat_k_norm, nat_rel_bias, nat_w_out,
           gmlp_w_in, gmlp_ln_g, gmlp_ln_b, gmlp_w_s, gmlp_b_s, gmlp_w_out, _nlayers=4):
    f = lambda a: np.ascontiguousarray(np.asarray(a, dtype=np.float32))
    x_prompt, x_sample = f(x_prompt), f(x_sample)
    w_ada_ = np.asarray(w_ada)
    win_w_in_ = np.asarray(win_w_in)
    win_w_out_ = np.asarray(win_w_out)
    shared = {
        "ident": np.eye(128, dtype=np.float32),
        "onesrow": np.ones((1, 4096), np.float32),
        "normg": f(np.asarray(norm_g).reshape(4, 16, 128).transpose(0, 2, 1)),
        "b_ada": f(np.asarray(b_ada).reshape(4, 1, 6144)),
        "win_qn": f(np.asarray(win_q_norm).reshape(2, 1, 128)),
        "win_kn": f(np.asarray(win_k_norm).reshape(2, 1, 128)), "win_sink": f(np.asarray(win_sink).reshape(2, 1, 16)),
        "nat_w_in": f(np.asarray(nat_w_in)[0]), "nat_qn": f(np.asarray(nat_q_norm).reshape(1, 1, 128)),
        "nat_kn": f(np.asarray(nat_k_norm).reshape(1, 1, 128)), "nat_w_out": f(np.asarray(nat_w_out)[0]),
        "g_w_in": f(np.asarray(gmlp_w_in)[0]), "g_ln_g": f(np.asarray(gmlp_ln_g).reshape(1, 1, 4096)),
        "g_ln_b": f(np.asarray(gmlp_ln_b).reshape(1, 1, 4096)), "g_w_s": f(gmlp_w_s),
        "g_b_s": f(np.asarray(gmlp_b_s).reshape(1, 1, 2048)), "g_w_out": f(np.asarray(gmlp_w_out)[0]),
    }
    for l_ in range(4):
        shared["w_ada%d" % l_] = f(w_ada_[l_])
    for i_ in range(2):
        shared["win_w_in%d" % i_] = f(win_w_in_[i_])
        shared["win_w_out%d" % i_] = f(win_w_out_[i_])
    rel_bias = f(nat_rel_bias)[0]
    wm = {False: _win_mask(False), True: _win_mask(True)}
    nm = {False: _nat_mask(False, rel_bias), True: _nat_mask(True, rel_bias)}
    rp = {False: _rope_tables(False), True: _rope_tables(True)}
    shared["zmask"] = np.zeros((2, 128, 128), np.float32)
    c = f(c)
    c_ctx = f(c_ctx)
    in_maps = []
    for core in range(NCORES):
        (kindA, whatA), seqsB = ASSIGN[core]
        sample = kindA == "s"
        m = dict(shared)
        if sample:
            b = whatA
            xa = x_sample[b]
            cv = c[b]
            m["cwk_0"] = f(np.asarray(cache_win_k)[b].reshape(2, 512, 512))
            m["cwv_0"] = f(np.asarray(cache_win_v)[b].reshape(2, 512, 512))
            m["cnk_0"] = f(np.asarray(cache_nat_k)[b, 0].reshape(512, 2048))
            m["cnv_0"] = f(np.asarray(cache_nat_v)[b, 0].reshape(512, 2048))
            m["ctxb_0"] = np.zeros((128, 1), np.float32)
        else:
            xa = np.concatenate([x_prompt[sq] for sq in whatA], axis=0)
            cv = c_ctx
            m["cwk_0"] = np.zeros((2, 512, 512), np.float32)
            m["cwv_0"] = np.zeros((2, 512, 512), np.float32)
            m["cnk_0"] = np.zeros((512, 2048), np.float32)
            m["cnv_0"] = np.zeros((512, 2048), np.float32)
            m["ctxb_0"] = np.full((128, 1), NEG, np.float32)
        m["xin_0"] = f(xa)
        m["cond_0"] = f(cv.reshape(16, 128).T)
        m["wmask_0"] = wm[sample]
        m["nmask_0"] = nm[sample]
        m["ropec_0"], m["ropes_0"] = rp[sample]
        m["xin_1"] = f(np.concatenate([x_prompt[sq] for sq in seqsB], axis=0))
        m["cond_1"] = f(c_ctx.reshape(16, 128).T)
        in_maps.append(m)
    key = (_nlayers,)
    if key not in _NC_CACHE:
        _NC_CACHE[key] = build(_nlayers, PASSES)
    nc = _NC_CACHE[key]
    res = run_bass_kernel_spmd(nc, in_maps, core_ids=list(range(NCORES)))
    outs = res.results
    y_prompt = np.zeros((16, 256, D), np.float32)
    y_sample = np.zeros((2, 1024, D), np.float32)
    nwk = np.zeros((16, 2, 256, 4, 128), np.float32)
    nwv = np.zeros((16, 2, 256, 4, 128), np.float32)
    nnk = np.zeros((16, 1, 256, 16, 128), np.float32)
    nnv = np.zeros((16, 1, 256, 16, 128), np.float32)

    def take(o, sfx, seqs):
        for s_, sq in enumerate(seqs):
            sl = slice(s_ * 256, (s_ + 1) * 256)
            y_prompt[sq] = o["y" + sfx][sl]
            for li in range(2):
                nwk[sq, li] = o["owk" + sfx][li, sl].reshape(256, 4, 128)
                nwv[sq, li] = o["owv" + sfx][li, sl].reshape(256, 4, 128)
            nnk[sq, 0] = o["onk" + sfx][sl].reshape(256, 16, 128)
            nnv[sq, 0] = o["onv" + sfx][sl].reshape(256, 16, 128)

    for core in range(NCORES):
        (kindA, whatA), seqsB = ASSIGN[core]
        o = outs[core]
        if kindA == "s":
            y_sample[whatA] = o["y_0"]
        else:
            take(o, "_0", whatA)
        if len(PASSES) > 1:
            take(o, "_1", seqsB)
    return (y_prompt, y_sample, nwk, nwv, nnk, nnv)
```

```python
import os
import numpy as np
import concourse.bass as bass
import concourse.mybir as mybir
from concourse.bass_utils import run_bass_kernel_spmd

F32 = mybir.dt.float32
BF16 = mybir.dt.bfloat16
AF = mybir.ActivationFunctionType
ALU = mybir.AluOpType
AX = mybir.AxisListType

D = 2048
T = 1024
NT = 8
KC = 16
NEG = -30000.0
EPS = 1e-6
SCALE = 128.0 ** -0.5
ENGS = ["pe", "act", "dve", "pool", "sp"]
EPI_DELAY = 2

NAT_J = {0: [0, 1, 2, 3], 1: [0, 1, 2, 3], 2: [0, 1, 2, 3, 4], 3: [1, 2, 3, 4, 5],
         4: [2, 3, 4, 5, 6], 5: [3, 4, 5, 6, 7], 6: [4, 5, 6, 7], 7: [4, 5, 6, 7]}
NAT_OFF = {}
_o = 0
for _i in range(8):
    NAT_OFF[_i] = _o
    _o += len(NAT_J[_i])
NAT_NB = _o


class Res:
    __slots__ = ("name", "w", "r")

    def __init__(self, name):
        self.name = name
        self.w = None
        self.r = {}


class Prog:
    def __init__(self, nc):
        self.nc = nc
        self.ops = {e: [] for e in ENGS}
        self.signal = {e: set() for e in ENGS}
        self.seen = {e: {} for e in ENGS}
        self.dcount = {}

    def _deps(self, eng, reads, writes, acc):
        evs = []
        for r in reads:
            if r.w is not None:
                evs.append(r.w)
        for w in writes:
            if w.w is not None:
                if not (acc and w.w[0] == "E" and w.w[1] == "pe" and eng == "pe"):
                    evs.append(w.w)
            evs.extend(w.r.values())
        out = []
        for ev in evs:
            kind, key, val = ev
            if kind == "D":
                val = self.dcount[key]
            if self.seen[eng].get((kind, key), -1) >= val:
                continue
            self.seen[eng][(kind, key)] = val
            out.append((kind, key, val))
            if kind == "E":
                self.signal[key].add(val)
        return out

    def op(self, eng, fn, reads=(), writes=(), acc=False):
        waits = self._deps(eng, reads, writes, acc)
        idx = len(self.ops[eng])
        ev = ("E", eng, idx)
        self.ops[eng].append((waits, fn, None))
        for r in reads:
            r.r[eng] = ev
        for w in writes:
            w.w = ev
            w.r = {}
        return ev

    def dma(self, eng, key, out, in_, reads=(), writes=(), **kw):
        key = key + "_" + eng
        waits = self._deps(eng, reads, writes, False)
        self.dcount[key] = self.dcount.get(key, 0) + 16
        ev = ("D", key, self.dcount[key])
        self.ops[eng].append((waits, (lambda e, o=out, i=in_, k=kw: e.dma_start(out=o, in_=i, **k)), key))
        for r in reads:
            r.r["D" + key] = ev
        for w in writes:
            w.w = ev
            w.r = {}
        return ev

    def wait_all(self, eng):
        waits = []
        for e in ENGS:
            n = len(self.ops[e])
            if n == 0:
                continue
            for idx in range(n - 1, -1, -1):
                if self.ops[e][idx][2] is None and self.ops[e][idx][1] is not None:
                    if self.seen[eng].get(("E", e), -1) < idx:
                        waits.append(("E", e, idx))
                        self.signal[e].add(idx)
                        self.seen[eng][("E", e)] = idx
                    break
        for key, cnt in self.dcount.items():
            if self.seen[eng].get(("D", key), -1) < cnt:
                waits.append(("D", key, cnt))
                self.seen[eng][("D", key)] = cnt
        self.ops[eng].append((waits, None, None))

    def emit(self, block, esem, dsem):
        tick = {}
        for e in ENGS:
            tick[e] = {}
            c = 0
            for idx in sorted(self.signal[e]):
                c += 1
                tick[e][idx] = c

        def run(e, h):
            for idx, (waits, fn, dkey) in enumerate(self.ops[e]):
                for kind, key, val in waits:
                    if kind == "E":
                        h.wait_ge(esem[key], tick[key][val])
                    else:
                        h.wait_ge(dsem[key], val)
                if fn is None:
                    continue
                ins = fn(h)
                if dkey is not None:
                    ins.then_inc(dsem[dkey], 16)
                elif idx in self.signal[e]:
                    ins.then_inc(esem[e], 1)

        @block.tensor
        def _(h):
            run("pe", h)

        @block.scalar
        def _(h):
            run("act", h)

        @block.vector
        def _(h):
            run("dve", h)

        @block.gpsimd
        def _(h):
            run("pool", h)

        @block.sync
        def _(h):
            run("sp", h)


class Ring:
    def __init__(self, items):
        self.items = items
        self.i = 0

    def next(self):
        it = self.items[self.i % len(self.items)]
        self.i += 1
        return it


def vap(ap, off, pairs, parts=None):
    p = ap.ap[0]
    n = p[1] if parts is None else parts
    return bass.AP(ap.tensor, ap.offset + off, [[p[0], n]] + [list(x) for x in pairs])


def build(nlayers=4, passes=((8, False), (4, True))):
    nc = bass.Bass("TRN2", target_bir_lowering=False)
    P = Prog(nc)

    def din(name, shape):
        return nc.dram_tensor(name, list(shape), F32, kind="ExternalInput").ap()

    def dout(name, shape):
        return nc.dram_tensor(name, list(shape), F32, kind="ExternalOutput").ap()

    PT = []
    for pi, (ntp, static_prompt) in enumerate(passes):
        d_ = {"nt": ntp, "static": static_prompt, "pi": pi}
        sfx = "_%d" % pi
        d_["xin"] = din("xin" + sfx, [ntp * 128, D])
        d_["cond"] = din("cond" + sfx, [128, 16])
        d_["y"] = dout("y" + sfx, [ntp * 128, D])
        d_["owk"] = dout("owk" + sfx, [2, ntp * 128, 512])
        d_["owv"] = dout("owv" + sfx, [2, ntp * 128, 512])
        d_["onk"] = dout("onk" + sfx, [ntp * 128, D])
        d_["onv"] = dout("onv" + sfx, [ntp * 128, D])
        if not static_prompt:
            d_["wmask"] = din("wmask" + sfx, [8, 3, 128, 128])
            d_["ctxb_d"] = din("ctxb" + sfx, [128, 1])
            d_["ropec"] = din("ropec" + sfx, [ntp * 128, 64])
            d_["ropes"] = din("ropes" + sfx, [ntp * 128, 64])
            d_["cwk"] = din("cwk" + sfx, [2, 512, 512])
            d_["cwv"] = din("cwv" + sfx, [2, 512, 512])
            if nlayers > 1:
                d_["nmask"] = din("nmask" + sfx, [16, NAT_NB, 128, 128])
                d_["cnk"] = din("cnk" + sfx, [512, 2048])
                d_["cnv"] = din("cnv" + sfx, [512, 2048])
        PT.append(d_)
    zmask = din("zmask", [2, 128, 128])
    gsc = nc.dram_tensor("gsc", [4, 2048], F32, kind="Internal").ap()
    cur = {}
    ident_d = din("ident", [128, 128])
    ones_d = din("onesrow", [1, 4096])
    normg = din("normg", [4, 128, 16])
    kinds_ = [0, 1, 2, 0][:nlayers]
    w_ada = [din("w_ada%d" % l, [D, 3 * D]) for l in range(nlayers)]
    b_ada = din("b_ada", [4, 1, 3 * D])
    nwin = len([k for k in kinds_ if k == 0])
    win_w_in = [din("win_w_in%d" % i, [D, 5120]) for i in range(nwin)]
    win_qn = din("win_qn", [2, 1, 128])
    win_kn = din("win_kn", [2, 1, 128])
    win_sink = din("win_sink", [2, 1, 16])
    win_w_out = [din("win_w_out%d" % i, [D, D]) for i in range(nwin)]
    if 1 in kinds_:
        nat_w_in = [din("nat_w_in", [D, 8192])]
        nat_qn = din("nat_qn", [1, 1, 128])
        nat_kn = din("nat_kn", [1, 1, 128])
        nat_w_out = [din("nat_w_out", [D, D])]
    if 2 in kinds_:
        g_w_in = [din("g_w_in", [D, 12288])]
        g_ln_g = din("g_ln_g", [1, 1, 4096])
        g_ln_b = din("g_ln_b", [1, 1, 4096])
        g_w_s = din("g_w_s", [1, 16, 128, 128])
        g_b_s = din("g_b_s", [1, 1, 2048])
        g_w_out = [din("g_w_out", [4096, D])]

    es = {}
    from contextlib import ExitStack
    st = ExitStack()
    with st:
        def sb(name, shape, dt):
            return st.enter_context(nc.sbuf_tensor(name, list(shape), dt))

        def pst(name, shape, dt):
            return st.enter_context(nc.psum_tensor(name, list(shape), dt))

        ident = sb("ident_sb", [128, 128], BF16)
        onesb = sb("onesb", [128, 2], BF16)
        onesf = sb("onesf", [33, 128], F32)
        scb2 = sb("scb2", [128, 16 * 33], BF16)
        condT2 = sb("condT2", [128, 16], F32)
        modA_st = sb("modA_st", [128, 64], F32)
        modB_st = sb("modB_st", [128, 64], F32)
        hnT = sb("hnT", [128, KC, T], BF16)
        big = sb("big", [128, 16 * T], BF16)
        attn_big = sb("attn_big", [128, 16640], BF16)
        wring = [sb("w%d" % i, [128, KC, 512], BF16) for i in range(2)]
        xt = [sb("xt%d" % i, [128, 2048], F32) for i in range(1)]
        xsb = [sb("xsb%d" % i, [128, 2048], BF16) for i in range(1)]
        gate_bc = sb("gate_bc", [128, 2048], F32)
        mrow = sb("mrow", [33, 512], F32)
        brow = sb("brow", [33, 512], F32)
        condT = sb("condT", [128, 16], F32)
        ng = sb("ng", [128, 16], F32)
        modA = sb("modA", [128, 16], F32)
        modB = sb("modB", [128, 16], F32)
        small = sb("small", [128, 64], F32)
        smalls = [sb("small_r%d" % i, [128, 64], F32) for i in range(3)]
        ctxb = sb("ctxb_sb", [128, 1], F32)
        qn_bc = sb("qn_bc", [128, 128], F32)
        kn_bc = sb("kn_bc", [128, 128], F32)
        esink = sb("esink", [128, 16], F32)
        cosb = sb("cosb", [128, NT, 64], F32)
        sinb = sb("sinb", [128, NT, 64], F32)
        qT = vap(attn_big[:], 0, [[T, 4], [1, T]])
        kT = vap(attn_big[:], 4096, [[T, 4], [1, T]])
        kTc = vap(attn_big[:], 8192, [[512, 4], [1, 512]])
        cstage = vap(attn_big[:], 10240, [[512, 4], [1, 512]])
        vaug = vap(attn_big[:], 12288, [[520, NT], [130, 4], [1, 130]])
        vaugc = sb("vaugc", [128, 4, 4, 130], BF16)
        t32 = [sb("t32_%d" % i, [128, 512], F32) for i in range(4)]
        t16 = [sb("t16_%d" % i, [128, 512], BF16) for i in range(4)]
        ebuf = sb("ebuf", [128, 4096], BF16)
        Eb = [vap(ebuf[:], i * 512, [[1, 512]]) for i in range(8)]
        svbuf = vap(ebuf[:], 0, [[512, NT], [1, 512]])
        mk = [sb("mk%d" % i, [128, 5, 128], F32) for i in range(2)]
        wm4 = sb("wm4", [128, 4, 128], BF16)
        xp = [sb("xp%d" % i, [128, 512], F32) for i in range(2)]
        wsn = sb("wsn", [128, 16, 128], BF16)
        wsT = sb("wsT", [128, 16, 128], BF16)
        rb2 = sb("rb2", [2, 256], F32)
        lb2 = sb("lb2", [2, 512], F32)
        lng = sb("lng", [128, 512], BF16)
        gstat = sb("gstat", [128, 2, NT, 8], F32)

        pj = [pst("pj%d" % i, [128, 512], F32) for i in range(2)]
        ps_ = [pst("ps%d" % i, [128, 512], F32) for i in range(2)]
        po = [pst("po%d" % i, [128, 512], F32) for i in range(2)]
        ptr = [pst("pt%d" % i, [128, 1024], BF16) for i in range(2)]

        R = {}

        def res(name):
            if name not in R:
                R[name] = Res(name)
            return R[name]

        pjR = Ring([(pj[i], res("pj%d" % i)) for i in range(2)] + [(ps_[i], res("ps%d" % i)) for i in range(2)])
        psR = pjR
        poR = Ring([(po[i], res("po%d" % i)) for i in range(2)])
        ptR = Ring([(ptr[i], res("pt%d" % i)) for i in range(2)])
        wR = Ring([(wring[i], res("w%d" % i), "w%d" % i) for i in range(2)])
        xtR = Ring([(xt[i], res("xt%d" % i), "xt%d" % i) for i in range(1)])
        xsR = Ring([(xsb[i], res("xsb%d" % i)) for i in range(1)])
        t32R = Ring([(t32[i], res("t32_%d" % i)) for i in range(4)])
        t16R = Ring([(t16[i], res("t16_%d" % i)) for i in range(4)])
        smR = Ring([(smalls[i], res("small_r%d" % i)) for i in range(3)])
        mkR = Ring([(mk[i], res("mk%d" % i), "mk%d" % i) for i in range(2)])
        xpR = Ring([(xp[i], res("xp%d" % i), "xp%d" % i) for i in range(2)])
        ER = [(Eb[i], res("E%d" % i)) for i in range(8)]
        evac_flip = [0]

        rS = res("setup")
        P.dma("pool", "setup", ident[:], ident_d[:, :], writes=[rS])
        P.dma("sp", "setup", onesf[0:1, :], ones_d[0:1, 0:128], writes=[rS])
        P.dma("sp", "setup", onesf[32:33, :], ones_d[0:1, 0:128], writes=[rS])
        P.op("dve", lambda e: e.memset(scb2[:], 0.0), writes=[res("scb")])
        P.op("dve", lambda e: e.memset(onesb[:], 1.0), writes=[res("onesb")])
        P.op("dve", lambda e: e.memset(vaugc[:], 1.0), writes=[res("vaugc")])
        rsc = res("scb")
        rPS = res("pass_setup")

        def pass_setup():
            P.dma("sp", "psetup", condT[:], cur["cond"][:, :], writes=[rPS])
            if not cur["static"]:
                ntp = cur["nt"]
                P.dma("sp", "psetup", ctxb[:], cur["ctxb_d"][:, :], writes=[rPS])
                P.dma("sp", "psetup", cosb[:, 0:ntp, :], cur["ropec"].rearrange("(t p) f -> p t f", p=128), writes=[rPS])
                P.dma("sp", "psetup", sinb[:, 0:ntp, :], cur["ropes"].rearrange("(t p) f -> p t f", p=128), writes=[rPS])
                for slot_, (i_, jj_) in enumerate([(2, 0), (1, 0), (0, 2), (1, 2)]):
                    P.dma("pool", "wm4", wm4[:, slot_, :], cur["wmask"][i_, jj_, :, :], writes=[res("wm4")])
            if cur["pi"] == 0:
                P.op("act", lambda e: e.activation(out=vap(scb2[:], 0, [[33, 16]]), in_=condT[:], func=AF.Silu),
                     reads=[rPS], writes=[rsc])
                if len(passes) > 1:
                    P.dma("sp", "psetup", condT2[:], PT[1]["cond"][:, :], writes=[rPS])
                    P.op("act", lambda e: e.activation(out=vap(scb2[:], 32, [[33, 16]]), in_=condT2[:], func=AF.Silu),
                         reads=[rPS], writes=[rsc])

        def load_w(src2d, r0, c0):
            slot, r, key = wR.next()
            P.dma("pool", key, slot[:], src2d[r0:r0 + 2048, c0:c0 + 512].rearrange("(kc p) n -> p kc n", p=128),
                  writes=[r])
            return slot, r

        def transpose_to(dst_fn, src_ap_fn, n, rsrc, rdst, evac=None, dstm=None):
            i = 0
            while i < n:
                m = min(4, n - i)
                pt, rpt = ptR.next()
                for jx in range(m):
                    P.op("pe", lambda e, o=pt[:, jx * 128:(jx + 1) * 128], s=src_ap_fn(i + jx): e.transpose(
                        out=o, in_=s, identity=ident[:]), reads=[rsrc, rS], writes=[rpt])
                if evac is not None:
                    evac(i, m, pt, rpt)
                elif dstm is not None:
                    dst = dstm(i, m)
                    src = vap(pt[:, :], 0, [[128, m], [1, 128]])
                    evac_flip[0] ^= 1
                    if evac_flip[0]:
                        P.op("dve", lambda e, o=dst, s=src: e.tensor_copy(out=o, in_=s), reads=[rpt], writes=[rdst])
                    else:
                        P.op("act", lambda e, o=dst, s=src: e.copy(out=o, in_=s), reads=[rpt], writes=[rdst])
                else:
                    for jx in range(m):
                        dst = dst_fn(i + jx)
                        src = pt[:, jx * 128:(jx + 1) * 128]
                        evac_flip[0] ^= 1
                        if evac_flip[0]:
                            P.op("dve", lambda e, o=dst, s=src: e.tensor_copy(out=o, in_=s), reads=[rpt], writes=[rdst])
                        else:
                            P.op("act", lambda e, o=dst, s=src: e.copy(out=o, in_=s), reads=[rpt], writes=[rdst])
                i += m

        def adaln(l):
            rm = res("mrow")
            rb = res("brow")
            rg = res("gate")
            rmod = res("mod")
            rst = res("modst")
            two = len(passes) > 1
            if cur["pi"] == 1:
                P.op("dve", lambda e: e.tensor_copy(out=modA[:], in_=modA_st[:, l * 16:(l + 1) * 16]), reads=[rst], writes=[rmod])
                P.op("dve", lambda e: e.tensor_copy(out=modB[:], in_=modB_st[:, l * 16:(l + 1) * 16]), reads=[rst], writes=[rmod])
                P.dma("sp", "gatebc", gate_bc[:], gsc[l:l + 1, :].partition_broadcast(128), reads=[res("gsc")], writes=[rg])
                return
            P.dma("sp", "ng", ng[:], normg[l, :, :], writes=[res("ng")])
            pc, rpc = psR.next()
            pcB, rpcB = psR.next()
            M = 33 if two else 1
            for cg in range(12):
                slot, rw = load_w(w_ada[l], 0, cg * 512)
                P.dma("sp", "brow", brow[0:1, :], b_ada[l, :, cg * 512:(cg + 1) * 512], writes=[rb])
                if two:
                    P.dma("sp", "brow", brow[32:33, :], b_ada[l, :, cg * 512:(cg + 1) * 512], writes=[rb])
                pp, rp = poR.next()
                for kc in range(KC):
                    P.op("pe", lambda e, o=pp[0:M, :], a=vap(scb2[:], kc * 33, [[1, M]]), b=slot[:, kc, :], s=(kc == 0), t=(kc == KC - 1):
                         e.matmul(o, lhsT=a, rhs=b, start=s, stop=t), reads=[rw, rsc], writes=[rp], acc=(kc > 0))
                P.op("dve", lambda e, a=pp[0:1, :]: e.tensor_tensor(out=mrow[0:1, :], in0=a, in1=brow[0:1, :], op=ALU.add),
                     reads=[rp, rb], writes=[rm])
                if two:
                    P.op("dve", lambda e, a=pp[32:33, :]: e.tensor_tensor(out=mrow[32:33, :], in0=a, in1=brow[32:33, :], op=ALU.add),
                         reads=[rp, rb], writes=[rm])
                if cg < 8:
                    for j in range(4):
                        col = cg * 4 + j
                        P.op("pe", lambda e, o=pc[:, 2 * col:2 * col + 2], a=mrow[0:1, j * 128:(j + 1) * 128], b=onesf[0:1, 0:2]:
                             e.matmul(o, lhsT=a, rhs=b, start=True, stop=True), reads=[rm, rS], writes=[rpc])
                        if two:
                            P.op("pe", lambda e, o=pcB[:, 2 * col:2 * col + 2], a=mrow[32:33, j * 128:(j + 1) * 128], b=onesf[32:33, 0:2]:
                                 e.matmul(o, lhsT=a, rhs=b, start=True, stop=True), reads=[rm, rS], writes=[rpcB])
                else:
                    dg = cg - 8
                    pg, rpg = poR.next()
                    P.op("pe", lambda e, o=pg[:, :], a=onesf[0:1, :], b=mrow[0:1, :]:
                         e.matmul(o, lhsT=a, rhs=b, start=True, stop=True), reads=[rm, rS], writes=[rpg])
                    P.op("act", lambda e, o=gate_bc[:, dg * 512:(dg + 1) * 512], a=pg[:, :]: e.copy(out=o, in_=a),
                         reads=[rpg], writes=[rg])
                    if two:
                        P.dma("sp", "gsc", gsc[l:l + 1, dg * 512:(dg + 1) * 512], mrow[32:33, :], reads=[rm], writes=[res("gsc")])
            P.op("dve", lambda e: e.tensor_copy(out=modB[:], in_=vap(pc[:, :], 0, [[2, 16]])), reads=[rpc], writes=[rmod])
            P.op("dve", lambda e: e.scalar_tensor_tensor(out=modA[:], in0=vap(pc[:, :], 32, [[2, 16]]), scalar=1.0, in1=ng[:],
                                                         op0=ALU.add, op1=ALU.mult), reads=[rpc, res("ng")], writes=[rmod])
            if two:
                P.op("dve", lambda e: e.tensor_copy(out=modB_st[:, l * 16:(l + 1) * 16], in_=vap(pcB[:, :], 0, [[2, 16]])),
                     reads=[rpcB], writes=[rst])
                P.op("dve", lambda e: e.scalar_tensor_tensor(out=modA_st[:, l * 16:(l + 1) * 16], in0=vap(pcB[:, :], 32, [[2, 16]]),
                                                             scalar=1.0, in1=ng[:], op0=ALU.add, op1=ALU.mult),
                     reads=[rpcB, res("ng")], writes=[rst])

        def norm_phase(l):
            src = cur["xin"] if l == 0 else cur["y"]
            rh = res("hnT")
            rmod = res("mod")
            for tt in range(cur["nt"]):
                xtile, rx, key = xtR.next()
                P.dma("sp", key, xtile[:], src[tt * 128:(tt + 1) * 128, :], reads=[res("y%d" % tt)], writes=[rx])
                xs, rxs = xsR.next()
                sm, rsm = smR.next()
                P.op("act", lambda e, o=xs[:], a=xtile[:], sm=sm: e.activation(out=o, in_=a, func=AF.Square,
                                                                                accum_out=sm[:, 0:1]),
                     reads=[rx], writes=[rxs, rsm])
                P.op("act", lambda e, sm=sm: e.activation(out=sm[:, 1:2], in_=sm[:, 0:1], func=AF.Sqrt, scale=1.0 / D,
                                                          bias=ctxb_eps[:]), reads=[rsm, rS], writes=[rsm])
                P.op("dve", lambda e, sm=sm: e.reciprocal(out=sm[:, 2:3], in_=sm[:, 1:2]), reads=[rsm], writes=[rsm])
                P.op("dve", lambda e, o=xs[:], a=xtile[:], sm=sm: e.tensor_scalar(out=o, in0=a, scalar1=sm[:, 2:3], scalar2=None,
                                                                                  op0=ALU.mult), reads=[rx, rsm, rxs], writes=[rxs])

                def evac(kc0, m, pt, rpt, tt=tt):
                    if (kc0 // 4) % 2 == 0:
                        dst = vap(hnT[:], kc0 * T + tt * 128, [[T, m], [1, 128]])
                        srcp = vap(pt[:, :], 0, [[128, m], [1, 128]])
                        P.op("dve", lambda e: e.tensor_tensor(out=dst, in0=srcp, in1=vap(modA[:], kc0, [[1, m], [0, 128]]), op=ALU.mult),
                             reads=[rpt, rmod], writes=[rh])
                        P.op("dve", lambda e: e.tensor_tensor(out=dst, in0=dst, in1=vap(modB[:], kc0, [[1, m], [0, 128]]), op=ALU.add),
                             reads=[rmod, rh], writes=[rh])
                    else:
                        for jx in range(m):
                            k = kc0 + jx
                            P.op("act", lambda e, o=hnT[:, k, tt * 128:(tt + 1) * 128], s_=pt[:, jx * 128:(jx + 1) * 128], k=k: e.activation(
                                out=o, in_=s_, func=AF.Identity, scale=modA[:, k:k + 1], bias=modB[:, k:k + 1]),
                                reads=[rpt, rmod], writes=[rh])

                transpose_to(lambda kc, tt=tt: hnT[:, kc, tt * 128:(tt + 1) * 128],
                             lambda kc, xs=xs: xs[:, kc * 128:(kc + 1) * 128], KC, rxs, rh, evac=evac)

        def project(wsrc, r0, cg, epilogue, lhs=None, rlhs=None, koff=0):
            lhs = hnT if lhs is None else lhs
            rlhs = res("hnT") if rlhs is None else rlhs
            slot, rw = load_w(wsrc, r0, cg * 512)
            pend = []
            for tt in range(cur["nt"]):
                pp, rp = pjR.next()
                for kc in range(KC):
                    P.op("pe", lambda e, o=pp[:, :], a=lhs[:, koff + kc, tt * 128:(tt + 1) * 128], b=slot[:, kc, :],
                         s=(kc == 0), t=(kc == KC - 1): e.matmul(o, lhsT=a, rhs=b, start=s, stop=t),
                         reads=[rw, rlhs], writes=[rp], acc=(kc > 0))
                pend.append((tt, pp, rp))
                if len(pend) > EPI_DELAY:
                    epilogue(*pend.pop(0))
            for p_ in pend:
                epilogue(*p_)

        def headnorm(pp, rp, gain_bc, rgain, rope):
            sq, rsq = t32R.next()
            sm, rsm = smR.next()
            v4 = lambda t: vap(t[:], 0, [[128, 4], [1, 128]])
            P.op("act", lambda e: e.activation(out=sq[:], in_=pp[:, :], func=AF.Square), reads=[rp], writes=[rsq])
            P.op("dve", lambda e: e.tensor_reduce(out=sm[:, 8:12], in_=v4(sq), axis=AX.X, op=ALU.add), reads=[rsq], writes=[rsm])
            P.op("act", lambda e: e.activation(out=sm[:, 12:16], in_=sm[:, 8:12], func=AF.Sqrt, scale=1.0 / 128,
                                               bias=ctxb_eps[:]), reads=[rsm, rS], writes=[rsm])
            P.op("dve", lambda e: e.reciprocal(out=sm[:, 16:20], in_=sm[:, 12:16]), reads=[rsm], writes=[rsm])
            P.op("dve", lambda e: e.tensor_tensor(out=v4(sq), in0=vap(pp[:, :], 0, [[128, 4], [1, 128]]),
                                                  in1=vap(sm[:], 16, [[1, 4], [0, 128]]), op=ALU.mult),
                 reads=[rp, rsm, rsq], writes=[rsq])
            P.op("dve", lambda e: e.tensor_tensor(out=v4(sq), in0=v4(sq), in1=vap(gain_bc[:], 0, [[0, 4], [1, 128]]), op=ALU.mult),
                 reads=[rsq, rgain], writes=[rsq])
            if rope is None:
                return sq, rsq
            tt = rope
            o2, ro2 = t32R.next()
            tmp, rtmp = t16R.next()
            cos3 = vap(cosb[:], tt * 64, [[0, 4], [32, 2], [1, 32]])
            sin3 = vap(sinb[:], tt * 64, [[0, 4], [32, 2], [1, 32]])

            def xv(t, p):
                return vap(t[:], p * 32, [[128, 4], [64, 2], [1, 32]])

            for p_ in range(2):
                P.op("dve", lambda e, p_=p_: e.tensor_tensor(out=xv(o2, p_), in0=xv(sq, p_), in1=cos3, op=ALU.mult),
                     reads=[rsq, rPS], writes=[ro2])
            P.op("dve", lambda e: e.tensor_tensor(out=xv(tmp, 0), in0=xv(sq, 1), in1=sin3, op=ALU.mult),
                 reads=[rsq, rPS], writes=[rtmp])
            P.op("dve", lambda e: e.tensor_tensor(out=xv(tmp, 1), in0=xv(sq, 0), in1=sin3, op=ALU.mult),
                 reads=[rsq, rPS, rtmp], writes=[rtmp])
            P.op("dve", lambda e: e.tensor_tensor(out=xv(o2, 0), in0=xv(o2, 0), in1=xv(tmp, 0), op=ALU.subtract),
                 reads=[rtmp, ro2], writes=[ro2])
            P.op("dve", lambda e: e.tensor_tensor(out=xv(o2, 1), in0=xv(o2, 1), in1=xv(tmp, 1), op=ALU.add),
                 reads=[rtmp, ro2], writes=[ro2])
            return o2, ro2

        def attention(i, nheads, khead_of, q_heads, local_blocks, mask_src, ctx, sink_cols, out_cols):
            G = len(q_heads)
            assert q_heads == list(range(q_heads[0], q_heads[0] + G))

            def etile(bi):
                if G == 4:
                    return ER[bi]
                t_, r_ = ER[bi // 4]
                return vap(t_, (bi % 4) * 128, [[1, 128]]), r_
            if mask_src is not None:
                nm = mask_src.shape[0]
                mt, rm_, mkey = mkR.next()
                P.dma("sp", mkey, mt[:, 0:nm, :], mask_src.rearrange("j k q -> k j q"), writes=[rm_])
            blocks = [("l", j, mi) for (j, mi) in local_blocks]
            if ctx:
                blocks += [("c", c, None) for c in range(4)]
            rq, rk, rv = res("qT"), res("kT"), res("vaug")
            rkc, rvc = res("kTc"), res("vaugc")
            for bi, (kind, j, mi) in enumerate(blocks):
                sp_, rsp = psR.next()
                Et, rE = etile(bi)
                rhs = vap(qT, q_heads[0] * T + i * 128, [[T, G], [1, 128]])
                if kind == "l":
                    lhsT = kT[:, khead_of, j * 128:(j + 1) * 128]
                    rr = [rq, rk]
                else:
                    lhsT = kTc[:, khead_of, j * 128:(j + 1) * 128]
                    rr = [rq, rkc]
                pemask = kind == "l" and isinstance(mi, tuple)
                P.op("pe", lambda e, o=sp_[:, 0:G * 128], a=lhsT, b=rhs, t=(not pemask): e.matmul(o, lhsT=a, rhs=b, start=True, stop=t),
                     reads=rr, writes=[rsp])
                if pemask:
                    P.op("pe", lambda e, o=sp_[:, 0:G * 128], b=vap(wm4[:], mi[1] * 128, [[0, G], [1, 128]]): e.matmul(
                        o, lhsT=ident[:], rhs=b, start=False, stop=True), reads=[res("wm4"), rS], writes=[rsp], acc=True)
                if kind == "l" and (mi is None or pemask):
                    P.op("act", lambda e, o=Et[:, 0:G * 128], a=sp_[:, 0:G * 128]: e.activation(
                        out=o, in_=a, func=AF.Exp, scale=SCALE), reads=[rsp], writes=[rE])
                elif kind == "l":
                    tb, rtb = t32R.next()
                    P.op("dve", lambda e, o=vap(tb[:], 0, [[128, G], [1, 128]]), a=vap(sp_[:, :], 0, [[128, G], [1, 128]]),
                         m=vap(mt[:], mi * 128, [[0, G], [1, 128]]): e.scalar_tensor_tensor(
                             out=o, in0=a, scalar=SCALE, in1=m, op0=ALU.mult, op1=ALU.add),
                         reads=[rsp, rm_], writes=[rtb])
                    P.op("act", lambda e, o=Et[:, 0:G * 128], a=tb[:, 0:G * 128]: e.activation(out=o, in_=a, func=AF.Exp),
                         reads=[rtb], writes=[rE])
                else:
                    P.op("act", lambda e, o=Et[:, 0:G * 128], a=sp_[:, 0:G * 128]: e.activation(
                        out=o, in_=a, func=AF.Exp, scale=SCALE, bias=ctxb[:]), reads=[rsp, rPS], writes=[rE])
            rsz = res("sz")
            for g in range(G):
                op_, rop = poR.next()
                nb = len(blocks)
                for bi, (kind, j, mi) in enumerate(blocks):
                    Et, rE = etile(bi)
                    if kind == "l":
                        rhs = vaug[:, j, khead_of, 0:129]
                        rr = [rE, rv]
                    else:
                        rhs = vaugc[:, j, khead_of, 0:129]
                        rr = [rE, rvc]
                    P.op("pe", lambda e, o=op_[:, 0:129], a=Et[:, g * 128:(g + 1) * 128], b=rhs, s=(bi == 0), t=(bi == nb - 1):
                         e.matmul(o, lhsT=a, rhs=b, start=s, stop=t), reads=rr, writes=[rop], acc=(bi > 0))
                sm, rsm = smR.next()
                if sink_cols is not None:
                    P.op("dve", lambda e, c=sink_cols[g], d=op_[:, 128:129], sm=sm: e.tensor_tensor(
                        out=sm[:, 32:33], in0=d, in1=esink[:, c:c + 1], op=ALU.add),
                         reads=[rop, res("esink")], writes=[rsm])
                    P.op("dve", lambda e, sm=sm: e.reciprocal(out=sm[:, 33:34], in_=sm[:, 32:33]), reads=[rsm], writes=[rsm])
                else:
                    P.op("dve", lambda e, d=op_[:, 128:129], sm=sm: e.reciprocal(out=sm[:, 33:34], in_=d), reads=[rop], writes=[rsm])
                oc = out_cols[g]
                szs = vap(big[:], i * 2048 + oc, [[1, 128]])
                P.op("dve", lambda e, o=szs, d=op_[:, 0:128], sm=sm: e.scalar_tensor_tensor(
                    out=o, in0=d, scalar=sm[:, 33:34], in1=o, op0=ALU.mult, op1=ALU.mult),
                     reads=[rop, rsm, rsz], writes=[rsz])

        def attention_nat(i, hg):
            static = cur["static"]
            if static:
                blocks = [("l", 2 * (i // 2), None), ("l", 2 * (i // 2) + 1, None)]
            else:
                J = NAT_J[i]
                blocks = [("l", j, jx) for jx, j in enumerate(J)] + [("c", c, None) for c in range(4)]
            rq, rk, rv = res("qT"), res("kT"), res("vaug")
            rkc, rvc = res("kTc"), res("vaugc")
            extra = None
            etl = []
            for bi in range(len(blocks)):
                if bi < 8:
                    etl.append(ER[bi])
                else:
                    if extra is None:
                        extra = t16R.next()
                    etl.append(extra)
            for bi, (kind, j, jx) in enumerate(blocks):
                sp_, rsp = psR.next()
                Et, rE = etl[bi]
                for h in range(4):
                    if kind == "l":
                        lhsT = kT[:, h, j * 128:(j + 1) * 128]
                        rr = [rq, rk]
                    else:
                        lhsT = kTc[:, h, j * 128:(j + 1) * 128]
                        rr = [rq, rkc]
                    P.op("pe", lambda e, o=sp_[:, h * 128:(h + 1) * 128], a=lhsT, b=qT[:, h, i * 128:(i + 1) * 128]:
                         e.matmul(o, lhsT=a, rhs=b, start=True, stop=True), reads=rr, writes=[rsp])
                if kind == "l" and static:
                    P.op("act", lambda e, o=Et[:, 0:512], a=sp_[:, :]: e.activation(
                        out=o, in_=a, func=AF.Exp, scale=SCALE), reads=[rsp], writes=[rE])
                elif kind == "l":
                    mt, rm_, mkey = mkR.next()
                    P.dma("sp", mkey, mt[:, 0:4, :],
                          cur["nmask"][hg * 4:(hg + 1) * 4, NAT_OFF[i] + jx, :, :].rearrange("h k q -> k h q"), writes=[rm_])
                    tb, rtb = t32R.next()
                    P.op("dve", lambda e, o=tb[:], a=sp_[:, :], m=vap(mt[:], 0, [[1, 512]]): e.scalar_tensor_tensor(
                        out=o, in0=a, scalar=SCALE, in1=m, op0=ALU.mult, op1=ALU.add), reads=[rsp, rm_], writes=[rtb])
                    P.op("act", lambda e, o=Et[:, 0:512], a=tb[:]: e.activation(out=o, in_=a, func=AF.Exp), reads=[rtb], writes=[rE])
                else:
                    P.op("act", lambda e, o=Et[:, 0:512], a=sp_[:, :]: e.activation(
                        out=o, in_=a, func=AF.Exp, scale=SCALE, bias=ctxb[:]), reads=[rsp, rPS], writes=[rE])
            rsz = res("sz")
            nb = len(blocks)
            for h in range(4):
                op_, rop = poR.next()
                for bi, (kind, j, jx) in enumerate(blocks):
                    Et, rE = etl[bi]
                    if kind == "l":
                        rhs = vaug[:, j, h, 0:129]
                        rr = [rE, rv]
                    else:
                        rhs = vaugc[:, j, h, 0:129]
                        rr = [rE, rvc]
                    P.op("pe", lambda e, o=op_[:, 0:129], a=Et[:, h * 128:(h + 1) * 128], b=rhs, s_=(bi == 0), t=(bi == nb - 1):
                         e.matmul(o, lhsT=a, rhs=b, start=s_, stop=t), reads=rr, writes=[rop], acc=(bi > 0))
                sm, rsm = smR.next()
                P.op("dve", lambda e, d=op_[:, 128:129], sm=sm: e.reciprocal(out=sm[:, 33:34], in_=d), reads=[rop], writes=[rsm])
                szs = vap(big[:], i * 2048 + (hg * 4 + h) * 128, [[1, 128]])
                P.op("dve", lambda e, o=szs, d=op_[:, 0:128], sm=sm: e.scalar_tensor_tensor(
                    out=o, in0=d, scalar=sm[:, 33:34], in1=o, op0=ALU.mult, op1=ALU.mult),
                     reads=[rop, rsm, rsz], writes=[rsz])

        def g_kv(khead_of, g):
            return khead_of

        def out_proj_residual(l, wsrc, nk, lhs, rlhs):
            rg = res("gate")
            for dg in range(4):
                for half in range(nk // KC):
                    src = cur["xin"] if (l == 0 and half == 0) else cur["y"]

                    def epi(tt, pp, rp, dg=dg, src=src):
                        xpt, rxp, key = xpR.next()
                        ry = res("y%d" % tt)
                        P.dma("sp", key, xpt[:], src[tt * 128:(tt + 1) * 128, dg * 512:(dg + 1) * 512], reads=[ry], writes=[rxp])
                        tb, rtb = t32R.next()
                        P.op("dve", lambda e: e.tensor_tensor(out=tb[:], in0=pp[:, :], in1=gate_bc[:, dg * 512:(dg + 1) * 512],
                                                              op=ALU.mult), reads=[rp, rg], writes=[rtb])
                        P.op("dve", lambda e: e.tensor_tensor(out=xpt[:], in0=xpt[:], in1=tb[:], op=ALU.add),
                             reads=[rtb, rxp], writes=[rxp])
                        P.dma("sp", "yst%d" % tt, cur["y"][tt * 128:(tt + 1) * 128, dg * 512:(dg + 1) * 512], xpt[:],
                              reads=[rxp], writes=[ry])
                    project(wsrc, half * 2048, dg, epi, lhs=lhs, rlhs=rlhs, koff=half * KC)

        def a_transpose():
            rh = res("hnT")
            for tt in range(cur["nt"]):
                transpose_to(None, lambda fc, tt=tt: vap(big[:], tt * 2048 + fc * 128, [[1, 128]]), KC, res("sz"), rh,
                             dstm=lambda i0, m, tt=tt: vap(hnT[:], i0 * T + tt * 128, [[T, m], [1, 128]]))

        def load_bc(dst, src_row, key):
            P.dma("sp", key, dst[:], src_row.partition_broadcast(128), writes=[res(key)])

        def win_layer(l, li):
            W = win_w_in[li]
            P.op("dve", lambda e: e.memset(attn_big[:, 12288:16640], 1.0), writes=[res("vaug")])
            load_bc(qn_bc, win_qn[li, :, :], "qn_bc")
            load_bc(kn_bc, win_kn[li, :, :], "kn_bc")
            P.dma("sp", "esink", esink[:], win_sink[li, :, :].partition_broadcast(128), writes=[res("esink")])
            P.op("act", lambda e: e.activation(out=esink[:], in_=esink[:], func=AF.Exp), reads=[res("esink")], writes=[res("esink")])
            static = cur["static"]
            if not static:
                P.dma("pool", "cstage", cstage[:], cur["cwk"][li].rearrange("(c p) n -> p c n", p=128),
                      reads=[res("cstage"), res("bar")], writes=[res("cstage")])
                for h in range(4):
                    transpose_to(lambda c, h=h: kTc[:, h, c * 128:(c + 1) * 128],
                                 lambda c, h=h: cstage[:, c, h * 128:(h + 1) * 128], 4, res("cstage"), res("kTc"))
                for c_ in range(4):
                    P.dma("pool", "vaugc", vaugc[:, c_, :, 0:128],
                          cur["cwv"][li, c_ * 128:(c_ + 1) * 128, :].rearrange("p (h d) -> p h d", d=128),
                          reads=[res("vaugc")], writes=[res("vaugc")])

            def k_epi(tt, pp, rp):
                kr, rkr = headnorm(pp, rp, kn_bc, res("kn_bc"), None if static else tt)
                P.dma("sp", "kout", cur["owk"][li, tt * 128:(tt + 1) * 128, :], kr[:], reads=[rkr])
                kb, rkb = t16R.next()
                P.op("act", lambda e: e.copy(out=kb[:], in_=kr[:]), reads=[rkr], writes=[rkb])
                transpose_to(None, lambda h: kb[:, h * 128:(h + 1) * 128], 4, rkb, res("kT"),
                             dstm=lambda i0, m: vap(kT, i0 * T + tt * 128, [[T, m], [1, 128]]))

            def v_epi(tt, pp, rp):
                vf, rvf = t32R.next()
                P.op("act", lambda e: e.copy(out=vf[:], in_=pp[:, :]), reads=[rp], writes=[rvf])
                P.dma("sp", "vout", cur["owv"][li, tt * 128:(tt + 1) * 128, :], vf[:], reads=[rvf])
                P.op("dve", lambda e: e.tensor_copy(out=vaug[:, tt, :, 0:128], in_=vap(vf[:], 0, [[128, 4], [1, 128]])),
                     reads=[rvf], writes=[res("vaug")])

            def z_epi_for(cgz):
                def z_epi(tt, pp, rp):
                    P.op("act", lambda e: e.activation(out=vap(big[:], tt * 2048 + cgz * 512, [[1, 512]]), in_=pp[:, :],
                                                       func=AF.Silu), reads=[rp], writes=[res("sz")])
                return z_epi

            def q_epi(tt, pp, rp):
                qr, rqr = headnorm(pp, rp, qn_bc, res("qn_bc"), None if static else tt)
                qb, rqb = t16R.next()
                P.op("act", lambda e: e.copy(out=qb[:], in_=qr[:]), reads=[rqr], writes=[rqb])
                transpose_to(None, lambda h: qb[:, h * 128:(h + 1) * 128], 4, rqb, res("qT"),
                             dstm=lambda i0, m: vap(qT, i0 * T + tt * 128, [[T, m], [1, 128]]))

            STG = int(os.environ.get("KSTAGE", "99"))
            project(W, 0, 4, k_epi)
            if STG < 5:
                return
            KSUB = os.environ.get("KSUB", "")
            if KSUB != "z":
                project(W, 0, 5, v_epi)
            if KSUB == "v":
                return
            for cgz in range(4):
                project(W, 0, 6 + cgz, z_epi_for(cgz))
            if STG < 6:
                return
            for g in range(4):
                if STG < 8 and g > 0:
                    return
                project(W, 0, g, q_epi)
                if STG < 7:
                    return
                for i in range(cur["nt"]):
                    if static:
                        loc = [(2 * (i // 2), None), (2 * (i // 2) + 1, None)]
                    else:
                        loc = []
                        if i - 1 >= 0:
                            loc.append((i - 1, ("pe", 0 if i % 2 == 0 else 1)))
                        loc.append((i, None))
                        if i + 1 < cur["nt"]:
                            loc.append((i + 1, ("pe", 2 if i % 2 == 0 else 3)))
                    msrc = None
                    attention(i, 4, g, [0, 1, 2, 3], loc, msrc, not static,
                              [4 * g + h for h in range(4)], [(4 * g + h) * 128 for h in range(4)])
            if STG < 9:
                return
            a_transpose()
            if STG < 10:
                return
            out_proj_residual(l, win_w_out[li], KC, hnT, res("hnT"))

        def nat_layer(l):
            W = nat_w_in[0]
            P.op("dve", lambda e: e.memset(attn_big[:, 12288:16640], 1.0), writes=[res("vaug")])
            load_bc(qn_bc, nat_qn[0, :, :], "qn_bc")
            load_bc(kn_bc, nat_kn[0, :, :], "kn_bc")

            def z_epi_for(cgz):
                def z_epi(tt, pp, rp):
                    P.op("act", lambda e: e.activation(out=vap(big[:], tt * 2048 + cgz * 512, [[1, 512]]), in_=pp[:, :],
                                                       func=AF.Silu), reads=[rp], writes=[res("sz")])
                return z_epi

            for cgz in range(4):
                project(W, 0, 12 + cgz, z_epi_for(cgz))
            for hg in range(4):
                def k_epi(tt, pp, rp, hg=hg):
                    kr, rkr = headnorm(pp, rp, kn_bc, res("kn_bc"), None)
                    P.dma("sp", "kout", cur["onk"][tt * 128:(tt + 1) * 128, hg * 512:(hg + 1) * 512], kr[:], reads=[rkr])
                    kb, rkb = t16R.next()
                    P.op("act", lambda e: e.copy(out=kb[:], in_=kr[:]), reads=[rkr], writes=[rkb])
                    transpose_to(None, lambda h: kb[:, h * 128:(h + 1) * 128], 4, rkb, res("kT"),
                             dstm=lambda i0, m: vap(kT, i0 * T + tt * 128, [[T, m], [1, 128]]))

                def v_epi(tt, pp, rp, hg=hg):
                    vf, rvf = t32R.next()
                    P.op("act", lambda e: e.copy(out=vf[:], in_=pp[:, :]), reads=[rp], writes=[rvf])
                    P.dma("sp", "vout", cur["onv"][tt * 128:(tt + 1) * 128, hg * 512:(hg + 1) * 512], vf[:], reads=[rvf])
                    P.op("dve", lambda e: e.tensor_copy(out=vaug[:, tt, :, 0:128], in_=vap(vf[:], 0, [[128, 4], [1, 128]])),
                         reads=[rvf], writes=[res("vaug")])

                def q_epi(tt, pp, rp):
                    qr, rqr = headnorm(pp, rp, qn_bc, res("qn_bc"), None)
                    qb, rqb = t16R.next()
                    P.op("act", lambda e: e.copy(out=qb[:], in_=qr[:]), reads=[rqr], writes=[rqb])
                    transpose_to(None, lambda h: qb[:, h * 128:(h + 1) * 128], 4, rqb, res("qT"),
                             dstm=lambda i0, m: vap(qT, i0 * T + tt * 128, [[T, m], [1, 128]]))

                static = cur["static"]
                project(W, 0, 4 + hg, k_epi)
                project(W, 0, 8 + hg, v_epi)
                if not static:
                    if hg == 0:
                        P.dma("pool", "cstage", cstage[:], cur["cnk"][:, 0:512].rearrange("(c p) n -> p c n", p=128),
                              reads=[res("cstage")], writes=[res("cstage")])
                    for h in range(4):
                        transpose_to(lambda c, h=h: kTc[:, h, c * 128:(c + 1) * 128],
                                     lambda c, h=h: cstage[:, c, h * 128:(h + 1) * 128], 4, res("cstage"), res("kTc"))
                    if hg < 3:
                        P.dma("pool", "cstage", cstage[:],
                              cur["cnk"][:, (hg + 1) * 512:(hg + 2) * 512].rearrange("(c p) n -> p c n", p=128),
                              reads=[res("cstage")], writes=[res("cstage")])
                    for c_ in range(4):
                        P.dma("pool", "vaugc", vaugc[:, c_, :, 0:128],
                              cur["cnv"][c_ * 128:(c_ + 1) * 128, hg * 512:(hg + 1) * 512].rearrange("p (h d) -> p h d", d=128),
                              reads=[res("vaugc")], writes=[res("vaugc")])
                project(W, 0, hg, q_epi)
                for i in range(cur["nt"]):
                    J = NAT_J[i]
                    attention_nat(i, hg)
                    continue
                    for h in range(4):
                        head = hg * 4 + h
                        if static:
                            attention(i, 1, h, [h], [(2 * (i // 2), None), (2 * (i // 2) + 1, None)], None, False, None,
                                      [head * 128])
                        else:
                            attention(i, 1, h, [h], [(j, jx) for jx, j in enumerate(J)],
                                      cur["nmask"][head, NAT_OFF[i]:NAT_OFF[i] + len(J), :, :], True, None, [head * 128])
            a_transpose()
            out_proj_residual(l, nat_w_out[0], KC, hnT, res("hnT"))

        def gmlp_layer(l):
            W = g_w_in[0]

            def chunk(fc, off, pairs):
                base = big if fc < 16 else attn_big
                return vap(base[:], (fc % 16) * T + off, pairs)

            P.dma("pool", "wsn", wsn[:], g_w_s[0].rearrange("g i j -> i g j"), writes=[res("wsn")])
            transpose_to(lambda g: wsT[:, g, :], lambda g: wsn[:, g, :], 16, res("wsn"), res("wsT"))
            rlb = res("lb2")
            rrb = res("rb2")
            P.dma("sp", "lb2", lb2[1:2, :], ones_d[0:1, 0:512], writes=[rlb])
            rvh = [res("vh%d" % fc) for fc in range(32)]
            rgs = res("gstat")

            def v_epi_for(cgv):
                def v_epi(tt, pp, rp):
                    tb, rtb = t32R.next()
                    P.op("act", lambda e: e.activation(out=tb[:], in_=pp[:, :], func=AF.Gelu_apprx_tanh,
                                                       accum_out=gstat[:, 0, tt, cgv:cgv + 1]), reads=[rp], writes=[rtb, rgs])
                    jb, rjb = t16R.next()
                    P.op("act", lambda e: e.activation(out=jb[:], in_=tb[:], func=AF.Square,
                                                       accum_out=gstat[:, 1, tt, cgv:cgv + 1]), reads=[rtb], writes=[rjb, rgs])
                    P.op("dve", lambda e: e.tensor_copy(out=chunk(4 * cgv, tt * 128, [[T, 4], [1, 128]]),
                                                        in_=vap(tb[:], 0, [[128, 4], [1, 128]])),
                         reads=[rtb], writes=[rvh[4 * cgv + q] for q in range(4)])
                return v_epi
            for cgv in range(8):
                project(W, 0, 8 + cgv, v_epi_for(cgv))
            rsm = res("small3")
            P.op("dve", lambda e: e.tensor_reduce(out=small[:, 40:56], in_=vap(gstat[:], 0, [[8, 16], [1, 8]]), axis=AX.X,
                                                  op=ALU.add), reads=[rgs], writes=[rsm])
            P.op("dve", lambda e: e.tensor_scalar(out=small[:, 40:56], in0=small[:, 40:56], scalar1=1.0 / 4096, scalar2=None,
                                                  op0=ALU.mult), reads=[rsm], writes=[rsm])
            P.op("dve", lambda e: e.tensor_tensor(out=small[:, 56:64], in0=small[:, 40:48], in1=small[:, 40:48], op=ALU.mult),
                 reads=[rsm], writes=[rsm])
            P.op("dve", lambda e: e.tensor_tensor(out=small[:, 48:56], in0=small[:, 48:56], in1=small[:, 56:64], op=ALU.subtract),
                 reads=[rsm], writes=[rsm])
            P.op("act", lambda e: e.activation(out=small[:, 48:56], in_=small[:, 48:56], func=AF.Sqrt, bias=ctxb_eps[:]),
                 reads=[rsm, rS], writes=[rsm])
            P.op("dve", lambda e: e.reciprocal(out=small[:, 48:56], in_=small[:, 48:56]), reads=[rsm], writes=[rsm])
            P.op("dve", lambda e: e.scalar_tensor_tensor(out=small[:, 56:64], in0=small[:, 40:48], scalar=-1.0, in1=small[:, 48:56],
                                                         op0=ALU.mult, op1=ALU.mult), reads=[rsm], writes=[rsm])
            for tt in range(cur["nt"]):
                for half in range(2):
                    view = chunk(16 * half, tt * 128, [[T, 16], [1, 128]])
                    rr_ = rvh[16 * half:16 * half + 16]
                    P.op("act", lambda e, v=view, tt=tt: e.activation(out=v, in_=v, func=AF.Identity,
                                                                      scale=small[:, 48 + tt:49 + tt],
                                                                      bias=small[:, 56 + tt:57 + tt]),
                         reads=[rsm] + rr_, writes=rr_)
            rsv = res("svbuf")
            rlng = res("lng")
            for cgu in range(8):
                P.dma("pool", "lng", lng[:], g_ln_g[0, :, cgu * 512:(cgu + 1) * 512].partition_broadcast(128), writes=[rlng])
                P.dma("sp", "lb2", lb2[0:1, :], g_ln_b[0, :, cgu * 512:(cgu + 1) * 512], writes=[rlb])
                P.dma("sp", "rb2", rb2[1:2, :], g_b_s[0, :, cgu * 256:(cgu + 1) * 256], writes=[rrb])
                pr, rpr = poR.next()
                P.op("pe", lambda e, o=pr[0:1, 0:256], b=vap(wsT[:], 2 * cgu * 128, [[1, 256]]): e.matmul(
                    o, lhsT=onesb[:, 0:1], rhs=b, start=True, stop=True), reads=[res("wsT"), res("onesb")], writes=[rpr])
                P.op("act", lambda e, a=pr[0:1, 0:256]: e.copy(out=rb2[0:1, :], in_=a), reads=[rpr], writes=[rrb])
                for tt in range(cur["nt"]):
                    v4 = chunk(4 * cgu, tt * 128, [[T, 4], [1, 128]])
                    r4 = rvh[4 * cgu:4 * cgu + 4]
                    P.op("dve", lambda e, v=v4: e.tensor_tensor(out=v, in0=v, in1=vap(lng[:], 0, [[128, 4], [1, 128]]), op=ALU.mult),
                         reads=[rlng] + r4, writes=r4)
                    pp, rp = psR.next()
                    for gi in range(2):
                        g = 2 * cgu + gi
                        P.op("pe", lambda e, o=pp[:, gi * 256:(gi + 1) * 256], a=wsT[:, g, :],
                             b=chunk(2 * g, tt * 128, [[T, 2], [1, 128]]): e.matmul(o, lhsT=a, rhs=b, start=True, stop=False),
                             reads=[res("wsT"), rvh[2 * g], rvh[2 * g + 1]], writes=[rp])
                        P.op("pe", lambda e, o=pp[:, gi * 256:(gi + 1) * 256], a=rb2[0:2, gi * 128:(gi + 1) * 128],
                             b=lb2[0:2, gi * 256:(gi + 1) * 256]: e.matmul(o, lhsT=a, rhs=b, start=False, stop=True),
                             reads=[rrb, rlb], writes=[rp], acc=True)
                    P.op("act", lambda e, o=svbuf[:, tt, :], a=pp[:, :]: e.copy(out=o, in_=a), reads=[rp], writes=[rsv])
                slot_u, rwu = load_w(W, 0, cgu * 512)
                for tt in range(cur["nt"]):
                    pu, rpu = pjR.next()
                    for kc in range(KC):
                        P.op("pe", lambda e, o=pu[:, :], a=hnT[:, kc, tt * 128:(tt + 1) * 128], b=slot_u[:, kc, :], s=(kc == 0),
                             t=(kc == KC - 1): e.matmul(o, lhsT=a, rhs=b, start=s, stop=t), reads=[rwu, res("hnT")], writes=[rpu], acc=(kc > 0))
                    gu, rgu = t16R.next()
                    P.op("act", lambda e, o=gu[:], a=pu[:, :]: e.activation(out=o, in_=a, func=AF.Gelu_apprx_tanh), reads=[rpu], writes=[rgu])
                    P.op("dve", lambda e, o=svbuf[:, tt, :], b=gu[:]: e.tensor_tensor(out=o, in0=o, in1=b, op=ALU.mult),
                         reads=[rsv, rgu], writes=[rsv])
                slot_z, rwz = load_w(W, 0, (16 + cgu) * 512)

                def z_epi(tt, pz, rpz, cgu=cgu):
                    gz, rgz = t16R.next()
                    P.op("act", lambda e, o=gz[:], a=pz[:, :]: e.activation(out=o, in_=a, func=AF.Silu), reads=[rpz], writes=[rgz])
                    P.op("dve", lambda e, o=gz[:], b=svbuf[:, tt, :]: e.tensor_tensor(out=o, in0=o, in1=b, op=ALU.mult),
                         reads=[rsv, rgz], writes=[rgz])
                    pt, rpt = ptR.next()
                    for q in range(4):
                        P.op("pe", lambda e, o=pt[:, q * 128:(q + 1) * 128], s_=gz[:, q * 128:(q + 1) * 128]: e.transpose(
                            out=o, in_=s_, identity=ident[:]), reads=[rgz, rS], writes=[rpt])
                    P.op("dve", lambda e, o=chunk(4 * cgu, tt * 128, [[T, 4], [1, 128]]),
                         s_=vap(pt[:, :], 0, [[128, 4], [1, 128]]): e.tensor_copy(out=o, in_=s_),
                         reads=[rpt], writes=[rvh[4 * cgu + q] for q in range(4)])

                pend = []
                for tt in range(cur["nt"]):
                    pz, rpz = pjR.next()
                    for kc in range(KC):
                        P.op("pe", lambda e, o=pz[:, :], a=hnT[:, kc, tt * 128:(tt + 1) * 128], b=slot_z[:, kc, :], s=(kc == 0),
                             t=(kc == KC - 1): e.matmul(o, lhsT=a, rhs=b, start=s, stop=t), reads=[rwz, res("hnT")], writes=[rpz], acc=(kc > 0))
                    pend.append((tt, pz, rpz))
                    if len(pend) > EPI_DELAY:
                        z_epi(*pend.pop(0))
                for p_ in pend:
                    z_epi(*p_)

            class L:
                def __getitem__(self, key):
                    _, kk, sl = key
                    return chunk(kk, sl.start, [[1, sl.stop - sl.start]])
            rall = res("aTall")
            P.op("dve", lambda e: e.memset(small[:, 63:64], 0.0), reads=rvh, writes=[rall, res("small4")])
            out_proj_residual(l, g_w_out[0], 32, L(), rall)

        ctxb_eps = sb("epsb", [128, 1], F32)
        P.op("dve", lambda e: e.memset(ctxb_eps[:], EPS), writes=[rS])

        kinds = [0, 1, 2, 0]
        first = True
        for pi in range(len(passes)):
            cur.clear()
            cur.update(PT[pi])
            STAGE = int(os.environ.get("KSTAGE", "99"))
            if STAGE < 1:
                break
            pass_setup()
            for l in range(nlayers):
                if kinds[l] == 2 or (l > 0 and kinds[l - 1] == 2):
                    for e_ in ENGS:
                        if e_ != "pool":
                            P.wait_all(e_)
                    P.op("dve", lambda e: e.memset(small[:, 62:63], 0.0), writes=[res("bar")])
                first = False
                if STAGE < 2:
                    break
                adaln(l)
                if STAGE < 3:
                    break
                norm_phase(l)
                if STAGE < 4:
                    break
                k = kinds[l]
                if k == 0:
                    win_layer(l, l // 3)
                elif k == 1:
                    nat_layer(l)
                else:
                    gmlp_layer(l)
        for e in ENGS:
            P.wait_all(e)

        esem = {e: st.enter_context(nc.semaphore("s_" + e)) for e in ENGS}
        dsem = {k: st.enter_context(nc.semaphore("d_" + k)) for k in P.dcount}
        block = st.enter_context(nc.Block())
        P.emit(block, esem, dsem)
    return nc


NCORES = 4
PASSES = ((8, False), (4, True)) if os.environ.get('KPASSES', '2') == '2' else ((8, False),)
ASSIGN = {0: (("s", 0), [12, 13]), 1: (("s", 1), [14, 15]),
          2: (("p", [0, 1, 2, 3]), [4, 5]), 3: (("p", [6, 7, 8, 9]), [10, 11])}


def _win_mask(sample):
    m = np.full((8, 3, 128, 128), NEG, np.float32)
    a = np.arange(128)[:, None]
    b = np.arange(128)[None, :]
    for i in range(8):
        m[i, 1] = 0.0
        if sample:
            m[i, 0] = np.where(b <= a, 0.0, NEG)
            m[i, 2] = np.where(a <= b, 0.0, NEG)
        else:
            if i % 2 == 1:
                m[i, 0] = 0.0
            else:
                m[i, 2] = 0.0
    return m


def _nat_mask(sample, rel_bias):
    m = np.full((16, NAT_NB, 128, 128), NEG, np.float32)
    a = np.arange(128)
    for i in range(8):
        for jx, j in enumerate(NAT_J[i]):
            blk = NAT_OFF[i] + jx
            if not sample:
                if j // 2 == i // 2:
                    m[:, blk] = 0.0
                continue
            krow = (2 * j + a // 64)[:, None]
            kcol = (a % 64)[:, None]
            qrow = (2 * i + a // 64)[None, :]
            qcol = (a % 64)[None, :]
            rs = np.clip(qrow - 4, 0, 8)
            cs = np.clip(qcol - 8, 0, 48)
            valid = (krow >= rs) & (krow < rs + 8) & (kcol >= cs) & (kcol < cs + 16)
            dr = np.clip(krow - qrow + 7, 0, 14)
            dc = np.clip(kcol - qcol + 15, 0, 30)
            g = rel_bias[:, dr, dc]
            m[:, blk] = np.where(valid[None], g, np.float32(NEG))
    return m


def _rope_tables(sample):
    if not sample:
        return np.ones((T, 64), np.float32), np.zeros((T, 64), np.float32)
    t = np.arange(T)
    row = (t // 64).astype(np.float32)
    col = (t % 64).astype(np.float32)
    inv = (np.float32(10000.0) ** (-np.arange(32, dtype=np.float32) / np.float32(32))).astype(np.float32)
    ang = np.concatenate([row[:, None] * inv[None], col[:, None] * inv[None]], axis=1).astype(np.float32)
    return np.cos(ang).astype(np.float32), np.sin(ang).astype(np.float32)


_NC_CACHE = {}


def kernel(x_prompt, x_sample, cache_win_k, cache_win_v, cache_nat_k, cache_nat_v, c, c_ctx,
           norm_g, w_ada, b_ada, win_w_in, win_q_norm, win_k_norm, win_sink, win_w_out,
           nat_w_in, nat_q_norm, nat_k_norm, nat_rel_bias, nat_w_out,
           gmlp_w_in, gmlp_ln_g, gmlp_ln_b, gmlp_w_s, gmlp_b_s, gmlp_w_out, _nlayers=4):
    f = lambda a: np.ascontiguousarray(np.asarray(a, dtype=np.float32))
    x_prompt, x_sample = f(x_prompt), f(x_sample)
    w_ada_ = np.asarray(w_ada)
    win_w_in_ = np.asarray(win_w_in)
    win_w_out_ = np.asarray(win_w_out)
    shared = {
        "ident": np.eye(128, dtype=np.float32),
        "onesrow": np.ones((1, 4096), np.float32),
        "normg": f(np.asarray(norm_g).reshape(4, 16, 128).transpose(0, 2, 1)),
        "b_ada": f(np.asarray(b_ada).reshape(4, 1, 6144)),
        "win_qn": f(np.asarray(win_q_norm).reshape(2, 1, 128)),
        "win_kn": f(np.asarray(win_k_norm).reshape(2, 1, 128)), "win_sink": f(np.asarray(win_sink).reshape(2, 1, 16)),
        "nat_w_in": f(np.asarray(nat_w_in)[0]), "nat_qn": f(np.asarray(nat_q_norm).reshape(1, 1, 128)),
        "nat_kn": f(np.asarray(nat_k_norm).reshape(1, 1, 128)), "nat_w_out": f(np.asarray(nat_w_out)[0]),
        "g_w_in": f(np.asarray(gmlp_w_in)[0]), "g_ln_g": f(np.asarray(gmlp_ln_g).reshape(1, 1, 4096)),
        "g_ln_b": f(np.asarray(gmlp_ln_b).reshape(1, 1, 4096)), "g_w_s": f(gmlp_w_s),
        "g_b_s": f(np.asarray(gmlp_b_s).reshape(1, 1, 2048)), "g_w_out": f(np.asarray(gmlp_w_out)[0]),
    }
    for l_ in range(4):
        shared["w_ada%d" % l_] = f(w_ada_[l_])
    for i_ in range(2):
        shared["win_w_in%d" % i_] = f(win_w_in_[i_])
        shared["win_w_out%d" % i_] = f(win_w_out_[i_])
    rel_bias = f(nat_rel_bias)[0]
    wm = {False: _win_mask(False), True: _win_mask(True)}
    nm = {False: _nat_mask(False, rel_bias), True: _nat_mask(True, rel_bias)}
    rp = {False: _rope_tables(False), True: _rope_tables(True)}
    shared["zmask"] = np.zeros((2, 128, 128), np.float32)
    c = f(c)
    c_ctx = f(c_ctx)
    in_maps = []
    for core in range(NCORES):
        (kindA, whatA), seqsB = ASSIGN[core]
        sample = kindA == "s"
        m = dict(shared)
        if sample:
            b = whatA
            xa = x_sample[b]
            cv = c[b]
            m["cwk_0"] = f(np.asarray(cache_win_k)[b].reshape(2, 512, 512))
            m["cwv_0"] = f(np.asarray(cache_win_v)[b].reshape(2, 512, 512))
            m["cnk_0"] = f(np.asarray(cache_nat_k)[b, 0].reshape(512, 2048))
            m["cnv_0"] = f(np.asarray(cache_nat_v)[b, 0].reshape(512, 2048))
            m["ctxb_0"] = np.zeros((128, 1), np.float32)
        else:
            xa = np.concatenate([x_prompt[sq] for sq in whatA], axis=0)
            cv = c_ctx
            m["cwk_0"] = np.zeros((2, 512, 512), np.float32)
            m["cwv_0"] = np.zeros((2, 512, 512), np.float32)
            m["cnk_0"] = np.zeros((512, 2048), np.float32)
            m["cnv_0"] = np.zeros((512, 2048), np.float32)
            m["ctxb_0"] = np.full((128, 1), NEG, np.float32)
        m["xin_0"] = f(xa)
        m["cond_0"] = f(cv.reshape(16, 128).T)
        m["wmask_0"] = wm[sample]
        m["nmask_0"] = nm[sample]
        m["ropec_0"], m["ropes_0"] = rp[sample]
        m["xin_1"] = f(np.concatenate([x_prompt[sq] for sq in seqsB], axis=0))
        m["cond_1"] = f(c_ctx.reshape(16, 128).T)
        in_maps.append(m)
    key = (_nlayers,)
    if key not in _NC_CACHE:
        _NC_CACHE[key] = build(_nlayers, PASSES)
    nc = _NC_CACHE[key]
    res = run_bass_kernel_spmd(nc, in_maps, core_ids=list(range(NCORES)))
    outs = res.results
    y_prompt = np.zeros((16, 256, D), np.float32)
    y_sample = np.zeros((2, 1024, D), np.float32)
    nwk = np.zeros((16, 2, 256, 4, 128), np.float32)
    nwv = np.zeros((16, 2, 256, 4, 128), np.float32)
    nnk = np.zeros((16, 1, 256, 16, 128), np.float32)
    nnv = np.zeros((16, 1, 256, 16, 128), np.float32)

    def take(o, sfx, seqs):
        for s_, sq in enumerate(seqs):
            sl = slice(s_ * 256, (s_ + 1) * 256)
            y_prompt[sq] = o["y" + sfx][sl]
            for li in range(2):
                nwk[sq, li] = o["owk" + sfx][li, sl].reshape(256, 4, 128)
                nwv[sq, li] = o["owv" + sfx][li, sl].reshape(256, 4, 128)
            nnk[sq, 0] = o["onk" + sfx][sl].reshape(256, 16, 128)
            nnv[sq, 0] = o["onv" + sfx][sl].reshape(256, 16, 128)

    for core in range(NCORES):
        (kindA, whatA), seqsB = ASSIGN[core]
        o = outs[core]
        if kindA == "s":
            y_sample[whatA] = o["y_0"]
        else:
            take(o, "_0", whatA)
        if len(PASSES) > 1:
            take(o, "_1", seqsB)
    return (y_prompt, y_sample, nwk, nwv, nnk, nnv)
```

```python
import os
import numpy as np
import concourse.bass as bass
import concourse.mybir as mybir
from concourse.bass_utils import run_bass_kernel_spmd

F32 = mybir.dt.float32
BF16 = mybir.dt.bfloat16
AF = mybir.ActivationFunctionType
ALU = mybir.AluOpType
AX = mybir.AxisListType

D = 2048
T = 1024
NT = 8
KC = 16
NEG = -30000.0
EPS = 1e-6
SCALE = 128.0 ** -0.5
ENGS = ["pe", "act", "dve", "pool", "sp"]
EPI_DELAY = 2

NAT_J = {0: [0, 1, 2, 3], 1: [0, 1, 2, 3], 2: [0, 1, 2, 3, 4], 3: [1, 2, 3, 4, 5],
         4: [2, 3, 4, 5, 6], 5: [3, 4, 5, 6, 7], 6: [4, 5, 6, 7], 7: [4, 5, 6, 7]}
NAT_OFF = {}
_o = 0
for _i in range(8):
    NAT_OFF[_i] = _o
    _o += len(NAT_J[_i])
NAT_NB = _o


class Res:
    __slots__ = ("name", "w", "r")

    def __init__(self, name):
        self.name = name
        self.w = None
        self.r = {}


class Prog:
    def __init__(self, nc):
        self.nc = nc
        self.ops = {e: [] for e in ENGS}
        self.signal = {e: set() for e in ENGS}
        self.seen = {e: {} for e in ENGS}
        self.dcount = {}

    def _deps(self, eng, reads, writes, acc):
        evs = []
        for r in reads:
            if r.w is not None:
                evs.append(r.w)
        for w in writes:
            if w.w is not None:
                if not (acc and w.w[0] == "E" and w.w[1] == "pe" and eng == "pe"):
                    evs.append(w.w)
            evs.extend(w.r.values())
        out = []
        for ev in evs:
            kind, key, val = ev
            if kind == "D":
                val = self.dcount[key]
            if self.seen[eng].get((kind, key), -1) >= val:
                continue
            self.seen[eng][(kind, key)] = val
            out.append((kind, key, val))
            if kind == "E":
                self.signal[key].add(val)
        return out

    def op(self, eng, fn, reads=(), writes=(), acc=False):
        waits = self._deps(eng, reads, writes, acc)
        idx = len(self.ops[eng])
        ev = ("E", eng, idx)
        self.ops[eng].append((waits, fn, None))
        for r in reads:
            r.r[eng] = ev
        for w in writes:
            w.w = ev
            w.r = {}
        return ev

    def dma(self, eng, key, out, in_, reads=(), writes=(), **kw):
        key = key + "_" + eng
        waits = self._deps(eng, reads, writes, False)
        self.dcount[key] = self.dcount.get(key, 0) + 16
        ev = ("D", key, self.dcount[key])
        self.ops[eng].append((waits, (lambda e, o=out, i=in_, k=kw: e.dma_start(out=o, in_=i, **k)), key))
        for r in reads:
            r.r["D" + key] = ev
        for w in writes:
            w.w = ev
            w.r = {}
        return ev

    def wait_all(self, eng):
        waits = []
        for e in ENGS:
            n = len(self.ops[e])
            if n == 0:
                continue
            for idx in range(n - 1, -1, -1):
                if self.ops[e][idx][2] is None and self.ops[e][idx][1] is not None:
                    if self.seen[eng].get(("E", e), -1) < idx:
                        waits.append(("E", e, idx))
                        self.signal[e].add(idx)
                        self.seen[eng][("E", e)] = idx
                    break
        for key, cnt in self.dcount.items():
            if self.seen[eng].get(("D", key), -1) < cnt:
                waits.append(("D", key, cnt))
                self.seen[eng][("D", key)] = cnt
        self.ops[eng].append((waits, None, None))

    def emit(self, block, esem, dsem):
        tick = {}
        for e in ENGS:
            tick[e] = {}
            c = 0
            for idx in sorted(self.signal[e]):
                c += 1
                tick[e][idx] = c

        def run(e, h):
            for idx, (waits, fn, dkey) in enumerate(self.ops[e]):
                for kind, key, val in waits:
                    if kind == "E":
                        h.wait_ge(esem[key], tick[key][val])
                    else:
                        h.wait_ge(dsem[key], val)
                if fn is None:
                    continue
                ins = fn(h)
                if dkey is not None:
                    ins.then_inc(dsem[dkey], 16)
                elif idx in self.signal[e]:
                    ins.then_inc(esem[e], 1)

        @block.tensor
        def _(h):
            run("pe", h)

        @block.scalar
        def _(h):
            run("act", h)

        @block.vector
        def _(h):
            run("dve", h)

        @block.gpsimd
        def _(h):
            run("pool", h)

        @block.sync
        def _(h):
            run("sp", h)


class Ring:
    def __init__(self, items):
        self.items = items
        self.i = 0

    def next(self):
        it = self.items[self.i % len(self.items)]
        self.i += 1
        return it


def vap(ap, off, pairs, parts=None):
    p = ap.ap[0]
    n = p[1] if parts is None else parts
    return bass.AP(ap.tensor, ap.offset + off, [[p[0], n]] + [list(x) for x in pairs])


def build(nlayers=4, passes=((8, False), (4, True))):
    nc = bass.Bass("TRN2", target_bir_lowering=False)
    P = Prog(nc)

    def din(name, shape):
        return nc.dram_tensor(name, list(shape), F32, kind="ExternalInput").ap()

    def dout(name, shape):
        return nc.dram_tensor(name, list(shape), F32, kind="ExternalOutput").ap()

    PT = []
    for pi, (ntp, static_prompt) in enumerate(passes):
        d_ = {"nt": ntp, "static": static_prompt, "pi": pi}
        sfx = "_%d" % pi
        d_["xin"] = din("xin" + sfx, [ntp * 128, D])
        d_["cond"] = din("cond" + sfx, [128, 16])
        d_["y"] = dout("y" + sfx, [ntp * 128, D])
        d_["owk"] = dout("owk" + sfx, [2, ntp * 128, 512])
        d_["owv"] = dout("owv" + sfx, [2, ntp * 128, 512])
        d_["onk"] = dout("onk" + sfx, [ntp * 128, D])
        d_["onv"] = dout("onv" + sfx, [ntp * 128, D])
        if not static_prompt:
            d_["wmask"] = din("wmask" + sfx, [8, 3, 128, 128])
            d_["ctxb_d"] = din("ctxb" + sfx, [128, 1])
            d_["ropec"] = din("ropec" + sfx, [ntp * 128, 64])
            d_["ropes"] = din("ropes" + sfx, [ntp * 128, 64])
            d_["cwk"] = din("cwk" + sfx, [2, 512, 512])
            d_["cwv"] = din("cwv" + sfx, [2, 512, 512])
            if nlayers > 1:
                d_["nmask"] = din("nmask" + sfx, [16, NAT_NB, 128, 128])
                d_["cnk"] = din("cnk" + sfx, [512, 2048])
                d_["cnv"] = din("cnv" + sfx, [512, 2048])
        PT.append(d_)
    zmask = din("zmask", [2, 128, 128])
    gsc = nc.dram_tensor("gsc", [4, 2048], F32, kind="Internal").ap()
    cur = {}
    ident_d = din("ident", [128, 128])
    ones_d = din("onesrow", [1, 4096])
    normg = din("normg", [4, 128, 16])
    kinds_ = [0, 1, 2, 0][:nlayers]
    w_ada = [din("w_ada%d" % l, [D, 3 * D]) for l in range(nlayers)]
    b_ada = din("b_ada", [4, 1, 3 * D])
    nwin = len([k for k in kinds_ if k == 0])
    win_w_in = [din("win_w_in%d" % i, [D, 5120]) for i in range(nwin)]
    win_qn = din("win_qn", [2, 1, 128])
    win_kn = din("win_kn", [2, 1, 128])
    win_sink = din("win_sink", [2, 1, 16])
    win_w_out = [din("win_w_out%d" % i, [D, D]) for i in range(nwin)]
    if 1 in kinds_:
        nat_w_in = [din("nat_w_in", [D, 8192])]
        nat_qn = din("nat_qn", [1, 1, 128])
        nat_kn = din("nat_kn", [1, 1, 128])
        nat_w_out = [din("nat_w_out", [D, D])]
    if 2 in kinds_:
        g_w_in = [din("g_w_in", [D, 12288])]
        g_ln_g = din("g_ln_g", [1, 1, 4096])
        g_ln_b = din("g_ln_b", [1, 1, 4096])
        g_w_s = din("g_w_s", [1, 16, 128, 128])
        g_b_s = din("g_b_s", [1, 1, 2048])
        g_w_out = [din("g_w_out", [4096, D])]

    es = {}
    from contextlib import ExitStack
    st = ExitStack()
    with st:
        def sb(name, shape, dt):
            return st.enter_context(nc.sbuf_tensor(name, list(shape), dt))

        def pst(name, shape, dt):
            return st.enter_context(nc.psum_tensor(name, list(shape), dt))

        ident = sb("ident_sb", [128, 128], BF16)
        onesb = sb("onesb", [128, 2], BF16)
        onesf = sb("onesf", [33, 128], F32)
        scb2 = sb("scb2", [128, 16 * 33], BF16)
        condT2 = sb("condT2", [128, 16], F32)
        modA_st = sb("modA_st", [128, 64], F32)
        modB_st = sb("modB_st", [128, 64], F32)
        hnT = sb("hnT", [128, KC, T], BF16)
        big = sb("big", [128, 16 * T], BF16)
        attn_big = sb("attn_big", [128, 16640], BF16)
        wring = [sb("w%d" % i, [128, KC, 512], BF16) for i in range(2)]
        xt = [sb("xt%d" % i, [128, 2048], F32) for i in range(1)]
        xsb = [sb("xsb%d" % i, [128, 2048], BF16) for i in range(1)]
        gate_bc = sb("gate_bc", [128, 2048], F32)
        mrow = sb("mrow", [33, 512], F32)
        brow = sb("brow", [33, 512], F32)
        condT = sb("condT", [128, 16], F32)
        ng = sb("ng", [128, 16], F32)
        modA = sb("modA", [128, 16], F32)
        modB = sb("modB", [128, 16], F32)
        small = sb("small", [128, 64], F32)
        smalls = [sb("small_r%d" % i, [128, 64], F32) for i in range(3)]
        ctxb = sb("ctxb_sb", [128, 1], F32)
        qn_bc = sb("qn_bc", [128, 128], F32)
        kn_bc = sb("kn_bc", [128, 128], F32)
        esink = sb("esink", [128, 16], F32)
        cosb = sb("cosb", [128, NT, 64], F32)
        sinb = sb("sinb", [128, NT, 64], F32)
        qT = vap(attn_big[:], 0, [[T, 4], [1, T]])
        kT = vap(attn_big[:], 4096, [[T, 4], [1, T]])
        kTc = vap(attn_big[:], 8192, [[512, 4], [1, 512]])
        cstage = vap(attn_big[:], 10240, [[512, 4], [1, 512]])
        vaug = vap(attn_big[:], 12288, [[520, NT], [130, 4], [1, 130]])
        vaugc = sb("vaugc", [128, 4, 4, 130], BF16)
        t32 = [sb("t32_%d" % i, [128, 512], F32) for i in range(4)]
        t16 = [sb("t16_%d" % i, [128, 512], BF16) for i in range(4)]
        ebuf = sb("ebuf", [128, 4096], BF16)
        Eb = [vap(ebuf[:], i * 512, [[1, 512]]) for i in range(8)]
        svbuf = vap(ebuf[:], 0, [[512, NT], [1, 512]])
        mk = [sb("mk%d" % i, [128, 5, 128], F32) for i in range(2)]
        wm4 = sb("wm4", [128, 4, 128], BF16)
        xp = [sb("xp%d" % i, [128, 512], F32) for i in range(2)]
        wsn = sb("wsn", [128, 16, 128], BF16)
        wsT = sb("wsT", [128, 16, 128], BF16)
        rb2 = sb("rb2", [2, 256], F32)
        lb2 = sb("lb2", [2, 512], F32)
        lng = sb("lng", [128, 512], BF16)
        gstat = sb("gstat", [128, 2, NT, 8], F32)

        pj = [pst("pj%d" % i, [128, 512], F32) for i in range(2)]
        ps_ = [pst("ps%d" % i, [128, 512], F32) for i in range(2)]
        po = [pst("po%d" % i, [128, 512], F32) for i in range(2)]
        ptr = [pst("pt%d" % i, [128, 1024], BF16) for i in range(2)]

        R = {}

        def res(name):
            if name not in R:
                R[name] = Res(name)
            return R[name]

        pjR = Ring([(pj[i], res("pj%d" % i)) for i in range(2)] + [(ps_[i], res("ps%d" % i)) for i in range(2)])
        psR = pjR
        poR = Ring([(po[i], res("po%d" % i)) for i in range(2)])
        ptR = Ring([(ptr[i], res("pt%d" % i)) for i in range(2)])
        wR = Ring([(wring[i], res("w%d" % i), "w%d" % i) for i in range(2)])
        xtR = Ring([(xt[i], res("xt%d" % i), "xt%d" % i) for i in range(1)])
        xsR = Ring([(xsb[i], res("xsb%d" % i)) for i in range(1)])
        t32R = Ring([(t32[i], res("t32_%d" % i)) for i in range(4)])
        t16R = Ring([(t16[i], res("t16_%d" % i)) for i in range(4)])
        smR = Ring([(smalls[i], res("small_r%d" % i)) for i in range(3)])
        mkR = Ring([(mk[i], res("mk%d" % i), "mk%d" % i) for i in range(2)])
        xpR = Ring([(xp[i], res("xp%d" % i), "xp%d" % i) for i in range(2)])
        ER = [(Eb[i], res("E%d" % i)) for i in range(8)]
        evac_flip = [0]

        rS = res("setup")
        P.dma("pool", "setup", ident[:], ident_d[:, :], writes=[rS])
        P.dma("sp", "setup", onesf[0:1, :], ones_d[0:1, 0:128], writes=[rS])
        P.dma("sp", "setup", onesf[32:33, :], ones_d[0:1, 0:128], writes=[rS])
        P.op("dve", lambda e: e.memset(scb2[:], 0.0), writes=[res("scb")])
        P.op("dve", lambda e: e.memset(onesb[:], 1.0), writes=[res("onesb")])
        P.op("dve", lambda e: e.memset(vaugc[:], 1.0), writes=[res("vaugc")])
        rsc = res("scb")
        rPS = res("pass_setup")

        def pass_setup():
            P.dma("sp", "psetup", condT[:], cur["cond"][:, :], writes=[rPS])
            if not cur["static"]:
                ntp = cur["nt"]
                P.dma("sp", "psetup", ctxb[:], cur["ctxb_d"][:, :], writes=[rPS])
                P.dma("sp", "psetup", cosb[:, 0:ntp, :], cur["ropec"].rearrange("(t p) f -> p t f", p=128), writes=[rPS])
                P.dma("sp", "psetup", sinb[:, 0:ntp, :], cur["ropes"].rearrange("(t p) f -> p t f", p=128), writes=[rPS])
                for slot_, (i_, jj_) in enumerate([(2, 0), (1, 0), (0, 2), (1, 2)]):
                    P.dma("pool", "wm4", wm4[:, slot_, :], cur["wmask"][i_, jj_, :, :], writes=[res("wm4")])
            if cur["pi"] == 0:
                P.op("act", lambda e: e.activation(out=vap(scb2[:], 0, [[33, 16]]), in_=condT[:], func=AF.Silu),
                     reads=[rPS], writes=[rsc])
                if len(passes) > 1:
                    P.dma("sp", "psetup", condT2[:], PT[1]["cond"][:, :], writes=[rPS])
                    P.op("act", lambda e: e.activation(out=vap(scb2[:], 32, [[33, 16]]), in_=condT2[:], func=AF.Silu),
                         reads=[rPS], writes=[rsc])

        def load_w(src2d, r0, c0):
            slot, r, key = wR.next()
            rs = [res(r.name + "_lo"), res(r.name + "_hi")]
            for hf in range(2):
                P.dma("pool", key + ("l" if hf == 0 else "h"), slot[:, hf * 8:(hf + 1) * 8, :],
                      src2d[r0 + hf * 1024:r0 + (hf + 1) * 1024, c0:c0 + 512].rearrange("(kc p) n -> p kc n", p=128),
                      writes=[rs[hf]])
            return slot, rs

        def transpose_to(dst_fn, src_ap_fn, n, rsrc, rdst, evac=None, dstm=None):
            i = 0
            while i < n:
                m = min(4, n - i)
                pt, rpt = ptR.next()
                for jx in range(m):
                    P.op("pe", lambda e, o=pt[:, jx * 128:(jx + 1) * 128], s=src_ap_fn(i + jx): e.transpose(
                        out=o, in_=s, identity=ident[:]), reads=[rsrc, rS], writes=[rpt])
                if evac is not None:
                    evac(i, m, pt, rpt)
                elif dstm is not None:
                    dst = dstm(i, m)
                    src = vap(pt[:, :], 0, [[128, m], [1, 128]])
                    evac_flip[0] ^= 1
                    if evac_flip[0]:
                        P.op("dve", lambda e, o=dst, s=src: e.tensor_copy(out=o, in_=s), reads=[rpt], writes=[rdst])
                    else:
                        P.op("act", lambda e, o=dst, s=src: e.copy(out=o, in_=s), reads=[rpt], writes=[rdst])
                else:
                    for jx in range(m):
                        dst = dst_fn(i + jx)
                        src = pt[:, jx * 128:(jx + 1) * 128]
                        evac_flip[0] ^= 1
                        if evac_flip[0]:
                            P.op("dve", lambda e, o=dst, s=src: e.tensor_copy(out=o, in_=s), reads=[rpt], writes=[rdst])
                        else:
                            P.op("act", lambda e, o=dst, s=src: e.copy(out=o, in_=s), reads=[rpt], writes=[rdst])
                i += m

        def adaln(l):
            rm = res("mrow")
            rb = res("brow")
            rg = res("gate")
            rmod = res("mod")
            rst = res("modst")
            two = len(passes) > 1
            if cur["pi"] == 1:
                P.op("dve", lambda e: e.tensor_copy(out=modA[:], in_=modA_st[:, l * 16:(l + 1) * 16]), reads=[rst], writes=[rmod])
                P.op("dve", lambda e: e.tensor_copy(out=modB[:], in_=modB_st[:, l * 16:(l + 1) * 16]), reads=[rst], writes=[rmod])
                P.dma("sp", "gatebc", gate_bc[:], gsc[l:l + 1, :].partition_broadcast(128), reads=[res("gsc")], writes=[rg])
                return
            P.dma("sp", "ng", ng[:], normg[l, :, :], writes=[res("ng")])
            pc, rpc = psR.next()
            pcB, rpcB = psR.next()
            M = 33 if two else 1
            for cg in range(12):
                slot, rw = load_w(w_ada[l], 0, cg * 512)
                P.dma("sp", "brow", brow[0:1, :], b_ada[l, :, cg * 512:(cg + 1) * 512], writes=[rb])
                if two:
                    P.dma("sp", "brow", brow[32:33, :], b_ada[l, :, cg * 512:(cg + 1) * 512], writes=[rb])
                pp, rp = poR.next()
                for kc in range(KC):
                    P.op("pe", lambda e, o=pp[0:M, :], a=vap(scb2[:], kc * 33, [[1, M]]), b=slot[:, kc, :], s=(kc == 0), t=(kc == KC - 1):
                         e.matmul(o, lhsT=a, rhs=b, start=s, stop=t), reads=[rw[kc // 8], rsc], writes=[rp], acc=(kc > 0))
                P.op("dve", lambda e, a=pp[0:1, :]: e.tensor_tensor(out=mrow[0:1, :], in0=a, in1=brow[0:1, :], op=ALU.add),
                     reads=[rp, rb], writes=[rm])
                if two:
                    P.op("dve", lambda e, a=pp[32:33, :]: e.tensor_tensor(out=mrow[32:33, :], in0=a, in1=brow[32:33, :], op=ALU.add),
                         reads=[rp, rb], writes=[rm])
                if cg < 8:
                    for j in range(4):
                        col = cg * 4 + j
                        P.op("pe", lambda e, o=pc[:, 2 * col:2 * col + 2], a=mrow[0:1, j * 128:(j + 1) * 128], b=onesf[0:1, 0:2]:
                             e.matmul(o, lhsT=a, rhs=b, start=True, stop=True), reads=[rm, rS], writes=[rpc])
                        if two:
                            P.op("pe", lambda e, o=pcB[:, 2 * col:2 * col + 2], a=mrow[32:33, j * 128:(j + 1) * 128], b=onesf[32:33, 0:2]:
                                 e.matmul(o, lhsT=a, rhs=b, start=True, stop=True), reads=[rm, rS], writes=[rpcB])
                else:
                    dg = cg - 8
                    pg, rpg = poR.next()
                    P.op("pe", lambda e, o=pg[:, :], a=onesf[0:1, :], b=mrow[0:1, :]:
                         e.matmul(o, lhsT=a, rhs=b, start=True, stop=True), reads=[rm, rS], writes=[rpg])
                    P.op("act", lambda e, o=gate_bc[:, dg * 512:(dg + 1) * 512], a=pg[:, :]: e.copy(out=o, in_=a),
                         reads=[rpg], writes=[rg])
                    if two:
                        P.dma("sp", "gsc", gsc[l:l + 1, dg * 512:(dg + 1) * 512], mrow[32:33, :], reads=[rm], writes=[res("gsc")])
            P.op("dve", lambda e: e.tensor_copy(out=modB[:], in_=vap(pc[:, :], 0, [[2, 16]])), reads=[rpc], writes=[rmod])
            P.op("dve", lambda e: e.scalar_tensor_tensor(out=modA[:], in0=vap(pc[:, :], 32, [[2, 16]]), scalar=1.0, in1=ng[:],
                                                         op0=ALU.add, op1=ALU.mult), reads=[rpc, res("ng")], writes=[rmod])
            if two:
                P.op("dve", lambda e: e.tensor_copy(out=modB_st[:, l * 16:(l + 1) * 16], in_=vap(pcB[:, :], 0, [[2, 16]])),
                     reads=[rpcB], writes=[rst])
                P.op("dve", lambda e: e.scalar_tensor_tensor(out=modA_st[:, l * 16:(l + 1) * 16], in0=vap(pcB[:, :], 32, [[2, 16]]),
                                                             scalar=1.0, in1=ng[:], op0=ALU.add, op1=ALU.mult),
                     reads=[rpcB, res("ng")], writes=[rst])

        def norm_phase(l):
            src = cur["xin"] if l == 0 else cur["y"]
            rh = res("hnT")
            rmod = res("mod")
            for tt in range(cur["nt"]):
                xtile, rx, key = xtR.next()
                P.dma("sp", key, xtile[:], src[tt * 128:(tt + 1) * 128, :], reads=[res("y%d" % tt)], writes=[rx])
                xs, rxs = xsR.next()
                sm, rsm = smR.next()
                P.op("act", lambda e, o=xs[:], a=xtile[:], sm=sm: e.activation(out=o, in_=a, func=AF.Square,
                                                                                accum_out=sm[:, 0:1]),
                     reads=[rx], writes=[rxs, rsm])
                P.op("act", lambda e, sm=sm: e.activation(out=sm[:, 1:2], in_=sm[:, 0:1], func=AF.Sqrt, scale=1.0 / D,
                                                          bias=ctxb_eps[:]), reads=[rsm, rS], writes=[rsm])
                P.op("dve", lambda e, sm=sm: e.reciprocal(out=sm[:, 2:3], in_=sm[:, 1:2]), reads=[rsm], writes=[rsm])
                P.op("dve", lambda e, o=xs[:], a=xtile[:], sm=sm: e.tensor_scalar(out=o, in0=a, scalar1=sm[:, 2:3], scalar2=None,
                                                                                  op0=ALU.mult), reads=[rx, rsm, rxs], writes=[rxs])

                def evac(kc0, m, pt, rpt, tt=tt):
                    if (kc0 // 4) % 2 == 0:
                        dst = vap(hnT[:], kc0 * T + tt * 128, [[T, m], [1, 128]])
                        srcp = vap(pt[:, :], 0, [[128, m], [1, 128]])
                        P.op("dve", lambda e: e.tensor_tensor(out=dst, in0=srcp, in1=vap(modA[:], kc0, [[1, m], [0, 128]]), op=ALU.mult),
                             reads=[rpt, rmod], writes=[rh])
                        P.op("dve", lambda e: e.tensor_tensor(out=dst, in0=dst, in1=vap(modB[:], kc0, [[1, m], [0, 128]]), op=ALU.add),
                             reads=[rmod, rh], writes=[rh])
                    else:
                        for jx in range(m):
                            k = kc0 + jx
                            P.op("act", lambda e, o=hnT[:, k, tt * 128:(tt + 1) * 128], s_=pt[:, jx * 128:(jx + 1) * 128], k=k: e.activation(
                                out=o, in_=s_, func=AF.Identity, scale=modA[:, k:k + 1], bias=modB[:, k:k + 1]),
                                reads=[rpt, rmod], writes=[rh])

                transpose_to(lambda kc, tt=tt: hnT[:, kc, tt * 128:(tt + 1) * 128],
                             lambda kc, xs=xs: xs[:, kc * 128:(kc + 1) * 128], KC, rxs, rh, evac=evac)

        def project(wsrc, r0, cg, epilogue, lhs=None, rlhs=None, koff=0):
            lhs = hnT if lhs is None else lhs
            rlhs = res("hnT") if rlhs is None else rlhs
            slot, rw = load_w(wsrc, r0, cg * 512)
            pend = []
            for tt in range(cur["nt"]):
                pp, rp = pjR.next()
                for kc in range(KC):
                    P.op("pe", lambda e, o=pp[:, :], a=lhs[:, koff + kc, tt * 128:(tt + 1) * 128], b=slot[:, kc, :],
                         s=(kc == 0), t=(kc == KC - 1): e.matmul(o, lhsT=a, rhs=b, start=s, stop=t),
                         reads=[rw[kc // 8], rlhs], writes=[rp], acc=(kc > 0))
                pend.append((tt, pp, rp))
                if len(pend) > EPI_DELAY:
                    epilogue(*pend.pop(0))
            for p_ in pend:
                epilogue(*p_)

        def headnorm(pp, rp, gain_bc, rgain, rope):
            sq, rsq = t32R.next()
            sm, rsm = smR.next()
            v4 = lambda t: vap(t[:], 0, [[128, 4], [1, 128]])
            P.op("act", lambda e: e.activation(out=sq[:], in_=pp[:, :], func=AF.Square), reads=[rp], writes=[rsq])
            P.op("dve", lambda e: e.tensor_reduce(out=sm[:, 8:12], in_=v4(sq), axis=AX.X, op=ALU.add), reads=[rsq], writes=[rsm])
            P.op("act", lambda e: e.activation(out=sm[:, 12:16], in_=sm[:, 8:12], func=AF.Sqrt, scale=1.0 / 128,
                                               bias=ctxb_eps[:]), reads=[rsm, rS], writes=[rsm])
            P.op("dve", lambda e: e.reciprocal(out=sm[:, 16:20], in_=sm[:, 12:16]), reads=[rsm], writes=[rsm])
            P.op("dve", lambda e: e.tensor_tensor(out=v4(sq), in0=vap(pp[:, :], 0, [[128, 4], [1, 128]]),
                                                  in1=vap(sm[:], 16, [[1, 4], [0, 128]]), op=ALU.mult),
                 reads=[rp, rsm, rsq], writes=[rsq])
            P.op("dve", lambda e: e.tensor_tensor(out=v4(sq), in0=v4(sq), in1=vap(gain_bc[:], 0, [[0, 4], [1, 128]]), op=ALU.mult),
                 reads=[rsq, rgain], writes=[rsq])
            if rope is None:
                return sq, rsq
            tt = rope
            o2, ro2 = t32R.next()
            tmp, rtmp = t16R.next()
            cos3 = vap(cosb[:], tt * 64, [[0, 4], [32, 2], [1, 32]])
            sin3 = vap(sinb[:], tt * 64, [[0, 4], [32, 2], [1, 32]])

            def xv(t, p):
                return vap(t[:], p * 32, [[128, 4], [64, 2], [1, 32]])

            for p_ in range(2):
                P.op("dve", lambda e, p_=p_: e.tensor_tensor(out=xv(o2, p_), in0=xv(sq, p_), in1=cos3, op=ALU.mult),
                     reads=[rsq, rPS], writes=[ro2])
            P.op("dve", lambda e: e.tensor_tensor(out=xv(tmp, 0), in0=xv(sq, 1), in1=sin3, op=ALU.mult),
                 reads=[rsq, rPS], writes=[rtmp])
            P.op("dve", lambda e: e.tensor_tensor(out=xv(tmp, 1), in0=xv(sq, 0), in1=sin3, op=ALU.mult),
                 reads=[rsq, rPS, rtmp], writes=[rtmp])
            P.op("dve", lambda e: e.tensor_tensor(out=xv(o2, 0), in0=xv(o2, 0), in1=xv(tmp, 0), op=ALU.subtract),
                 reads=[rtmp, ro2], writes=[ro2])
            P.op("dve", lambda e: e.tensor_tensor(out=xv(o2, 1), in0=xv(o2, 1), in1=xv(tmp, 1), op=ALU.add),
                 reads=[rtmp, ro2], writes=[ro2])
            return o2, ro2

        def attention(i, nheads, khead_of, q_heads, local_blocks, mask_src, ctx, sink_cols, out_cols):
            G = len(q_heads)
            assert q_heads == list(range(q_heads[0], q_heads[0] + G))

            def etile(bi):
                if G == 4:
                    return ER[bi]
                t_, r_ = ER[bi // 4]
                return vap(t_, (bi % 4) * 128, [[1, 128]]), r_
            if mask_src is not None:
                nm = mask_src.shape[0]
                mt, rm_, mkey = mkR.next()
                P.dma("sp", mkey, mt[:, 0:nm, :], mask_src.rearrange("j k q -> k j q"), writes=[rm_])
            blocks = [("l", j, mi) for (j, mi) in local_blocks]
            if ctx:
                blocks += [("c", c, None) for c in range(4)]
            rq, rk, rv = res("qT"), res("kT"), res("vaug")
            rkc, rvc = res("kTc"), res("vaugc")
            for bi, (kind, j, mi) in enumerate(blocks):
                sp_, rsp = psR.next()
                Et, rE = etile(bi)
                rhs = vap(qT, q_heads[0] * T + i * 128, [[T, G], [1, 128]])
                if kind == "l":
                    lhsT = kT[:, khead_of, j * 128:(j + 1) * 128]
                    rr = [rq, rk]
                else:
                    lhsT = kTc[:, khead_of, j * 128:(j + 1) * 128]
                    rr = [rq, rkc]
                pemask = kind == "l" and isinstance(mi, tuple)
                P.op("pe", lambda e, o=sp_[:, 0:G * 128], a=lhsT, b=rhs, t=(not pemask): e.matmul(o, lhsT=a, rhs=b, start=True, stop=t),
                     reads=rr, writes=[rsp])
                if pemask:
                    P.op("pe", lambda e, o=sp_[:, 0:G * 128], b=vap(wm4[:], mi[1] * 128, [[0, G], [1, 128]]): e.matmul(
                        o, lhsT=ident[:], rhs=b, start=False, stop=True), reads=[res("wm4"), rS], writes=[rsp], acc=True)
                if kind == "l" and (mi is None or pemask):
                    P.op("act", lambda e, o=Et[:, 0:G * 128], a=sp_[:, 0:G * 128]: e.activation(
                        out=o, in_=a, func=AF.Exp, scale=SCALE), reads=[rsp], writes=[rE])
                elif kind == "l":
                    tb, rtb = t32R.next()
                    P.op("dve", lambda e, o=vap(tb[:], 0, [[128, G], [1, 128]]), a=vap(sp_[:, :], 0, [[128, G], [1, 128]]),
                         m=vap(mt[:], mi * 128, [[0, G], [1, 128]]): e.scalar_tensor_tensor(
                             out=o, in0=a, scalar=SCALE, in1=m, op0=ALU.mult, op1=ALU.add),
                         reads=[rsp, rm_], writes=[rtb])
                    P.op("act", lambda e, o=Et[:, 0:G * 128], a=tb[:, 0:G * 128]: e.activation(out=o, in_=a, func=AF.Exp),
                         reads=[rtb], writes=[rE])
                else:
                    P.op("act", lambda e, o=Et[:, 0:G * 128], a=sp_[:, 0:G * 128]: e.activation(
                        out=o, in_=a, func=AF.Exp, scale=SCALE, bias=ctxb[:]), reads=[rsp, rPS], writes=[rE])
            rsz = res("sz")
            for g in range(G):
                op_, rop = poR.next()
                nb = len(blocks)
                for bi, (kind, j, mi) in enumerate(blocks):
                    Et, rE = etile(bi)
                    if kind == "l":
                        rhs = vaug[:, j, khead_of, 0:129]
                        rr = [rE, rv]
                    else:
                        rhs = vaugc[:, j, khead_of, 0:129]
                        rr = [rE, rvc]
                    P.op("pe", lambda e, o=op_[:, 0:129], a=Et[:, g * 128:(g + 1) * 128], b=rhs, s=(bi == 0), t=(bi == nb - 1):
                         e.matmul(o, lhsT=a, rhs=b, start=s, stop=t), reads=rr, writes=[rop], acc=(bi > 0))
                sm, rsm = smR.next()
                if sink_cols is not None:
                    P.op("dve", lambda e, c=sink_cols[g], d=op_[:, 128:129], sm=sm: e.tensor_tensor(
                        out=sm[:, 32:33], in0=d, in1=esink[:, c:c + 1], op=ALU.add),
                         reads=[rop, res("esink")], writes=[rsm])
                    P.op("dve", lambda e, sm=sm: e.reciprocal(out=sm[:, 33:34], in_=sm[:, 32:33]), reads=[rsm], writes=[rsm])
                else:
                    P.op("dve", lambda e, d=op_[:, 128:129], sm=sm: e.reciprocal(out=sm[:, 33:34], in_=d), reads=[rop], writes=[rsm])
                oc = out_cols[g]
                szs = vap(big[:], i * 2048 + oc, [[1, 128]])
                P.op("dve", lambda e, o=szs, d=op_[:, 0:128], sm=sm: e.scalar_tensor_tensor(
                    out=o, in0=d, scalar=sm[:, 33:34], in1=o, op0=ALU.mult, op1=ALU.mult),
                     reads=[rop, rsm, rsz], writes=[rsz])

        def attention_nat(i, hg):
            static = cur["static"]
            if static:
                blocks = [("l", 2 * (i // 2), None), ("l", 2 * (i // 2) + 1, None)]
            else:
                J = NAT_J[i]
                blocks = [("l", j, jx) for jx, j in enumerate(J)] + [("c", c, None) for c in range(4)]
            rq, rk, rv = res("qT"), res("kT"), res("vaug")
            rkc, rvc = res("kTc"), res("vaugc")
            extra = None
            etl = []
            for bi in range(len(blocks)):
                if bi < 8:
                    etl.append(ER[bi])
                else:
                    if extra is None:
                        extra = t16R.next()
                    etl.append(extra)
            for bi, (kind, j, jx) in enumerate(blocks):
                sp_, rsp = psR.next()
                Et, rE = etl[bi]
                for h in range(4):
                    if kind == "l":
                        lhsT = kT[:, h, j * 128:(j + 1) * 128]
                        rr = [rq, rk]
                    else:
                        lhsT = kTc[:, h, j * 128:(j + 1) * 128]
                        rr = [rq, rkc]
                    P.op("pe", lambda e, o=sp_[:, h * 128:(h + 1) * 128], a=lhsT, b=qT[:, h, i * 128:(i + 1) * 128]:
                         e.matmul(o, lhsT=a, rhs=b, start=True, stop=True), reads=rr, writes=[rsp])
                if kind == "l" and static:
                    P.op("act", lambda e, o=Et[:, 0:512], a=sp_[:, :]: e.activation(
                        out=o, in_=a, func=AF.Exp, scale=SCALE), reads=[rsp], writes=[rE])
                elif kind == "l":
                    mt, rm_, mkey = mkR.next()
                    P.dma("sp", mkey, mt[:, 0:4, :],
                          cur["nmask"][hg * 4:(hg + 1) * 4, NAT_OFF[i] + jx, :, :].rearrange("h k q -> k h q"), writes=[rm_])
                    tb, rtb = t32R.next()
                    P.op("dve", lambda e, o=tb[:], a=sp_[:, :], m=vap(mt[:], 0, [[1, 512]]): e.scalar_tensor_tensor(
                        out=o, in0=a, scalar=SCALE, in1=m, op0=ALU.mult, op1=ALU.add), reads=[rsp, rm_], writes=[rtb])
                    P.op("act", lambda e, o=Et[:, 0:512], a=tb[:]: e.activation(out=o, in_=a, func=AF.Exp), reads=[rtb], writes=[rE])
                else:
                    P.op("act", lambda e, o=Et[:, 0:512], a=sp_[:, :]: e.activation(
                        out=o, in_=a, func=AF.Exp, scale=SCALE, bias=ctxb[:]), reads=[rsp, rPS], writes=[rE])
            rsz = res("sz")
            nb = len(blocks)
            for h in range(4):
                op_, rop = poR.next()
                for bi, (kind, j, jx) in enumerate(blocks):
                    Et, rE = etl[bi]
                    if kind == "l":
                        rhs = vaug[:, j, h, 0:129]
                        rr = [rE, rv]
                    else:
                        rhs = vaugc[:, j, h, 0:129]
                        rr = [rE, rvc]
                    P.op("pe", lambda e, o=op_[:, 0:129], a=Et[:, h * 128:(h + 1) * 128], b=rhs, s_=(bi == 0), t=(bi == nb - 1):
                         e.matmul(o, lhsT=a, rhs=b, start=s_, stop=t), reads=rr, writes=[rop], acc=(bi > 0))
                sm, rsm = smR.next()
                P.op("dve", lambda e, d=op_[:, 128:129], sm=sm: e.reciprocal(out=sm[:, 33:34], in_=d), reads=[rop], writes=[rsm])
                szs = vap(big[:], i * 2048 + (hg * 4 + h) * 128, [[1, 128]])
                P.op("dve", lambda e, o=szs, d=op_[:, 0:128], sm=sm: e.scalar_tensor_tensor(
                    out=o, in0=d, scalar=sm[:, 33:34], in1=o, op0=ALU.mult, op1=ALU.mult),
                     reads=[rop, rsm, rsz], writes=[rsz])

        def g_kv(khead_of, g):
            return khead_of

        def out_proj_residual(l, wsrc, nk, lhs, rlhs):
            rg = res("gate")
            for dg in range(4):
                for half in range(nk // KC):
                    src = cur["xin"] if (l == 0 and half == 0) else cur["y"]

                    def epi(tt, pp, rp, dg=dg, src=src):
                        xpt, rxp, key = xpR.next()
                        ry = res("y%d" % tt)
                        P.dma("sp", key, xpt[:], src[tt * 128:(tt + 1) * 128, dg * 512:(dg + 1) * 512], reads=[ry], writes=[rxp])
                        tb, rtb = t32R.next()
                        P.op("dve", lambda e: e.tensor_tensor(out=tb[:], in0=pp[:, :], in1=gate_bc[:, dg * 512:(dg + 1) * 512],
                                                              op=ALU.mult), reads=[rp, rg], writes=[rtb])
                        P.op("dve", lambda e: e.tensor_tensor(out=xpt[:], in0=xpt[:], in1=tb[:], op=ALU.add),
                             reads=[rtb, rxp], writes=[rxp])
                        P.dma("sp", "yst%d" % tt, cur["y"][tt * 128:(tt + 1) * 128, dg * 512:(dg + 1) * 512], xpt[:],
                              reads=[rxp], writes=[ry])
                    project(wsrc, half * 2048, dg, epi, lhs=lhs, rlhs=rlhs, koff=half * KC)

        def a_transpose():
            rh = res("hnT")
            for tt in range(cur["nt"]):
                transpose_to(None, lambda fc, tt=tt: vap(big[:], tt * 2048 + fc * 128, [[1, 128]]), KC, res("sz"), rh,
                             dstm=lambda i0, m, tt=tt: vap(hnT[:], i0 * T + tt * 128, [[T, m], [1, 128]]))

        def load_bc(dst, src_row, key):
            P.dma("sp", key, dst[:], src_row.partition_broadcast(128), writes=[res(key)])

        def win_layer(l, li):
            W = win_w_in[li]
            P.op("dve", lambda e: e.memset(attn_big[:, 12288:16640], 1.0), writes=[res("vaug")])
            load_bc(qn_bc, win_qn[li, :, :], "qn_bc")
            load_bc(kn_bc, win_kn[li, :, :], "kn_bc")
            P.dma("sp", "esink", esink[:], win_sink[li, :, :].partition_broadcast(128), writes=[res("esink")])
            P.op("act", lambda e: e.activation(out=esink[:], in_=esink[:], func=AF.Exp), reads=[res("esink")], writes=[res("esink")])
            static = cur["static"]
            if not static:
                P.dma("pool", "cstage", cstage[:], cur["cwk"][li].rearrange("(c p) n -> p c n", p=128),
                      reads=[res("cstage")], writes=[res("cstage")])
                for h in range(4):
                    transpose_to(lambda c, h=h: kTc[:, h, c * 128:(c + 1) * 128],
                                 lambda c, h=h: cstage[:, c, h * 128:(h + 1) * 128], 4, res("cstage"), res("kTc"))
                for c_ in range(4):
                    P.dma("pool", "vaugc", vaugc[:, c_, :, 0:128],
                          cur["cwv"][li, c_ * 128:(c_ + 1) * 128, :].rearrange("p (h d) -> p h d", d=128),
                          reads=[res("vaugc")], writes=[res("vaugc")])

            def k_epi(tt, pp, rp):
                kr, rkr = headnorm(pp, rp, kn_bc, res("kn_bc"), None if static else tt)
                P.dma("sp", "kout", cur["owk"][li, tt * 128:(tt + 1) * 128, :], kr[:], reads=[rkr])
                kb, rkb = t16R.next()
                P.op("act", lambda e: e.copy(out=kb[:], in_=kr[:]), reads=[rkr], writes=[rkb])
                transpose_to(None, lambda h: kb[:, h * 128:(h + 1) * 128], 4, rkb, res("kT"),
                             dstm=lambda i0, m: vap(kT, i0 * T + tt * 128, [[T, m], [1, 128]]))

            def v_epi(tt, pp, rp):
                vf, rvf = t32R.next()
                P.op("act", lambda e: e.copy(out=vf[:], in_=pp[:, :]), reads=[rp], writes=[rvf])
                P.dma("sp", "vout", cur["owv"][li, tt * 128:(tt + 1) * 128, :], vf[:], reads=[rvf])
                P.op("dve", lambda e: e.tensor_copy(out=vaug[:, tt, :, 0:128], in_=vap(vf[:], 0, [[128, 4], [1, 128]])),
                     reads=[rvf], writes=[res("vaug")])

            def z_epi_for(cgz):
                def z_epi(tt, pp, rp):
                    P.op("act", lambda e: e.activation(out=vap(big[:], tt * 2048 + cgz * 512, [[1, 512]]), in_=pp[:, :],
                                                       func=AF.Silu), reads=[rp], writes=[res("sz")])
                return z_epi

            def q_epi(tt, pp, rp):
                qr, rqr = headnorm(pp, rp, qn_bc, res("qn_bc"), None if static else tt)
                qb, rqb = t16R.next()
                P.op("act", lambda e: e.copy(out=qb[:], in_=qr[:]), reads=[rqr], writes=[rqb])
                transpose_to(None, lambda h: qb[:, h * 128:(h + 1) * 128], 4, rqb, res("qT"),
                             dstm=lambda i0, m: vap(qT, i0 * T + tt * 128, [[T, m], [1, 128]]))

            STG = int(os.environ.get("KSTAGE", "99"))
            project(W, 0, 4, k_epi)
            if STG < 5:
                return
            KSUB = os.environ.get("KSUB", "")
            if KSUB != "z":
                project(W, 0, 5, v_epi)
            if KSUB == "v":
                return
            for cgz in range(4):
                project(W, 0, 6 + cgz, z_epi_for(cgz))
            if STG < 6:
                return
            for g in range(4):
                if STG < 8 and g > 0:
                    return
                project(W, 0, g, q_epi)
                if STG < 7:
                    return
                for i in range(cur["nt"]):
                    if static:
                        loc = [(2 * (i // 2), None), (2 * (i // 2) + 1, None)]
                    else:
                        loc = []
                        if i - 1 >= 0:
                            loc.append((i - 1, ("pe", 0 if i % 2 == 0 else 1)))
                        loc.append((i, None))
                        if i + 1 < cur["nt"]:
                            loc.append((i + 1, ("pe", 2 if i % 2 == 0 else 3)))
                    msrc = None
                    attention(i, 4, g, [0, 1, 2, 3], loc, msrc, not static,
                              [4 * g + h for h in range(4)], [(4 * g + h) * 128 for h in range(4)])
            if STG < 9:
                return
            a_transpose()
            if STG < 10:
                return
            out_proj_residual(l, win_w_out[li], KC, hnT, res("hnT"))

        def nat_layer(l):
            W = nat_w_in[0]
            P.op("dve", lambda e: e.memset(attn_big[:, 12288:16640], 1.0), writes=[res("vaug")])
            load_bc(qn_bc, nat_qn[0, :, :], "qn_bc")
            load_bc(kn_bc, nat_kn[0, :, :], "kn_bc")

            def z_epi_for(cgz):
                def z_epi(tt, pp, rp):
                    P.op("act", lambda e: e.activation(out=vap(big[:], tt * 2048 + cgz * 512, [[1, 512]]), in_=pp[:, :],
                                                       func=AF.Silu), reads=[rp], writes=[res("sz")])
                return z_epi

            for cgz in range(4):
                project(W, 0, 12 + cgz, z_epi_for(cgz))
            for hg in range(4):
                def k_epi(tt, pp, rp, hg=hg):
                    kr, rkr = headnorm(pp, rp, kn_bc, res("kn_bc"), None)
                    P.dma("sp", "kout", cur["onk"][tt * 128:(tt + 1) * 128, hg * 512:(hg + 1) * 512], kr[:], reads=[rkr])
                    kb, rkb = t16R.next()
                    P.op("act", lambda e: e.copy(out=kb[:], in_=kr[:]), reads=[rkr], writes=[rkb])
                    transpose_to(None, lambda h: kb[:, h * 128:(h + 1) * 128], 4, rkb, res("kT"),
                             dstm=lambda i0, m: vap(kT, i0 * T + tt * 128, [[T, m], [1, 128]]))

                def v_epi(tt, pp, rp, hg=hg):
                    vf, rvf = t32R.next()
                    P.op("act", lambda e: e.copy(out=vf[:], in_=pp[:, :]), reads=[rp], writes=[rvf])
                    P.dma("sp", "vout", cur["onv"][tt * 128:(tt + 1) * 128, hg * 512:(hg + 1) * 512], vf[:], reads=[rvf])
                    P.op("dve", lambda e: e.tensor_copy(out=vaug[:, tt, :, 0:128], in_=vap(vf[:], 0, [[128, 4], [1, 128]])),
                         reads=[rvf], writes=[res("vaug")])

                def q_epi(tt, pp, rp):
                    qr, rqr = headnorm(pp, rp, qn_bc, res("qn_bc"), None)
                    qb, rqb = t16R.next()
                    P.op("act", lambda e: e.copy(out=qb[:], in_=qr[:]), reads=[rqr], writes=[rqb])
                    transpose_to(None, lambda h: qb[:, h * 128:(h + 1) * 128], 4, rqb, res("qT"),
                             dstm=lambda i0, m: vap(qT, i0 * T + tt * 128, [[T, m], [1, 128]]))

                static = cur["static"]
                project(W, 0, 4 + hg, k_epi)
                project(W, 0, 8 + hg, v_epi)
                if not static:
                    if hg == 0:
                        P.dma("pool", "cstage", cstage[:], cur["cnk"][:, 0:512].rearrange("(c p) n -> p c n", p=128),
                              reads=[res("cstage")], writes=[res("cstage")])
                    for h in range(4):
                        transpose_to(lambda c, h=h: kTc[:, h, c * 128:(c + 1) * 128],
                                     lambda c, h=h: cstage[:, c, h * 128:(h + 1) * 128], 4, res("cstage"), res("kTc"))
                    if hg < 3:
                        P.dma("pool", "cstage", cstage[:],
                              cur["cnk"][:, (hg + 1) * 512:(hg + 2) * 512].rearrange("(c p) n -> p c n", p=128),
                              reads=[res("cstage")], writes=[res("cstage")])
                    for c_ in range(4):
                        P.dma("pool", "vaugc", vaugc[:, c_, :, 0:128],
                              cur["cnv"][c_ * 128:(c_ + 1) * 128, hg * 512:(hg + 1) * 512].rearrange("p (h d) -> p h d", d=128),
                              reads=[res("vaugc")], writes=[res("vaugc")])
                project(W, 0, hg, q_epi)
                for i in range(cur["nt"]):
                    J = NAT_J[i]
                    attention_nat(i, hg)
                    continue
                    for h in range(4):
                        head = hg * 4 + h
                        if static:
                            attention(i, 1, h, [h], [(2 * (i // 2), None), (2 * (i // 2) + 1, None)], None, False, None,
                                      [head * 128])
                        else:
                            attention(i, 1, h, [h], [(j, jx) for jx, j in enumerate(J)],
                                      cur["nmask"][head, NAT_OFF[i]:NAT_OFF[i] + len(J), :, :], True, None, [head * 128])
            a_transpose()
            out_proj_residual(l, nat_w_out[0], KC, hnT, res("hnT"))

        def gmlp_layer(l):
            W = g_w_in[0]

            def chunk(fc, off, pairs):
                base = big if fc < 16 else attn_big
                return vap(base[:], (fc % 16) * T + off, pairs)

            P.dma("pool", "wsn", wsn[:], g_w_s[0].rearrange("g i j -> i g j"), writes=[res("wsn")])
            transpose_to(lambda g: wsT[:, g, :], lambda g: wsn[:, g, :], 16, res("wsn"), res("wsT"))
            rlb = res("lb2")
            rrb = res("rb2")
            P.dma("sp", "lb2", lb2[1:2, :], ones_d[0:1, 0:512], writes=[rlb])
            rvh = [res("vh%d" % fc) for fc in range(32)]
            rgs = res("gstat")

            def v_epi_for(cgv):
                def v_epi(tt, pp, rp):
                    tb, rtb = t32R.next()
                    P.op("act", lambda e: e.activation(out=tb[:], in_=pp[:, :], func=AF.Gelu_apprx_tanh,
                                                       accum_out=gstat[:, 0, tt, cgv:cgv + 1]), reads=[rp], writes=[rtb, rgs])
                    jb, rjb = t16R.next()
                    P.op("act", lambda e: e.activation(out=jb[:], in_=tb[:], func=AF.Square,
                                                       accum_out=gstat[:, 1, tt, cgv:cgv + 1]), reads=[rtb], writes=[rjb, rgs])
                    P.op("dve", lambda e: e.tensor_copy(out=chunk(4 * cgv, tt * 128, [[T, 4], [1, 128]]),
                                                        in_=vap(tb[:], 0, [[128, 4], [1, 128]])),
                         reads=[rtb], writes=[rvh[4 * cgv + q] for q in range(4)])
                return v_epi
            for cgv in range(8):
                project(W, 0, 8 + cgv, v_epi_for(cgv))
            rsm = res("small3")
            P.op("dve", lambda e: e.tensor_reduce(out=small[:, 40:56], in_=vap(gstat[:], 0, [[8, 16], [1, 8]]), axis=AX.X,
                                                  op=ALU.add), reads=[rgs], writes=[rsm])
            P.op("dve", lambda e: e.tensor_scalar(out=small[:, 40:56], in0=small[:, 40:56], scalar1=1.0 / 4096, scalar2=None,
                                                  op0=ALU.mult), reads=[rsm], writes=[rsm])
            P.op("dve", lambda e: e.tensor_tensor(out=small[:, 56:64], in0=small[:, 40:48], in1=small[:, 40:48], op=ALU.mult),
                 reads=[rsm], writes=[rsm])
            P.op("dve", lambda e: e.tensor_tensor(out=small[:, 48:56], in0=small[:, 48:56], in1=small[:, 56:64], op=ALU.subtract),
                 reads=[rsm], writes=[rsm])
            P.op("act", lambda e: e.activation(out=small[:, 48:56], in_=small[:, 48:56], func=AF.Sqrt, bias=ctxb_eps[:]),
                 reads=[rsm, rS], writes=[rsm])
            P.op("dve", lambda e: e.reciprocal(out=small[:, 48:56], in_=small[:, 48:56]), reads=[rsm], writes=[rsm])
            P.op("dve", lambda e: e.scalar_tensor_tensor(out=small[:, 56:64], in0=small[:, 40:48], scalar=-1.0, in1=small[:, 48:56],
                                                         op0=ALU.mult, op1=ALU.mult), reads=[rsm], writes=[rsm])
            for tt in range(cur["nt"]):
                for half in range(2):
                    view = chunk(16 * half, tt * 128, [[T, 16], [1, 128]])
                    rr_ = rvh[16 * half:16 * half + 16]
                    P.op("act", lambda e, v=view, tt=tt: e.activation(out=v, in_=v, func=AF.Identity,
                                                                      scale=small[:, 48 + tt:49 + tt],
                                                                      bias=small[:, 56 + tt:57 + tt]),
                         reads=[rsm] + rr_, writes=rr_)
            rsv = res("svbuf")
            rlng = res("lng")
            for cgu in range(8):
                P.dma("pool", "lng", lng[:], g_ln_g[0, :, cgu * 512:(cgu + 1) * 512].partition_broadcast(128), writes=[rlng])
                P.dma("sp", "lb2", lb2[0:1, :], g_ln_b[0, :, cgu * 512:(cgu + 1) * 512], writes=[rlb])
                P.dma("sp", "rb2", rb2[1:2, :], g_b_s[0, :, cgu * 256:(cgu + 1) * 256], writes=[rrb])
                pr, rpr = poR.next()
                P.op("pe", lambda e, o=pr[0:1, 0:256], b=vap(wsT[:], 2 * cgu * 128, [[1, 256]]): e.matmul(
                    o, lhsT=onesb[:, 0:1], rhs=b, start=True, stop=True), reads=[res("wsT"), res("onesb")], writes=[rpr])
                P.op("act", lambda e, a=pr[0:1, 0:256]: e.copy(out=rb2[0:1, :], in_=a), reads=[rpr], writes=[rrb])
                for tt in range(cur["nt"]):
                    v4 = chunk(4 * cgu, tt * 128, [[T, 4], [1, 128]])
                    r4 = rvh[4 * cgu:4 * cgu + 4]
                    P.op("dve", lambda e, v=v4: e.tensor_tensor(out=v, in0=v, in1=vap(lng[:], 0, [[128, 4], [1, 128]]), op=ALU.mult),
                         reads=[rlng] + r4, writes=r4)
                    pp, rp = psR.next()
                    for gi in range(2):
                        g = 2 * cgu + gi
                        P.op("pe", lambda e, o=pp[:, gi * 256:(gi + 1) * 256], a=wsT[:, g, :],
                             b=chunk(2 * g, tt * 128, [[T, 2], [1, 128]]): e.matmul(o, lhsT=a, rhs=b, start=True, stop=False),
                             reads=[res("wsT"), rvh[2 * g], rvh[2 * g + 1]], writes=[rp])
                        P.op("pe", lambda e, o=pp[:, gi * 256:(gi + 1) * 256], a=rb2[0:2, gi * 128:(gi + 1) * 128],
                             b=lb2[0:2, gi * 256:(gi + 1) * 256]: e.matmul(o, lhsT=a, rhs=b, start=False, stop=True),
                             reads=[rrb, rlb], writes=[rp], acc=True)
                    P.op("act", lambda e, o=svbuf[:, tt, :], a=pp[:, :]: e.copy(out=o, in_=a), reads=[rp], writes=[rsv])
                slot_u, rwu = load_w(W, 0, cgu * 512)
                for tt in range(cur["nt"]):
                    pu, rpu = pjR.next()
                    for kc in range(KC):
                        P.op("pe", lambda e, o=pu[:, :], a=hnT[:, kc, tt * 128:(tt + 1) * 128], b=slot_u[:, kc, :], s=(kc == 0),
                             t=(kc == KC - 1): e.matmul(o, lhsT=a, rhs=b, start=s, stop=t), reads=[rwu[kc // 8], res("hnT")], writes=[rpu], acc=(kc > 0))
                    gu, rgu = t16R.next()
                    P.op("act", lambda e, o=gu[:], a=pu[:, :]: e.activation(out=o, in_=a, func=AF.Gelu_apprx_tanh), reads=[rpu], writes=[rgu])
                    P.op("dve", lambda e, o=svbuf[:, tt, :], b=gu[:]: e.tensor_tensor(out=o, in0=o, in1=b, op=ALU.mult),
                         reads=[rsv, rgu], writes=[rsv])
                slot_z, rwz = load_w(W, 0, (16 + cgu) * 512)

                def z_epi(tt, pz, rpz, cgu=cgu):
                    gz, rgz = t16R.next()
                    P.op("act", lambda e, o=gz[:], a=pz[:, :]: e.activation(out=o, in_=a, func=AF.Silu), reads=[rpz], writes=[rgz])
                    P.op("dve", lambda e, o=gz[:], b=svbuf[:, tt, :]: e.tensor_tensor(out=o, in0=o, in1=b, op=ALU.mult),
                         reads=[rsv, rgz], writes=[rgz])
                    pt, rpt = ptR.next()
                    for q in range(4):
                        P.op("pe", lambda e, o=pt[:, q * 128:(q + 1) * 128], s_=gz[:, q * 128:(q + 1) * 128]: e.transpose(
                            out=o, in_=s_, identity=ident[:]), reads=[rgz, rS], writes=[rpt])
                    P.op("dve", lambda e, o=chunk(4 * cgu, tt * 128, [[T, 4], [1, 128]]),
                         s_=vap(pt[:, :], 0, [[128, 4], [1, 128]]): e.tensor_copy(out=o, in_=s_),
                         reads=[rpt], writes=[rvh[4 * cgu + q] for q in range(4)])

                pend = []
                for tt in range(cur["nt"]):
                    pz, rpz = pjR.next()
                    for kc in range(KC):
                        P.op("pe", lambda e, o=pz[:, :], a=hnT[:, kc, tt * 128:(tt + 1) * 128], b=slot_z[:, kc, :], s=(kc == 0),
                             t=(kc == KC - 1): e.matmul(o, lhsT=a, rhs=b, start=s, stop=t), reads=[rwz[kc // 8], res("hnT")], writes=[rpz], acc=(kc > 0))
                    pend.append((tt, pz, rpz))
                    if len(pend) > EPI_DELAY:
                        z_epi(*pend.pop(0))
                for p_ in pend:
                    z_epi(*p_)

            class L:
                def __getitem__(self, key):
                    _, kk, sl = key
                    return chunk(kk, sl.start, [[1, sl.stop - sl.start]])
            rall = res("aTall")
            P.op("dve", lambda e: e.memset(small[:, 63:64], 0.0), reads=rvh, writes=[rall, res("small4")])
            out_proj_residual(l, g_w_out[0], 32, L(), rall)

        ctxb_eps = sb("epsb", [128, 1], F32)
        P.op("dve", lambda e: e.memset(ctxb_eps[:], EPS), writes=[rS])

        kinds = [0, 1, 2, 0]
        first = True
        for pi in range(len(passes)):
            cur.clear()
            cur.update(PT[pi])
            STAGE = int(os.environ.get("KSTAGE", "99"))
            if STAGE < 1:
                break
            pass_setup()
            for l in range(nlayers):
                if kinds[l] == 2 or (l > 0 and kinds[l - 1] == 2):
                    for e_ in ENGS:
                        P.wait_all(e_)
                first = False
                if STAGE < 2:
                    break
                adaln(l)
                if STAGE < 3:
                    break
                norm_phase(l)
                if STAGE < 4:
                    break
                k = kinds[l]
                if k == 0:
                    win_layer(l, l // 3)
                elif k == 1:
                    nat_layer(l)
                else:
                    gmlp_layer(l)
        for e in ENGS:
            P.wait_all(e)

        esem = {e: st.enter_context(nc.semaphore("s_" + e)) for e in ENGS}
        dsem = {k: st.enter_context(nc.semaphore("d_" + k)) for k in P.dcount}
        block = st.enter_context(nc.Block())
        P.emit(block, esem, dsem)
    return nc


NCORES = 4
PASSES = ((8, False), (4, True)) if os.environ.get('KPASSES', '2') == '2' else ((8, False),)
ASSIGN = {0: (("s", 0), [12, 13]), 1: (("s", 1), [14, 15]),
          2: (("p", [0, 1, 2, 3]), [4, 5]), 3: (("p", [6, 7, 8, 9]), [10, 11])}


def _win_mask(sample):
    m = np.full((8, 3, 128, 128), NEG, np.float32)
    a = np.arange(128)[:, None]
    b = np.arange(128)[None, :]
    for i in range(8):
        m[i, 1] = 0.0
        if sample:
            m[i, 0] = np.where(b <= a, 0.0, NEG)
            m[i, 2] = np.where(a <= b, 0.0, NEG)
        else:
            if i % 2 == 1:
                m[i, 0] = 0.0
            else:
                m[i, 2] = 0.0
    return m


def _nat_mask(sample, rel_bias):
    m = np.full((16, NAT_NB, 128, 128), NEG, np.float32)
    a = np.arange(128)
    for i in range(8):
        for jx, j in enumerate(NAT_J[i]):
            blk = NAT_OFF[i] + jx
            if not sample:
                if j // 2 == i // 2:
                    m[:, blk] = 0.0
                continue
            krow = (2 * j + a // 64)[:, None]
            kcol = (a % 64)[:, None]
            qrow = (2 * i + a // 64)[None, :]
            qcol = (a % 64)[None, :]
            rs = np.clip(qrow - 4, 0, 8)
            cs = np.clip(qcol - 8, 0, 48)
            valid = (krow >= rs) & (krow < rs + 8) & (kcol >= cs) & (kcol < cs + 16)
            dr = np.clip(krow - qrow + 7, 0, 14)
            dc = np.clip(kcol - qcol + 15, 0, 30)
            g = rel_bias[:, dr, dc]
            m[:, blk] = np.where(valid[None], g, np.float32(NEG))
    return m


def _rope_tables(sample):
    if not sample:
        return np.ones((T, 64), np.float32), np.zeros((T, 64), np.float32)
    t = np.arange(T)
    row = (t // 64).astype(np.float32)
    col = (t % 64).astype(np.float32)
    inv = (np.float32(10000.0) ** (-np.arange(32, dtype=np.float32) / np.float32(32))).astype(np.float32)
    ang = np.concatenate([row[:, None] * inv[None], col[:, None] * inv[None]], axis=1).astype(np.float32)
    return np.cos(ang).astype(np.float32), np.sin(ang).astype(np.float32)


_NC_CACHE = {}


def kernel(x_prompt, x_sample, cache_win_k, cache_win_v, cache_nat_k, cache_nat_v, c, c_ctx,
           norm_g, w_ada, b_ada, win_w_in, win_q_norm, win_k_norm, win_sink, win_w_out,
           nat_w_in, nat_q_norm, nat_k_norm, nat_rel_bias, nat_w_out,
           gmlp_w_in, gmlp_ln_g, gmlp_ln_b, gmlp_w_s, gmlp_b_s, gmlp_w_out, _nlayers=4):
    f = lambda a: np.ascontiguousarray(np.asarray(a, dtype=np.float32))
    x_prompt, x_sample = f(x_prompt), f(x_sample)
    w_ada_ = np.asarray(w_ada)
    win_w_in_ = np.asarray(win_w_in)
    win_w_out_ = np.asarray(win_w_out)
    shared = {
        "ident": np.eye(128, dtype=np.float32),
        "onesrow": np.ones((1, 4096), np.float32),
        "normg": f(np.asarray(norm_g).reshape(4, 16, 128).transpose(0, 2, 1)),
        "b_ada": f(np.asarray(b_ada).reshape(4, 1, 6144)),
        "win_qn": f(np.asarray(win_q_norm).reshape(2, 1, 128)),
        "win_kn": f(np.asarray(win_k_norm).reshape(2, 1, 128)), "win_sink": f(np.asarray(win_sink).reshape(2, 1, 16)),
        "nat_w_in": f(np.asarray(nat_w_in)[0]), "nat_qn": f(np.asarray(nat_q_norm).reshape(1, 1, 128)),
        "nat_kn": f(np.asarray(nat_k_norm).reshape(1, 1, 128)), "nat_w_out": f(np.asarray(nat_w_out)[0]),
        "g_w_in": f(np.asarray(gmlp_w_in)[0]), "g_ln_g": f(np.asarray(gmlp_ln_g).reshape(1, 1, 4096)),
        "g_ln_b": f(np.asarray(gmlp_ln_b).reshape(1, 1, 4096)), "g_w_s": f(gmlp_w_s),
        "g_b_s": f(np.asarray(gmlp_b_s).reshape(1, 1, 2048)), "g_w_out": f(np.asarray(gmlp_w_out)[0]),
    }
    for l_ in range(4):
        shared["w_ada%d" % l_] = f(w_ada_[l_])
    for i_ in range(2):
        shared["win_w_in%d" % i_] = f(win_w_in_[i_])
        shared["win_w_out%d" % i_] = f(win_w_out_[i_])
    rel_bias = f(nat_rel_bias)[0]
    wm = {False: _win_mask(False), True: _win_mask(True)}
    nm = {False: _nat_mask(False, rel_bias), True: _nat_mask(True, rel_bias)}
    rp = {False: _rope_tables(False), True: _rope_tables(True)}
    shared["zmask"] = np.zeros((2, 128, 128), np.float32)
    c = f(c)
    c_ctx = f(c_ctx)
    in_maps = []
    for core in range(NCORES):
        (kindA, whatA), seqsB = ASSIGN[core]
        sample = kindA == "s"
        m = dict(shared)
        if sample:
            b = whatA
            xa = x_sample[b]
            cv = c[b]
            m["cwk_0"] = f(np.asarray(cache_win_k)[b].reshape(2, 512, 512))
            m["cwv_0"] = f(np.asarray(cache_win_v)[b].reshape(2, 512, 512))
            m["cnk_0"] = f(np.asarray(cache_nat_k)[b, 0].reshape(512, 2048))
            m["cnv_0"] = f(np.asarray(cache_nat_v)[b, 0].reshape(512, 2048))
            m["ctxb_0"] = np.zeros((128, 1), np.float32)
        else:
            xa = np.concatenate([x_prompt[sq] for sq in whatA], axis=0)
            cv = c_ctx
            m["cwk_0"] = np.zeros((2, 512, 512), np.float32)
            m["cwv_0"] = np.zeros((2, 512, 512), np.float32)
            m["cnk_0"] = np.zeros((512, 2048), np.float32)
            m["cnv_0"] = np.zeros((512, 2048), np.float32)
            m["ctxb_0"] = np.full((128, 1), NEG, np.float32)
        m["xin_0"] = f(xa)
        m["cond_0"] = f(cv.reshape(16, 128).T)
        m["wmask_0"] = wm[sample]
        m["nmask_0"] = nm[sample]
        m["ropec_0"], m["ropes_0"] = rp[sample]
        m["xin_1"] = f(np.concatenate([x_prompt[sq] for sq in seqsB], axis=0))
        m["cond_1"] = f(c_ctx.reshape(16, 128).T)
        in_maps.append(m)
    key = (_nlayers,)
    if key not in _NC_CACHE:
        _NC_CACHE[key] = build(_nlayers, PASSES)
    nc = _NC_CACHE[key]
    res = run_bass_kernel_spmd(nc, in_maps, core_ids=list(range(NCORES)))
    outs = res.results
    y_prompt = np.zeros((16, 256, D), np.float32)
    y_sample = np.zeros((2, 1024, D), np.float32)
    nwk = np.zeros((16, 2, 256, 4, 128), np.float32)
    nwv = np.zeros((16, 2, 256, 4, 128), np.float32)
    nnk = np.zeros((16, 1, 256, 16, 128), np.float32)
    nnv = np.zeros((16, 1, 256, 16, 128), np.float32)

    def take(o, sfx, seqs):
        for s_, sq in enumerate(seqs):
            sl = slice(s_ * 256, (s_ + 1) * 256)
            y_prompt[sq] = o["y" + sfx][sl]
            for li in range(2):
                nwk[sq, li] = o["owk" + sfx][li, sl].reshape(256, 4, 128)
                nwv[sq, li] = o["owv" + sfx][li, sl].reshape(256, 4, 128)
            nnk[sq, 0] = o["onk" + sfx][sl].reshape(256, 16, 128)
            nnv[sq, 0] = o["onv" + sfx][sl].reshape(256, 16, 128)

    for core in range(NCORES):
        (kindA, whatA), seqsB = ASSIGN[core]
        o = outs[core]
        if kindA == "s":
            y_sample[whatA] = o["y_0"]
        else:
            take(o, "_0", whatA)
        if len(PASSES) > 1:
            take(o, "_1", seqsB)
    return (y_prompt, y_sample, nwk, nwv, nnk, nnv)
```
